# Optimizing a Trainium2 kernel written in Bass

```python
import math
import jax, jax.numpy as jnp
from jax import lax
import numpy as np

D_MODEL = 1024
BATCH = 4
SEQ = 4096
DEPTH = 2

D_MIX = D_MODEL
D_A = 3 * D_MIX // 8
D_B = 3 * D_MIX // 8
D_C = D_MIX - D_A - D_B
HEAD_DIM = 64
N_Q_HEADS = D_B // HEAD_DIM
N_KV_HEADS = 2
GROUP = N_Q_HEADS // N_KV_HEADS
D_KV = N_KV_HEADS * HEAD_DIM
D_IN = 2 * D_A + D_B + 2 * D_KV + 3 * D_C
LRU_BW = 64
LRU_BLOCKS = D_A // LRU_BW
C_LRU = 8.0
CONV_A = 4
WINDOW = 128
BLOCK = 128
ROPE_THETA = 500000.0
ROT_DIM = HEAD_DIM // 4
CONV_C = 3
HY_EMB = 33
HY_BANDS = (HY_EMB - 1) // 2
HY_WIDTH = 64
HY_TARGET = 1e-2
HY_MAX_DECAY = math.log(HY_TARGET) / 0.3
HY_MIN_DECAY = math.log(HY_TARGET) / 1.5
D_FF = 4 * D_MODEL
EPS = 1e-6
NEG = -1e30

kernel_name = "hybrid_rglru_swa_hyena_encoder"

F32 = jnp.float32


def rmsnorm(x, g):
    xf = x.astype(F32)
    y = xf * lax.rsqrt(jnp.mean(xf * xf, axis=-1, keepdims=True) + EPS) * g.astype(F32)
    return y.astype(x.dtype)


def depthwise_conv(x, w, b, pad_left, pad_right):
    L = x.shape[1]
    xp = jnp.pad(x, ((0, 0), (pad_left, pad_right), (0, 0)))
    y = xp[:, 0:L] * w[0]
    for k in range(1, w.shape[0]):
        y = y + xp[:, k:k + L] * w[k]
    return y + b


def rope_tables(L):
    pos = jnp.arange(L, dtype=F32)
    inv_freq = ROPE_THETA ** (-jnp.arange(0, ROT_DIM, 2, dtype=F32) / ROT_DIM)
    ang = pos[:, None] * inv_freq[None, :]
    return jnp.cos(ang), jnp.sin(ang)


def rope_partial(t, cos, sin):
    half = ROT_DIM // 2
    tf = t.astype(F32)
    t1, t2, rest = tf[..., :half], tf[..., half:ROT_DIM], tf[..., ROT_DIM:]
    c = cos[None, :, None, :]
    s = sin[None, :, None, :]
    return jnp.concatenate([t1 * c - t2 * s, t2 * c + t1 * s, rest], axis=-1).astype(t.dtype)


def linear_scan(a, b, reverse):
    def combine(c1, c2):
        a1, b1 = c1
        a2, b2 = c2
        return a1 * a2, a2 * b1 + b2
    _, h = lax.associative_scan(combine, (a, b), axis=1, reverse=reverse)
    return h


def rglru_mixer(u, gate, conv_w, conv_b, wa, ba, wx, bx, lam):
    xc = depthwise_conv(u, conv_w, conv_b, 2, 1).astype(F32)
    Bn, L, _ = xc.shape
    xb = xc.reshape(Bn, L, LRU_BLOCKS, LRU_BW)
    r = jax.nn.sigmoid(jnp.einsum("blhi,nhij->nblhj", xb, wa.astype(F32)).reshape(2, Bn, L, D_A)
                       + ba.astype(F32)[:, None, None, :])
    i = jax.nn.sigmoid(jnp.einsum("blhi,nhij->nblhj", xb, wx.astype(F32)).reshape(2, Bn, L, D_A)
                       + bx.astype(F32)[:, None, None, :])
    log_a = -C_LRU * r * jax.nn.softplus(-lam.astype(F32))[:, None, None, :]
    a = jnp.exp(log_a)
    b = jnp.sqrt(-jnp.expm1(2.0 * log_a)) * (i * xc[None])
    h = linear_scan(a[0], b[0], reverse=False) + linear_scan(a[1], b[1], reverse=True)
    return (h * jax.nn.gelu(gate.astype(F32))).astype(u.dtype)


def window_attention(q, k, v, sink, cos, sin):
    Bn, L = q.shape[0], q.shape[1]
    nblk = L // BLOCK
    q = rope_partial(q.reshape(Bn, L, N_Q_HEADS, HEAD_DIM), cos, sin)
    k = rope_partial(k.reshape(Bn, L, N_KV_HEADS, HEAD_DIM), cos, sin)
    v = v.reshape(Bn, L, N_KV_HEADS, HEAD_DIM)
    qb = q.reshape(Bn, nblk, BLOCK, N_KV_HEADS, GROUP, HEAD_DIM)

    def band(t):
        tp = jnp.pad(t, ((0, 0), (BLOCK, BLOCK), (0, 0), (0, 0)))
        tp = tp.reshape(Bn, nblk + 2, BLOCK, N_KV_HEADS, HEAD_DIM)
        return jnp.concatenate([tp[:, :-2], tp[:, 1:-1], tp[:, 2:]], axis=2)

    kb, vb = band(k), band(v)
    s = jnp.einsum("bnqhgd,bnshd->bnhgqs", qb.astype(F32), kb.astype(F32)) * (HEAD_DIM ** -0.5)
    blk = jnp.arange(nblk)[:, None]
    qpos = blk * BLOCK + jnp.arange(BLOCK)[None, :]
    kpos = (blk - 1) * BLOCK + jnp.arange(3 * BLOCK)[None, :]
    diff = qpos[:, :, None] - kpos[:, None, :]
    valid = (jnp.abs(diff) <= WINDOW) & (kpos[:, None, :] >= 0) & (kpos[:, None, :] < L)
    s = jnp.where(valid[None, :, None, None], s, NEG)
    sk = sink.astype(F32).reshape(N_KV_HEADS, GROUP)[None, None, :, :, None, None]
    m = jnp.maximum(jnp.max(s, axis=-1, keepdims=True), sk)
    p = jnp.exp(s - m)
    denom = jnp.sum(p, axis=-1, keepdims=True) + jnp.exp(sk - m)
    o = jnp.einsum("bnhgqs,bnshd->bnqhgd", p / denom, vb.astype(F32))
    return o.reshape(Bn, L, D_B).astype(q.dtype)


def hyena_filters(L, w1, b1, freq, w2, b2, w3):
    t = jnp.linspace(0.0, 1.0, L, dtype=F32)[:, None]
    w = 2.0 * math.pi * jnp.arange(L, dtype=F32)[:, None] / L
    f = jnp.linspace(1e-4, HY_BANDS - 1, HY_BANDS, dtype=F32)[None, :]
    z = jnp.concatenate([t, jnp.cos(f * w), -jnp.sin(f * w)], axis=-1)
    fr = freq.astype(F32)
    hdn = jnp.sin(fr * (z @ w1.astype(F32) + b1.astype(F32)))
    hdn = jnp.sin(fr * (hdn @ w2.astype(F32) + b2.astype(F32)))
    filt = (hdn @ w3.astype(F32)).reshape(L, 2, D_C)
    deltas = jnp.abs(jnp.linspace(HY_MIN_DECAY, HY_MAX_DECAY, D_C, dtype=F32))
    decay = jnp.exp(-t * deltas[None, :])
    filt = filt * decay[:, None, :]
    return filt[:, 0], filt[:, 1]


def hyena_mixer(u, conv_w, conv_b, h_fwd, h_bwd, bias):
    uc = depthwise_conv(u, conv_w, conv_b, 1, 1).astype(F32)
    L = uc.shape[1]
    x0, x1, v = jnp.split(uc, 3, axis=-1)
    z = v * x1
    filt_circ = jnp.concatenate([h_fwd, jnp.zeros((1, D_C), F32), h_bwd[1:][::-1]], axis=0)
    zf = jnp.fft.rfft(z, n=2 * L, axis=1)
    hf = jnp.fft.rfft(filt_circ, n=2 * L, axis=0)
    y = jnp.fft.irfft(zf * hf[None], n=2 * L, axis=1)[:, :L] + z * bias.astype(F32)
    return (y * x0).astype(u.dtype)


def setup_inputs(seed: int = 0) -> dict:
    key = jax.random.key(seed)
    ks = jax.random.split(key, 32)

    def nrm(k, shape, scale):
        return jax.random.normal(k, shape, F32) * scale

    a0 = jax.random.uniform(ks[9], (DEPTH, 2, D_A), F32, 0.9, 0.999)
    return {
        "x": nrm(ks[0], (BATCH, SEQ, D_MODEL), 1.0),
        "norm_mix_g": 1.0 + nrm(ks[1], (DEPTH, D_MODEL), 0.02),
        "w_in": nrm(ks[2], (DEPTH, D_MODEL, D_IN), D_MODEL ** -0.5),
        "conv_a_w": nrm(ks[3], (DEPTH, CONV_A, D_A), CONV_A ** -0.5),
        "conv_a_b": nrm(ks[4], (DEPTH, D_A), 0.01),
        "lru_wa": nrm(ks[5], (DEPTH, 2, LRU_BLOCKS, LRU_BW, LRU_BW), LRU_BW ** -0.5),
        "lru_ba": nrm(ks[6], (DEPTH, 2, D_A), 0.01),
        "lru_wx": nrm(ks[7], (DEPTH, 2, LRU_BLOCKS, LRU_BW, LRU_BW), LRU_BW ** -0.5),
        "lru_bx": nrm(ks[8], (DEPTH, 2, D_A), 0.01),
        "lru_lambda": jnp.log(a0) - jnp.log1p(-a0),
        "attn_sink": nrm(ks[10], (DEPTH, N_Q_HEADS), 0.5),
        "hy_conv_w": nrm(ks[11], (DEPTH, CONV_C, 3 * D_C), CONV_C ** -0.5),
        "hy_conv_b": nrm(ks[12], (DEPTH, 3 * D_C), 0.01),
        "hy_w1": nrm(ks[13], (DEPTH, HY_EMB, HY_WIDTH), HY_EMB ** -0.5),
        "hy_b1": nrm(ks[14], (DEPTH, HY_WIDTH), 0.1),
        "hy_freq": 1.0 + nrm(ks[15], (DEPTH, HY_WIDTH), 0.05),
        "hy_w2": nrm(ks[16], (DEPTH, HY_WIDTH, HY_WIDTH), HY_WIDTH ** -0.5),
        "hy_b2": nrm(ks[17], (DEPTH, HY_WIDTH), 0.1),
        "hy_w3": nrm(ks[18], (DEPTH, HY_WIDTH, 2 * D_C), HY_WIDTH ** -0.5),
        "hy_bias": nrm(ks[19], (DEPTH, D_C), 0.1),
        "gnorm_a": 1.0 + nrm(ks[20], (DEPTH, D_A), 0.02),
        "gnorm_b": 1.0 + nrm(ks[21], (DEPTH, D_B), 0.02),
        "gnorm_c": 1.0 + nrm(ks[22], (DEPTH, D_C), 0.02),
        "w_out": nrm(ks[23], (DEPTH, D_MIX, D_MODEL), D_MIX ** -0.5),
        "norm_mlp_g": 1.0 + nrm(ks[24], (DEPTH, D_MODEL), 0.02),
        "w_up": nrm(ks[25], (DEPTH, D_MODEL, D_FF), D_MODEL ** -0.5),
        "w_down": nrm(ks[26], (DEPTH, D_FF, D_MODEL), D_FF ** -0.5),
        "final_norm_g": 1.0 + nrm(ks[27], (D_MODEL,), 0.02),
    }


def reference(x, norm_mix_g, w_in, conv_a_w, conv_a_b, lru_wa, lru_ba, lru_wx, lru_bx, lru_lambda,
              attn_sink, hy_conv_w, hy_conv_b, hy_w1, hy_b1, hy_freq, hy_w2, hy_b2, hy_w3, hy_bias,
              gnorm_a, gnorm_b, gnorm_c, w_out, norm_mlp_g, w_up, w_down, final_norm_g):
    L = x.shape[1]
    cos, sin = rope_tables(L)
    splits = [D_A, 2 * D_A, 2 * D_A + D_B, 2 * D_A + D_B + D_KV, 2 * D_A + D_B + 2 * D_KV]
    for i in range(DEPTH):
        h = rmsnorm(x, norm_mix_g[i])
        p = h @ w_in[i]
        a_x, a_g, q, k, v, c_u = jnp.split(p, splits, axis=-1)
        y_a = rglru_mixer(a_x, a_g, conv_a_w[i], conv_a_b[i], lru_wa[i], lru_ba[i],
                          lru_wx[i], lru_bx[i], lru_lambda[i])
        y_b = window_attention(q, k, v, attn_sink[i], cos, sin)
        h_fwd, h_bwd = hyena_filters(L, hy_w1[i], hy_b1[i], hy_freq[i], hy_w2[i], hy_b2[i], hy_w3[i])
        y_c = hyena_mixer(c_u, hy_conv_w[i], hy_conv_b[i], h_fwd, h_bwd, hy_bias[i])
        y = jnp.concatenate([rmsnorm(y_a, gnorm_a[i]), rmsnorm(y_b, gnorm_b[i]),
                             rmsnorm(y_c, gnorm_c[i])], axis=-1)
        x = x + (y @ w_out[i]).astype(x.dtype)
        h2 = rmsnorm(x, norm_mlp_g[i])
        x = x + (jnp.square(jax.nn.relu(h2 @ w_up[i])) @ w_down[i]).astype(x.dtype)
    return rmsnorm(x, final_norm_g)
```

```python
import contextlib
import math
import numpy as np
import concourse.bass as bass
import concourse.mybir as mybir
from concourse.bass_utils import run_bass_kernel_spmd

F32 = mybir.dt.float32
BF16 = mybir.dt.bfloat16
AF = mybir.ActivationFunctionType
ALU = mybir.AluOpType

D = 1024
L = 4096
DEPTH = 2
D_A = 384
D_B = 384
D_C = 256
D_IN = 2176
D_FF = 4096
EPS = 1e-6
NT = L // 128
ROPE_THETA = 500000.0
HY_BANDS = 16
HY_MAX_DECAY = math.log(1e-2) / 0.3
HY_MIN_DECAY = math.log(1e-2) / 1.5
MAGIC = 12582912.0
TWO_PI = 2.0 * math.pi

PP = {}
_c = 0
for _n, _w in [("g1", 8), ("g2", 8), ("caw", 12), ("cab", 3), ("ba", 6), ("bx", 6), ("lam", 6), ("gna", 3),
               ("gnb", 3), ("hcw", 18), ("hcb", 6), ("hbias", 2), ("gnc", 2), ("hb1", 1), ("hfr", 1), ("hb2", 1),
               ("sink", 6)]:
    PP[_n] = _c
    _c += _w
NPP = _c

ENGS = ("pe", "act", "dve", "pool", "sp")
NDSEM = 8


class Res:
    __slots__ = ("name", "writers", "readers")

    def __init__(self, name=""):
        self.name = name
        self.writers = {}
        self.readers = {}


class Prog:
    def __init__(self, nc):
        self.nc = nc
        self.ops = {e: [] for e in ENGS}
        self.cnt = {e: 0 for e in ENGS}
        self.waited = {e: {} for e in ENGS}
        self.dma_n = {e: 0 for e in ENGS}
        self.dsem_uses = {}
        self.sems = {}
        self.keys = list(ENGS) + [("d", q, i) for q in ("sp", "pool", "act") for i in range(NDSEM)]

    def alloc_sems(self, st):
        for k in self.keys:
            nm = k if isinstance(k, str) else "d_%s_%d" % (k[1], k[2])
            self.sems[k] = st.enter_context(self.nc.semaphore("s_" + nm))

    def _deps(self, eng, reads, writes):
        need = {}
        for r in reads:
            for k, v in r.writers.items():
                if need.get(k, 0) < v:
                    need[k] = v
        for w in writes:
            for k, v in w.writers.items():
                if need.get(k, 0) < v:
                    need[k] = v
            for k, v in w.readers.items():
                if need.get(k, 0) < v:
                    need[k] = v
        waits = []
        wd = self.waited[eng]
        for k, v in need.items():
            if eng == "pe" and k == "pe":
                continue
            if wd.get(k, 0) >= v:
                continue
            wd[k] = v
            waits.append((k, v))
        return waits

    def _mark(self, tok, reads, writes):
        k, v = tok
        for w in writes:
            w.writers = {k: v}
            w.readers = {}
        for r in reads:
            if r.readers.get(k, 0) < v:
                r.readers[k] = v

    def op(self, eng, fn, reads=(), writes=()):
        waits = self._deps(eng, reads, writes)
        self.cnt[eng] += 1
        tok = (eng, self.cnt[eng])
        self.ops[eng].append((waits, fn, tok, 1))
        self._mark(tok, reads, writes)
        return tok

    def dma(self, q, out, in_, reads=(), writes=()):
        n = self.dma_n[q]
        self.dma_n[q] += 1
        key = ("d", q, n % NDSEM)
        uses = self.dsem_uses.get(key, 0)
        waits = self._deps(q, reads, writes)
        if uses > 0 and self.waited[q].get(key, 0) < 16 * uses:
            self.waited[q][key] = 16 * uses
            waits.append((key, 16 * uses))
        self.dsem_uses[key] = uses + 1
        tok = (key, 16 * (uses + 1))
        self.ops[q].append((waits, (lambda e: e.dma_start(out=out, in_=in_)), tok, 16))
        self._mark(tok, reads, writes)
        return tok

    def barrier(self):
        cur = {e: self.cnt[e] for e in ENGS}
        for k, u in self.dsem_uses.items():
            cur[k] = 16 * u
        for e in ENGS:
            waits = []
            for k, v in cur.items():
                if k == e or v == 0:
                    continue
                if self.waited[e].get(k, 0) >= v:
                    continue
                self.waited[e][k] = v
                waits.append((k, v))
            self.ops[e].append((waits, None, None, 0))

    def emit_block(self):
        nc = self.nc
        sems = self.sems
        ops = self.ops

        def run(e, ename):
            for waits, fn, tok, inc in ops[ename]:
                for k, v in waits:
                    e.wait_ge(sems[k], v)
                if fn is not None:
                    fn(e).then_inc(sems[tok[0]], inc)

        with nc.Block() as block:
            @block.tensor
            def _(e):
                run(e, "pe")

            @block.scalar
            def _(e):
                run(e, "act")

            @block.vector
            def _(e):
                run(e, "dve")

            @block.gpsimd
            def _(e):
                run(e, "pool")

            @block.sync
            def _(e):
                run(e, "sp")
        self.ops = {e: [] for e in ENGS}

    def mm(self, out, lhsT, rhs, start=True, stop=True, reads=(), writes=()):
        return self.op("pe", lambda e: e.matmul(out, lhsT, rhs, start=start, stop=stop), reads, writes)

    def tr(self, out, in_, ident, reads=(), writes=()):
        return self.op("pe", lambda e: e.transpose(out, in_, ident), reads, writes)

    def act(self, out, in_, func, bias=None, scale=None, accum_out=None, reads=(), writes=()):
        kw = {}
        if bias is not None:
            kw["bias"] = bias
        if scale is not None:
            kw["scale"] = scale
        if accum_out is not None:
            kw["accum_out"] = accum_out
        return self.op("act", lambda e: e.activation(out=out, in_=in_, func=func, **kw), reads, writes)

    def ts(self, eng, out, in0, s1, s2, op0, op1=None, reads=(), writes=()):
        if op1 is None:
            return self.op(eng, lambda e: e.tensor_scalar(out=out, in0=in0, scalar1=s1, scalar2=None, op0=op0),
                           reads, writes)
        return self.op(eng, lambda e: e.tensor_scalar(out=out, in0=in0, scalar1=s1, scalar2=s2, op0=op0, op1=op1),
                       reads, writes)

    def tt(self, eng, out, in0, in1, op, reads=(), writes=()):
        return self.op(eng, lambda e: e.tensor_tensor(out=out, in0=in0, in1=in1, op=op), reads, writes)

    def stt(self, out, in0, scalar, in1, op0, op1, reads=(), writes=(), accum_out=None):
        if accum_out is not None:
            return self.op("dve", lambda e: e.scalar_tensor_tensor(out=out, in0=in0, scalar=scalar, in1=in1,
                                                                    op0=op0, op1=op1, accum_out=accum_out),
                           reads, writes)
        return self.op("dve", lambda e: e.scalar_tensor_tensor(out=out, in0=in0, scalar=scalar, in1=in1,
                                                                op0=op0, op1=op1), reads, writes)

    def cp(self, eng, out, in_, reads=(), writes=()):
        if eng == "act":
            return self.act(out, in_, AF.Copy, reads=reads, writes=writes)
        return self.op(eng, lambda e: e.tensor_copy(out=out, in_=in_), reads, writes)

    def memset(self, eng, ap, val, writes=()):
        return self.op(eng, lambda e: e.memset(ap, val), (), writes)

    def recip(self, out, in_, reads=(), writes=()):
        return self.op("dve", lambda e: e.reciprocal(out=out, in_=in_), reads, writes)

    def scan(self, out, d0, d1, reads=(), writes=()):
        return self.op("dve", lambda e: e.tensor_tensor_scan(out=out, data0=d0, data1=d1, initial=0.0,
                                                              op0=ALU.mult, op1=ALU.add), reads, writes)


_UID = [0]


def _un(n):
    _UID[0] += 1
    return "%s_%d" % (n, _UID[0])


def _sb(nc, st):
    return lambda n, s, d: st.enter_context(nc.sbuf_tensor(_un(n), s, d))


class Ring:
    def __init__(self, n, name):
        self.n = n
        self.i = 0
        self.res = [Res("%s%d" % (name, j)) for j in range(n)]

    def next(self):
        j = self.i % self.n
        self.i += 1
        return j, self.res[j]


def host_consts():
    C = {}
    C["ident"] = np.eye(128, dtype=np.float32)
    C["ones"] = np.ones((128, 128), np.float32)
    pos = np.arange(L, dtype=np.float32)
    inv = (np.float32(ROPE_THETA) ** (-np.arange(0, 16, 2, dtype=np.float32) / np.float32(16))).astype(np.float32)
    ang = (pos[:, None] * inv[None, :]).astype(np.float32).astype(np.float64)
    cos, sin = np.cos(ang).T, np.sin(ang).T
    ct = np.ones((128, L), np.float64)
    stb = np.zeros((128, L), np.float64)
    prot = np.zeros((128, 128), np.float32)
    for r in range(128):
        d = r % 64
        if d < 8:
            ct[r] = cos[d]
            stb[r] = -sin[d]
            prot[r + 8, r] = 1.0
        elif d < 16:
            ct[r] = cos[d - 8]
            stb[r] = sin[d - 8]
            prot[r - 8, r] = 1.0
    C["ropec"] = ct.astype(np.float32)
    C["ropes"] = stb.astype(np.float32)
    C["prot"] = prot
    j = np.arange(128)[:, None]
    q = np.arange(128)[None, :]
    C["maskn"] = (j <= q).astype(np.float32)
    C["maskp"] = (j >= q).astype(np.float32)
    t = np.linspace(0.0, 1.0, L)
    w = TWO_PI * np.arange(L) / L
    f = np.linspace(1e-4, HY_BANDS - 1, HY_BANDS)
    z = np.concatenate([t[None, :], np.cos(f[:, None] * w[None, :]), -np.sin(f[:, None] * w[None, :])], 0)
    deltas = np.abs(np.linspace(HY_MIN_DECAY, HY_MAX_DECAY, D_C))
    decay = np.exp(-t[None, :] * deltas[:, None])
    idx = (L - np.arange(L)) % L
    z2 = np.concatenate([z, z[:, idx]], 1)
    dec2 = np.concatenate([decay, decay[:, idx]], 1)
    dec2[:, L] = 0.0
    C["hyz"] = z2.astype(np.float32)
    C["hydec"] = dec2.astype(np.float32)
    N = 2 * L
    na = np.arange(64)[:, None]
    ka = np.arange(64)[None, :]
    a1 = TWO_PI * na * ka / 64.0
    C["e1"] = np.concatenate([np.cos(a1), -np.sin(a1)], 1).astype(np.float32)
    nb = np.arange(128)[:, None, None]
    kk = (np.arange(64)[None, :, None] + 64 * np.arange(128)[None, None, :])
    a2 = TWO_PI * ((nb * kk) % N) / N
    C["mc"] = np.cos(a2).reshape(128, 64 * 128).astype(np.float32)
    C["ms"] = np.sin(a2).reshape(128, 64 * 128).astype(np.float32)
    kb = np.arange(128)[:, None]
    nb2 = np.arange(128)[None, :]
    a3 = TWO_PI * ((kb * nb2) % 128) / 128.0
    C["g1"] = np.concatenate([np.cos(a3), np.sin(a3)], 1).astype(np.float32)
    C["g2"] = np.concatenate([-np.sin(a3), np.cos(a3)], 1).astype(np.float32)
    ka3 = np.arange(64)[:, None, None]
    nn = 128 * np.arange(32)[None, None, :] + np.arange(128)[None, :, None]
    a4 = TWO_PI * ((ka3 * nn) % N) / N
    C["ir"] = (np.cos(a4) / N).reshape(64, 128 * 32).astype(np.float32)
    C["ii"] = (-np.sin(a4) / N).reshape(64, 128 * 32).astype(np.float32)
    return C


def host_small_consts():
    C = host_consts()
    out = {k: C[k] for k in ("ident", "ones", "prot", "maskn", "maskp", "e1", "g1", "g2")}
    e1 = C["e1"]
    e1r = e1.reshape(64, 2, 64)[:, :, 0:33]
    e1z = np.zeros((128, 4, 2, 33), np.float32)
    for cl in range(4):
        e1z[cl * 32:(cl + 1) * 32, cl] = e1r[0:32]
    e1h = np.zeros((128, 2, 2, 33), np.float32)
    for cl in range(2):
        e1h[cl * 64:(cl + 1) * 64, cl] = e1r
    out["e1z"] = e1z.reshape(128, 264)
    out["e1h"] = e1h.reshape(128, 132)
    pos = np.arange(L, dtype=np.float32)
    inv = (np.float32(ROPE_THETA) ** (-np.arange(0, 16, 2, dtype=np.float32) / np.float32(16))).astype(np.float32)
    ang = (pos[:, None] * inv[None, :]).astype(np.float32).astype(np.float64)
    out["rope"] = np.concatenate([np.cos(ang).T, np.sin(ang).T], 0).astype(np.float32)
    out["hyz"] = C["hyz"][:, :L].copy()
    t = np.linspace(0.0, 1.0, L)
    idx = (L - np.arange(L)) % L
    tp = np.concatenate([t, t[idx]])
    tp[L] = 1.0e4
    out["tpos"] = tp[None, :].astype(np.float32)
    deltas = np.abs(np.linspace(HY_MIN_DECAY, HY_MAX_DECAY, D_C))
    cv = np.zeros((128, 4), np.float32)
    cv[:, 0] = -deltas[:128]
    cv[:, 1] = -deltas[128:]
    cv[0:33, 2] = 2.0
    cv[0, 2] = 1.0
    cv[32, 2] = 1.0
    out["cvec"] = cv
    return out


HEAD_PERM = [0, 3, 1, 4, 2, 5]


def _perm_heads(v, axis):
    v = np.asarray(v)
    idx = np.concatenate([np.arange(64) + 64 * h for h in HEAD_PERM])
    return np.take(v, idx, axis=axis)


def pack_params(I, l):
    pp = np.zeros((128, NPP), np.float32)

    def cols(name, vec, n):
        pp[:, PP[name]:PP[name] + n] = np.asarray(vec, np.float32).reshape(n, 128).T

    cols("g1", I["norm_mix_g"][l], 8)
    cols("g2", I["norm_mlp_g"][l], 8)
    for k in range(4):
        pp[:, PP["caw"] + 3 * k:PP["caw"] + 3 * k + 3] = np.asarray(I["conv_a_w"][l][k]).reshape(3, 128).T
    cols("cab", I["conv_a_b"][l], 3)
    for d in range(2):
        pp[:, PP["ba"] + 3 * d:PP["ba"] + 3 * d + 3] = np.asarray(I["lru_ba"][l][d]).reshape(3, 128).T
        pp[:, PP["bx"] + 3 * d:PP["bx"] + 3 * d + 3] = np.asarray(I["lru_bx"][l][d]).reshape(3, 128).T
        pp[:, PP["lam"] + 3 * d:PP["lam"] + 3 * d + 3] = np.asarray(I["lru_lambda"][l][d]).reshape(3, 128).T
    cols("gna", I["gnorm_a"][l], 3)
    cols("gnb", _perm_heads(I["gnorm_b"][l], 0), 3)
    for k in range(3):
        pp[:, PP["hcw"] + 6 * k:PP["hcw"] + 6 * k + 6] = np.asarray(I["hy_conv_w"][l][k]).reshape(6, 128).T
    cols("hcb", I["hy_conv_b"][l], 6)
    cols("hbias", I["hy_bias"][l], 2)
    cols("gnc", I["gnorm_c"][l], 2)
    pp[:64, PP["hb1"]] = I["hy_b1"][l]
    pp[:64, PP["hfr"]] = I["hy_freq"][l]
    pp[:64, PP["hb2"]] = I["hy_b2"][l]
    pp[:, PP["sink"]:PP["sink"] + 6] = np.asarray(I["attn_sink"][l])[HEAD_PERM][None, :]
    return pp


def pack_gates(I, l):
    g = np.zeros((4, 3, 128, 128), np.float32)
    for d in range(2):
        for wi, nm in enumerate(("lru_wa", "lru_wx")):
            W = np.asarray(I[nm][l][d])
            for blk in range(6):
                t, o = blk // 2, 64 * (blk % 2)
                g[2 * d + wi, t, o:o + 64, o:o + 64] = W[blk]
    return g


def phase_inproj(P, nc, T, l, xsrc, with_prep=False):
    with contextlib.ExitStack() as st:
        sb = _sb(nc, st)
        wi = sb("wi", [128, 8, D_IN], BF16)
        stg = sb("stg", [128, 2, D_IN], F32)
        ppt = sb("ppt", [128, NPP], F32)
        idf = sb("idf", [128, 128], F32)
        idb = sb("idb", [128, 128], BF16)
        xt = sb("xt", [128, 3, D], F32)
        sq2 = sb("sq", [128, 2, D], BF16)
        xn3 = sb("xn", [128, 3, D], BF16)
        hT = sb("hT", [128, 2, 8, 512], BF16)
        sm4 = sb("sm", [128, 4, 8], F32)
        ost = sb("ost", [128, 4, 512], F32)
        vst = sb("vst", [128, 2, 4, 128], F32)
        ptr = st.enter_context(nc.psum_tensor(_un("ptr"), [128, 2, 1024], BF16))
        pmm = st.enter_context(nc.psum_tensor(_un("pmm"), [128, 5, 512], F32))
        r_pp, r_id, r_wi = Res("pp"), Res("id"), Res("wi")
        P.dma("sp", ppt[:], T["pp"][l], writes=[r_pp])
        P.dma("sp", idf[:], T["ident"], writes=[r_id])
        P.cp("dve", idb[:], idf[:], reads=[r_id], writes=[r_id])
        stg_r = Ring(2, "stg")
        r_wik = [Res("wi%d" % i) for i in range(8)]
        wv = T["w_in"][l].rearrange("(kt p) n -> p kt n", p=128)
        for kt in range(8):
            j, r = stg_r.next()
            P.dma("sp", stg[:, j, :], wv[:, kt, :], writes=[r])
            if kt % 2:
                P.act(wi[:, kt, :], stg[:, j, :], AF.Copy, scale=ppt[:, PP["g1"] + kt:PP["g1"] + kt + 1],
                      reads=[r, r_pp], writes=[r_wik[kt]])
            else:
                P.ts("dve", wi[:, kt, :], stg[:, j, :], ppt[:, PP["g1"] + kt:PP["g1"] + kt + 1],
                     None, ALU.mult, reads=[r, r_pp], writes=[r_wik[kt]])
        xt_r, xn_r, hT_r, ptr_r, pmm_r = Ring(3, "xt"), Ring(3, "xn"), Ring(2, "hT"), Ring(2, "ptr"), Ring(5, "pmm")
        ost_r, vst_r = Ring(4, "ost"), Ring(2, "vst")
        sm_r = [Res("sm%d" % i) for i in range(4)]
        sq_r = Ring(2, "sq")
        xv = xsrc.rearrange("(n p) d -> n p d", p=128)
        st8 = {}
        hslot = {}
        hTres = [[Res("hT%d_%d" % (a_, b_)) for b_ in range(4)] for a_ in range(2)]
        cnt = [0]

        def s1(n):
            xj, xr = xt_r.next()
            P.dma("sp", xt[:, xj, :], xv[n], writes=[xr])
            k = n % 4
            smv, smr = sm4[:, k, :], sm_r[k]
            qj, qr = sq_r.next()
            P.act(sq2[:, qj, :], xt[:, xj, :], AF.Square, accum_out=smv[:, 0:1], reads=[xr], writes=[qr, smr])
            P.ts("dve", smv[:, 1:2], smv[:, 0:1], 1.0 / D, EPS, ALU.mult, ALU.add, reads=[smr], writes=[smr])
            P.act(smv[:, 2:3], smv[:, 1:2], AF.Sqrt, reads=[smr], writes=[smr])
            P.recip(smv[:, 3:4], smv[:, 2:3], reads=[smr], writes=[smr])
            nj, nr = xn_r.next()
            P.act(xn3[:, nj, :], xt[:, xj, :], AF.Copy, scale=smv[:, 3:4], reads=[xr, smr], writes=[nr])
            st8[n] = (nj, nr)

        def s2(n):
            nj, nr = st8.pop(n)
            ch, tt = n // 4, n % 4
            if tt == 0:
                hslot[ch] = hT_r.next()
            hj, hr = hslot[ch]
            hr = hTres[hj][tt]
            pj, pr = ptr_r.next()
            for kt in range(8):
                P.tr(ptr[:, pj, kt * 128:(kt + 1) * 128], xn3[:, nj, kt * 128:(kt + 1) * 128], idb[:],
                     reads=[nr, r_id], writes=[pr])
            P.cp("act" if tt % 2 else "dve", hT[:, hj, :, tt * 128:(tt + 1) * 128],
                 ptr[:, pj, :].rearrange("p (k t) -> p k t", k=8), reads=[pr], writes=[hr])

        def mgroup(ch, m):
            hj, hr = hslot[ch]
            hrs = hTres[hj]
            c0 = ch * 512
            if m == 10:
                vj, vr = vst_r.next()
                mj, mr = pmm_r.next()
                for tt in range(4):
                    for kt in range(8):
                        P.mm(pmm[:, mj, tt * 128:(tt + 1) * 128], hT[:, hj, kt, tt * 128:(tt + 1) * 128],
                             wi[:, kt, 1280:1408], start=(kt == 0), stop=(kt == 7), reads=[hrs[tt], r_wik[kt]],
                             writes=[mr])
                P.cp("act", vst[:, vj, :, :], pmm[:, mj, :].rearrange("p (t c) -> p t c", t=4),
                     reads=[mr], writes=[vr])
                P.dma("pool", T["pV"][c0:c0 + 512, :].rearrange("(t p) c -> p t c", p=128), vst[:, vj, :, :],
                      reads=[vr])
                return
            mj, mr = pmm_r.next()
            for kt in range(8):
                P.mm(pmm[:, mj, :], wi[:, kt, m * 128:(m + 1) * 128], hT[:, hj, kt, :], start=(kt == 0),
                     stop=(kt == 7), reads=hrs + [r_wik[kt]], writes=[mr])
            oj, orr = ost_r.next()
            P.cp("act" if cnt[0] % 2 else "dve", ost[:, oj, :], pmm[:, mj, :], reads=[mr], writes=[orr])
            cnt[0] += 1
            if m < 6:
                dst = T["pA"][m * 128:(m + 1) * 128, c0:c0 + 512]
            elif m < 10:
                dst = T["pQK"][(m - 6) * 128:(m - 5) * 128, c0:c0 + 512]
            else:
                dst = T["pC"][(m - 11) * 128:(m - 10) * 128, c0:c0 + 512]
            P.dma("pool", dst, ost[:, oj, :], reads=[orr])

        pending = []
        bgq = prep_ops(P, nc, T, sb) if with_prep else []
        for step in range(NT + 2):
            for _ in range(3):
                if bgq:
                    bgq.pop(0)()
            if step < NT:
                s1(step)
            if 0 <= step - 1 < NT:
                s2(step - 1)
                if (step - 1) % 4 == 3:
                    pending.extend([((step - 1) // 4, m) for m in range(17)])
            for _ in range(5):
                if pending:
                    mgroup(*pending.pop(0))
        while pending:
            mgroup(*pending.pop(0))
        while bgq:
            bgq.pop(0)()
        P.barrier()
        P.emit_block()


def phase_outproj(P, nc, T, l, xsrc, pre=None):
    with contextlib.ExitStack() as st:
        sb = _sb(nc, st)
        wo = sb("wo", [128, 8, D], BF16)
        stg = sb("stg", [128, 2, D], F32)
        idf = sb("idf", [128, 128], F32)
        idb = sb("idb", [128, 128], BF16)
        yc = sb("yc", [128, 2, 8, 512], BF16)
        xt = sb("xt", [128, 3, D], F32)
        x1 = sb("x1", [128, 3, D], F32)
        sq2 = sb("sq", [128, 2, D], BF16)
        xn3 = sb("xn", [128, 3, D], BF16)
        h2 = sb("h2", [128, 2, 8, 512], BF16)
        sm4 = sb("sm", [128, 4, 8], F32)
        pmm = st.enter_context(nc.psum_tensor(_un("pmm"), [128, 2, 1024], F32))
        ptr = st.enter_context(nc.psum_tensor(_un("ptr"), [128, 2, 1024], BF16))
        r_id, r_wo = Res("id"), Res("wo")
        P.dma("sp", idf[:], T["ident"], writes=[r_id])
        P.cp("dve", idb[:], idf[:], reads=[r_id], writes=[r_id])
        stg_r = Ring(2, "stg")
        r_wok = [Res("wo%d" % i) for i in range(8)]
        wv = T["w_out"][l].rearrange("(kt p) n -> p kt n", p=128)
        for kt in range(8):
            j, r = stg_r.next()
            P.dma("sp", stg[:, j, :], wv[:, kt, :], writes=[r])
            P.cp("act" if kt % 2 else "dve", wo[:, kt, :], stg[:, j, :], reads=[r], writes=[r_wok[kt]])
        yc_r, xt_r, x1_r, xn_r, h2_r = Ring(2, "yc"), Ring(3, "xt"), Ring(3, "x1"), Ring(3, "xn"), Ring(2, "h2")
        pmm_r, ptr_r = Ring(2, "pmm"), Ring(2, "ptr")
        sm_r = [Res("sm%d" % i) for i in range(4)]
        sq_r = Ring(2, "sq")
        xv = xsrc.rearrange("(n p) d -> n p d", p=128)
        x1v = T["x1"].rearrange("(n p) d -> n p d", p=128)
        yv = T["yT"].rearrange("(kt p) t -> p kt t", p=128)
        hv = T["h2T"].rearrange("(kt p) t -> p kt t", p=128)
        ycs, h2s, sA, sB = {}, {}, {}, {}

        def s1(n):
            ch, tt = n // 4, n % 4
            if tt == 0:
                yj, yr = yc_r.next()
                P.dma("sp", yc[:, yj, :, :], yv[:, :, ch * 512:(ch + 1) * 512], writes=[yr])
                ycs[ch] = (yj, yr)
            yj, yr = ycs[ch]
            xj, xr = xt_r.next()
            P.dma("sp", xt[:, xj, :], xv[n], writes=[xr])
            mj, mr = pmm_r.next()
            for half in range(2):
                for kt in range(8):
                    P.mm(pmm[:, mj, half * 512:(half + 1) * 512], yc[:, yj, kt, tt * 128:(tt + 1) * 128],
                         wo[:, kt, half * 512:(half + 1) * 512], start=(kt == 0), stop=(kt == 7),
                         reads=[yr, r_wok[kt]], writes=[mr])
            sA[n] = (xj, xr, mj, mr)

        def s2(n):
            xj, xr, mj, mr = sA.pop(n)
            oj, orr = x1_r.next()
            P.tt("dve", x1[:, oj, :], xt[:, xj, :], pmm[:, mj, :], ALU.add, reads=[xr, mr], writes=[orr])
            P.dma("pool", x1v[n], x1[:, oj, :], reads=[orr])
            k = n % 4
            smv, smr = sm4[:, k, :], sm_r[k]
            qj, qr = sq_r.next()
            P.act(sq2[:, qj, :], x1[:, oj, :], AF.Square, accum_out=smv[:, 0:1], reads=[orr], writes=[qr, smr])
            P.ts("dve", smv[:, 1:2], smv[:, 0:1], 1.0 / D, EPS, ALU.mult, ALU.add, reads=[smr], writes=[smr])
            P.act(smv[:, 2:3], smv[:, 1:2], AF.Sqrt, reads=[smr], writes=[smr])
            P.recip(smv[:, 3:4], smv[:, 2:3], reads=[smr], writes=[smr])
            nj, nr = xn_r.next()
            P.act(xn3[:, nj, :], x1[:, oj, :], AF.Copy, scale=smv[:, 3:4], reads=[orr, smr], writes=[nr])
            sB[n] = (nj, nr)

        def s3(n):
            nj, nr = sB.pop(n)
            ch, tt = n // 4, n % 4
            if tt == 0:
                h2s[ch] = h2_r.next()
            hj, hr = h2s[ch]
            pj, pr = ptr_r.next()
            for kt in range(8):
                P.tr(ptr[:, pj, kt * 128:(kt + 1) * 128], xn3[:, nj, kt * 128:(kt + 1) * 128], idb[:],
                     reads=[nr, r_id], writes=[pr])
            P.cp("act", h2[:, hj, :, tt * 128:(tt + 1) * 128],
                 ptr[:, pj, :].rearrange("p (k t) -> p k t", k=8), reads=[pr], writes=[hr])
            if tt == 3:
                P.dma("pool", hv[:, :, ch * 512:(ch + 1) * 512], h2[:, hj, :, :], reads=[hr])

        bgq = []
        if pre is not None:
            stgw = sb("stgw", [128, 2, 1024], F32)
            pp2 = sb("pp2", [128, NPP], F32)
            r_pp2 = Res("pp2")
            P.dma("sp", pp2[:], T["pp"][l], writes=[r_pp2])
            sw_r = Ring(2, "stgw")
            uv = T["w_up"][l].rearrange("(kt p) n -> p kt n", p=128)
            wu_p, r_wuh = pre["wu"], pre["r_wuh"]

            def wu_load(q4, kt, k):
                def f():
                    j, r = sw_r.next()
                    P.dma("sp", stgw[:, j, :], uv[:, kt, q4 * 1024:(q4 + 1) * 1024], writes=[r])
                    gcol = pp2[:, PP["g2"] + kt:PP["g2"] + kt + 1]
                    if k % 2:
                        P.act(wu_p[:, kt, q4 * 1024:(q4 + 1) * 1024], stgw[:, j, :], AF.Copy, scale=gcol,
                              reads=[r, r_pp2], writes=[r_wuh[q4][kt]])
                    else:
                        P.ts("dve", wu_p[:, kt, q4 * 1024:(q4 + 1) * 1024], stgw[:, j, :], gcol, None, ALU.mult,
                             reads=[r, r_pp2], writes=[r_wuh[q4][kt]])
                return f

            k = 0
            for q4 in range(4):
                for kt in range(8):
                    bgq.append(wu_load(q4, kt, k))
                    k += 1
        for step in range(NT + 2):
            if step < NT:
                s1(step)
            if 0 <= step - 1 < NT:
                s2(step - 1)
            if 0 <= step - 2 < NT:
                s3(step - 2)
            if bgq:
                bgq.pop(0)()
        while bgq:
            bgq.pop(0)()
        P.barrier()
        P.emit_block()


def phase_mlp(P, nc, T, l, xdst, last, pre=None):
    CH = 512
    with contextlib.ExitStack() as st:
        sb = _sb(nc, st)
        wu = pre["wu"] if pre is not None else sb("wu", [128, 8, D_FF], BF16)
        wd = sb("wd", [128, 32, D], BF16)
        stg = sb("stg", [128, 2, 1024], F32)
        ppt = sb("ppt", [128, NPP], F32)
        hc = sb("hc", [128, 2, 8, CH], BF16)
        aT = sb("aT", [128, 32, CH], BF16)
        rr = sb("rr", [128, 2, CH], F32)
        x1 = sb("x1", [128, 2, D], F32)
        sm = sb("sm", [128, 8], F32)
        if last:
            fg = sb("fg", [128, D], F32)
            sq = sb("sq", [128, D], BF16)
        pup = st.enter_context(nc.psum_tensor(_un("pup"), [128, 4, 512], F32))
        pdn = st.enter_context(nc.psum_tensor(_un("pdn"), [128, 2, 1024], F32))
        r_pp, r_wu, r_wd, r_fg = Res("pp"), Res("wu"), Res("wd"), Res("fg")
        P.dma("sp", ppt[:], T["pp"][l], writes=[r_pp])
        if last:
            P.dma("sp", fg[:], T["fng"], writes=[r_fg])
        stg_r = Ring(2, "stg")
        hc_r = Ring(2, "hc")
        hv = T["h2T"].rearrange("(kt p) t -> p kt t", p=128)
        hc0 = hc_r.next()
        P.dma("sp", hc[:, hc0[0], :, :], hv[:, :, 0:CH], writes=[hc0[1]])
        uv = T["w_up"][l].rearrange("(kt p) n -> p kt n", p=128)
        r_wuh = pre["r_wuh"] if pre is not None else [[Res("wu") for _ in range(8)] for _ in range(4)]
        r_wdf = [Res("wd%d" % i) for i in range(32)]
        bgq = []

        def wu_load(q4, kt, k):
            def f():
                j, r = stg_r.next()
                P.dma("sp", stg[:, j, :], uv[:, kt, q4 * 1024:(q4 + 1) * 1024], writes=[r])
                if k % 2:
                    P.act(wu[:, kt, q4 * 1024:(q4 + 1) * 1024], stg[:, j, :], AF.Copy,
                          scale=ppt[:, PP["g2"] + kt:PP["g2"] + kt + 1], reads=[r, r_pp], writes=[r_wuh[q4][kt]])
                else:
                    P.ts("dve", wu[:, kt, q4 * 1024:(q4 + 1) * 1024], stg[:, j, :],
                         ppt[:, PP["g2"] + kt:PP["g2"] + kt + 1], None, ALU.mult, reads=[r, r_pp],
                         writes=[r_wuh[q4][kt]])
            return f

        dv = T["w_down"][l].rearrange("(ft p) n -> p ft n", p=128)

        def wd_load(f2):
            def f():
                j, r = stg_r.next()
                P.dma("sp", stg[:, j, :], dv[:, f2, :], writes=[r])
                P.cp("act" if f2 % 2 else "dve", wd[:, f2, :], stg[:, j, :], reads=[r], writes=[r_wdf[f2]])
            return f

        k = 0
        for q4 in range(4):
            for kt in range(8):
                if pre is None:
                    if q4 < 2:
                        wu_load(q4, kt, k)()
                    else:
                        bgq.append(wu_load(q4, kt, k))
                k += 1
        for f2 in range(32):
            bgq.append(wd_load(f2))
        rr_r, x1_r, pup_r, pdn_r = (Ring(2, "rr"), Ring(2, "x1"), Ring(4, "pup"), Ring(2, "pdn"))
        r_aT, sm_r, r_sq = Res("aT"), Res("sm"), Res("sq")
        x1v = T["x1"].rearrange("(n p) d -> n p d", p=128)
        xov = xdst.rearrange("(n p) d -> n p d", p=128)
        for ch in range(L // CH):
            c0 = ch * CH
            if ch == 0:
                hj, hr = hc0
            else:
                hj, hr = hc_r.next()
                P.dma("sp", hc[:, hj, :, :], hv[:, :, c0:c0 + CH], writes=[hr])
            for ft in range(32):
                uj, ur = pup_r.next()
                for kt in range(8):
                    P.mm(pup[:, uj, 0:CH], wu[:, kt, ft * 128:(ft + 1) * 128], hc[:, hj, kt, :], start=(kt == 0),
                         stop=(kt == 7), reads=[hr, r_wuh[ft // 8][kt]], writes=[ur])
                rj, rres = rr_r.next()
                P.act(rr[:, rj, :], pup[:, uj, 0:CH], AF.Relu, reads=[ur], writes=[rres])
                P.tt("pool" if ft % 4 == 3 else "dve", aT[:, ft, :], rr[:, rj, :], rr[:, rj, :], ALU.mult,
                     reads=[rres], writes=[r_aT])
                for _ in range((1 if ft < 16 else 2) if pre is None else 1):
                    if bgq:
                        bgq.pop(0)()
            while bgq:
                bgq.pop(0)()
            for tt in range(CH // 128):
                n = ch * (CH // 128) + tt
                xj, xr = x1_r.next()
                P.dma("sp", x1[:, xj, :], x1v[n], writes=[xr])
                dj, dr = pdn_r.next()
                for half in range(2):
                    for ft in range(32):
                        P.mm(pdn[:, dj, half * 512:(half + 1) * 512], aT[:, ft, tt * 128:(tt + 1) * 128],
                             wd[:, ft, half * 512:(half + 1) * 512], start=(ft == 0), stop=(ft == 31),
                             reads=[r_aT, r_wdf[ft]], writes=[dr])
                P.tt("dve", x1[:, xj, :], x1[:, xj, :], pdn[:, dj, :], ALU.add, reads=[dr], writes=[xr])
                if last:
                    P.act(sq[:], x1[:, xj, :], AF.Square, accum_out=sm[:, 0:1], reads=[xr], writes=[r_sq, sm_r])
                    P.ts("dve", sm[:, 1:2], sm[:, 0:1], 1.0 / D, EPS, ALU.mult, ALU.add, reads=[sm_r], writes=[sm_r])
                    P.act(sm[:, 2:3], sm[:, 1:2], AF.Sqrt, reads=[sm_r], writes=[sm_r])
                    P.recip(sm[:, 3:4], sm[:, 2:3], reads=[sm_r], writes=[sm_r])
                    P.stt(x1[:, xj, :], x1[:, xj, :], sm[:, 3:4], fg[:], ALU.mult, ALU.mult,
                          reads=[sm_r, r_fg], writes=[xr])
                P.dma("pool", xov[n], x1[:, xj, :], reads=[xr])
        P.barrier()
        P.emit_block()


def dve_mod8192(P, X, Tm, res):
    P.ts("dve", Tm, X, 1.0 / 8192.0, -0.49999, ALU.mult, ALU.add, reads=[res], writes=[res])
    P.ts("dve", Tm, Tm, MAGIC, None, ALU.add, reads=[res], writes=[res])
    P.ts("dve", Tm, Tm, -MAGIC, -8192.0, ALU.add, ALU.mult, reads=[res], writes=[res])
    P.tt("dve", X, X, Tm, ALU.add, reads=[res], writes=[res])


def prep_ops(P, nc, T, sb):
    I32 = mybir.dt.int32
    W = 2048
    ki = sb("pki", [128, W], I32)
    X = sb("pX", [128, W], F32)
    Tm = sb("pTm", [128, W], F32)
    Yc = sb("pY", [128, W], F32)
    ob = sb("pob", [128, 3, W], BF16)
    pi = sb("ppi", [128, 1], I32)
    pf = sb("ppf", [128, 1], F32)
    pf_pi = sb("ppfpi", [128, 1], F32)
    pf_npi = sb("ppfnpi", [128, 1], F32)
    r = Res("prep")
    sc = TWO_PI / 8192.0
    ops = []
    A = ops.append
    A(lambda: P.memset("dve", pf_pi[:], math.pi, writes=[r]))
    A(lambda: P.memset("dve", pf_npi[:], -math.pi, writes=[r]))
    A(lambda: P.op("pool", lambda e: e.iota(pi[:], pattern=[[0, 1]], base=0, channel_multiplier=1), writes=[r]))
    A(lambda: P.cp("dve", pf[:], pi[:], reads=[r], writes=[r]))

    def gen(npart, pattern, base, dsts, col0, want, rowscale=None):
        x, t, y = X[:npart, :], Tm[:npart, :], Yc[:npart, :]
        A(lambda: P.op("pool", lambda e: e.iota(ki[:npart, :], pattern=pattern, base=base, channel_multiplier=0),
                       writes=[r]))
        A(lambda: P.cp("dve", x, ki[:npart, :], reads=[r], writes=[r]))
        A(lambda: P.ts("dve", x, x, pf[:npart, :], None, ALU.mult, reads=[r], writes=[r]))
        A(lambda: P.ts("dve", t, x, 1.0 / 8192.0, -0.49999, ALU.mult, ALU.add, reads=[r], writes=[r]))
        A(lambda: P.ts("dve", t, t, MAGIC, None, ALU.add, reads=[r], writes=[r]))
        A(lambda: P.ts("dve", t, t, -MAGIC, -8192.0, ALU.add, ALU.mult, reads=[r], writes=[r]))
        A(lambda: P.tt("dve", x, x, t, ALU.add, reads=[r], writes=[r]))
        if want[1]:
            A(lambda: P.act(ob[:npart, 1, :], x, AF.Sin, bias=pf_pi[:npart, :], scale=-sc, reads=[r], writes=[r]))
        A(lambda: P.act(ob[:npart, 2, :], x, AF.Sin, bias=pf_npi[:npart, :], scale=sc, reads=[r], writes=[r]))
        A(lambda: P.ts("dve", y, x, 6144.0, -8192.0, ALU.is_ge, ALU.mult, reads=[r], writes=[r]))
        A(lambda: P.stt(y, x, 2048.0, y, ALU.add, ALU.add, reads=[r], writes=[r]))
        A(lambda: P.act(ob[:npart, 0, :], y, AF.Sin, bias=pf_pi[:npart, :], scale=-sc, reads=[r], writes=[r]))
        for j, d in enumerate(dsts):
            if d is not None:
                if rowscale is not None:
                    A(lambda j=j: P.ts("dve", ob[:npart, j, :], ob[:npart, j, :], rowscale, None, ALU.mult,
                                       reads=[r], writes=[r]))
                A(lambda j=j, d=d: P.dma("pool", d[:, col0:col0 + W], ob[:npart, j, :], reads=[r]))

    for chn in range(4):
        gen(128, [[1, 16], [64, 128]], 16 * chn, [T["mcb"], T["msb"], T["msnb"]], chn * W, (1, 1, 1))
    wcol = sb("pwcol", [128, 4], F32)
    A(lambda: P.dma("sp", wcol[:], T["cvec"], writes=[r]))
    for chn in range(2):
        gen(64, [[1, 64], [128, 32]], 64 * chn, [T["irb"], None, T["iib"]], chn * W, (1, 0, 1),
            rowscale=wcol[0:64, 2:3])
    return ops


def phase_prep_only(P, nc, T):
    with contextlib.ExitStack() as st:
        sb = _sb(nc, st)
        for f in prep_ops(P, nc, T, sb):
            f()
        P.barrier()
        P.emit_block()


def phase_prep(P, nc, T):
    with contextlib.ExitStack() as st:
        sb = _sb(nc, st)
        I32 = mybir.dt.int32
        ki = sb("ki", [128, 8192], I32)
        X = sb("X", [128, 8192], F32)
        Tm = sb("Tm", [128, 8192], F32)
        Y = sb("Y", [128, 8192], F32)
        ob = sb("ob", [128, 3, 8192], BF16)
        pi = sb("pi", [128, 1], I32)
        pf = sb("pf", [128, 1], F32)
        r = Res("prep")
        s = TWO_PI / 8192.0
        P.op("pool", lambda e: e.iota(pi[:], pattern=[[0, 1]], base=0, channel_multiplier=1), writes=[r])
        P.cp("dve", pf[:], pi[:], reads=[r], writes=[r])

        def gen(npart, pattern, dsts):
            P.op("pool", lambda e: e.iota(ki[:npart, :], pattern=pattern, base=0, channel_multiplier=0), writes=[r])
            P.cp("dve", X[:npart, :], ki[:npart, :], reads=[r], writes=[r])
            P.ts("dve", X[:npart, :], X[:npart, :], pf[:npart, :], None, ALU.mult, reads=[r], writes=[r])
            dve_mod8192(P, X[:npart, :], Tm[:npart, :], r)
            P.act(ob[:npart, 1, :], X[:npart, :], AF.Sin, bias=pf_pi[:npart, :], scale=-s, reads=[r], writes=[r])
            P.act(ob[:npart, 2, :], X[:npart, :], AF.Sin, bias=pf_npi[:npart, :], scale=s, reads=[r], writes=[r])
            P.ts("dve", Y[:npart, :], X[:npart, :], 6144.0, -8192.0, ALU.is_ge, ALU.mult, reads=[r], writes=[r])
            P.stt(Y[:npart, :], X[:npart, :], 2048.0, Y[:npart, :], ALU.add, ALU.add, reads=[r], writes=[r])
            P.act(ob[:npart, 0, :], Y[:npart, :], AF.Sin, bias=pf_pi[:npart, :], scale=-s, reads=[r], writes=[r])
            for j, d in enumerate(dsts):
                if d is not None:
                    P.dma("sp", d, ob[:npart, j, 0:d.shape[1]], reads=[r])

        pf_pi = sb("pfpi", [128, 1], F32)
        pf_npi = sb("pfnpi", [128, 1], F32)
        P.memset("dve", pf_pi[:], math.pi, writes=[r])
        P.memset("dve", pf_npi[:], -math.pi, writes=[r])
        gen(128, [[128, 64], [1, 128]] if False else [[1, 64], [64, 128]], [T["mcb"], T["msb"], T["msnb"]])
        P.barrier()
        P.emit_block()
    with contextlib.ExitStack() as st:
        sb = _sb(nc, st)
        I32 = mybir.dt.int32
        ki = sb("ki", [64, 4096], I32)
        X = sb("X", [64, 4096], F32)
        Tm = sb("Tm", [64, 4096], F32)
        Y = sb("Y", [64, 4096], F32)
        ob = sb("ob", [64, 3, 4096], BF16)
        pi = sb("pi", [64, 1], I32)
        pf = sb("pf", [64, 1], F32)
        pf_pi = sb("pfpi", [64, 1], F32)
        pf_npi = sb("pfnpi", [64, 1], F32)
        r = Res("prep2")
        s = TWO_PI / 8192.0
        P.memset("dve", pf_pi[:], math.pi, writes=[r])
        P.memset("dve", pf_npi[:], -math.pi, writes=[r])
        P.op("pool", lambda e: e.iota(pi[:], pattern=[[0, 1]], base=0, channel_multiplier=1), writes=[r])
        P.cp("dve", pf[:], pi[:], reads=[r], writes=[r])
        P.op("pool", lambda e: e.iota(ki[:], pattern=[[1, 128], [128, 32]], base=0, channel_multiplier=0), writes=[r])
        P.cp("dve", X[:], ki[:], reads=[r], writes=[r])
        P.ts("dve", X[:], X[:], pf[:], None, ALU.mult, reads=[r], writes=[r])
        dve_mod8192(P, X[:], Tm[:], r)
        P.act(ob[:, 2, :], X[:], AF.Sin, bias=pf_npi[:], scale=s, reads=[r], writes=[r])
        P.ts("dve", Y[:], X[:], 6144.0, -8192.0, ALU.is_ge, ALU.mult, reads=[r], writes=[r])
        P.stt(Y[:], X[:], 2048.0, Y[:], ALU.add, ALU.add, reads=[r], writes=[r])
        P.act(ob[:, 0, :], Y[:], AF.Sin, bias=pf_pi[:], scale=-s, reads=[r], writes=[r])
        P.dma("sp", T["irb"], ob[:, 0, :], reads=[r])
        P.dma("sp", T["iib"], ob[:, 2, :], reads=[r])
        P.barrier()
        P.emit_block()


def phase_lru(P, nc, T, l):
    with contextlib.ExitStack() as st:
        sb = _sb(nc, st)
        ppt = sb("ppt", [128, NPP], F32)
        gst = sb("gst", [128, 4, 128], F32)
        gw = sb("gw", [128, 4, 128], BF16)
        U = sb("U", [128, L + 3], F32)
        XC = sb("XC", [128, L], F32)
        XCB = sb("XCB", [128, L], BF16)
        UB = sb("UB", [128, L + 3], BF16)
        DG = sb("DG", [128, 4, 128], BF16)
        idf = sb("idf", [128, 128], F32)
        r_UB, r_DG = Res("UB"), Res("DG")
        Ad = [sb("A%d" % d, [128, L], F32) for d in range(2)]
        Bd = [sb("B%d" % d, [128, L], F32) for d in range(2)]
        TMP = sb("TMP", [128, L], F32)
        G = sb("G", [128, L], F32)
        r_G = Res("G")
        YA = sb("YA", [128, 3, L], F32)
        cs = sb("cs", [128, 8], F32)
        onf = sb("onf", [128, 128], F32)
        onb = sb("onb", [128, 128], BF16)
        sqb = sb("sqb", [128, 3, 512], BF16)
        rst = sb("rst", [128, 512], F32)
        ob = sb("ob", [128, 2, 512], BF16)
        pg = st.enter_context(nc.psum_tensor(_un("pg"), [128, 4, 512], F32))
        pn = st.enter_context(nc.psum_tensor(_un("pn"), [128, 2, 512], F32))
        r_pp, r_on, r_gw, r_U, r_XC, r_XCB, r_T, r_cs = (Res("pp"), Res("on"), Res("gw"), Res("U"), Res("XC"),
                                                         Res("XCB"), Res("TMP"), Res("cs"))
        r_A = [Res("A0"), Res("A1")]
        r_B = [Res("B0"), Res("B1")]
        r_YA = [Res("YA%d" % i) for i in range(3)]
        pg_r, pn_r, ob_r = Ring(4, "pg"), Ring(2, "pn"), Ring(2, "ob")
        r_sq, r_rst, r_gst = Res("sq"), Res("rst"), Res("gst")
        P.dma("sp", ppt[:], T["pp"][l], writes=[r_pp])
        P.dma("sp", onf[:], T["ones"], writes=[r_on])
        P.dma("sp", idf[:], T["ident"], writes=[r_on])
        P.cp("dve", onb[:], onf[:], reads=[r_on], writes=[r_on])
        for ta in range(3):
            c = lambda nm, k=0: ppt[:, PP[nm] + k:PP[nm] + k + 1]
            P.dma("sp", gst[:], T["gates"][l, :, ta, :, :].rearrange("g p m -> p g m"), writes=[r_gst])
            P.cp("dve", gw[:], gst[:], reads=[r_gst], writes=[r_gw])
            for d in range(2):
                P.act(cs[:, d:d + 1], c("lam", 3 * d + ta), AF.Exp, scale=-1.0, reads=[r_pp], writes=[r_cs])
            for d in range(2):
                P.act(cs[:, d:d + 1], cs[:, d:d + 1], AF.Ln, bias=1.0, reads=[r_cs], writes=[r_cs])
            P.ts("dve", cs[:, 0:2], cs[:, 0:2], -8.0, None, ALU.mult, reads=[r_cs], writes=[r_cs])
            P.dma("sp", G[:], T["pA"][384 + ta * 128:384 + (ta + 1) * 128, :], writes=[r_G])
            P.act(G[:], G[:], AF.Gelu, reads=[r_G], writes=[r_G])
            P.memset("pool", U[:, 0:2], 0.0, writes=[r_U])
            P.memset("pool", U[:, L + 2:L + 3], 0.0, writes=[r_U])
            P.dma("sp", U[:, 2:L + 2], T["pA"][ta * 128:(ta + 1) * 128, :], writes=[r_U])
            P.cp("act", UB[:], U[:], reads=[r_U], writes=[r_UB])
            for k in range(4):
                P.ts("dve", DG[:, k, :], idf[:], c("caw", 3 * k + ta), None, ALU.mult, reads=[r_on, r_pp],
                     writes=[r_DG])
            for ch in range(8):
                j, r = pg_r.next()
                for k in range(4):
                    P.mm(pg[:, j, :], DG[:, k, :], UB[:, ch * 512 + k:ch * 512 + k + 512], start=(k == 0), stop=(k == 3),
                         reads=[r_DG, r_UB], writes=[r])
                P.act(XC[:, ch * 512:(ch + 1) * 512], pg[:, j, :], AF.Identity, bias=c("cab", ta), reads=[r, r_pp],
                      writes=[r_XC])
            P.cp("dve", XCB[:], XC[:], reads=[r_XC], writes=[r_XCB])
            for d in range(2):
                for ch in range(8):
                    sl = slice(ch * 512, (ch + 1) * 512)
                    j, r = pg_r.next()
                    P.mm(pg[:, j, :], gw[:, 2 * d, :], XCB[:, sl], reads=[r_gw, r_XCB], writes=[r])
                    P.act(Ad[d][:, sl], pg[:, j, :], AF.Sigmoid, bias=c("ba", 3 * d + ta), reads=[r, r_pp],
                          writes=[r_A[d]])
                    j, r = pg_r.next()
                    P.mm(pg[:, j, :], gw[:, 2 * d + 1, :], XCB[:, sl], reads=[r_gw, r_XCB], writes=[r])
                    P.act(Bd[d][:, sl], pg[:, j, :], AF.Sigmoid, bias=c("bx", 3 * d + ta), reads=[r, r_pp],
                          writes=[r_B[d]])
                scr, r_scr = (TMP[:], r_T) if d == 0 else (U[:, 0:L], r_U)
                P.act(Ad[d][:], Ad[d][:], AF.Exp, scale=cs[:, d:d + 1], reads=[r_cs], writes=[r_A[d]])
                P.act(scr, Ad[d][:], AF.Square, reads=[r_A[d]], writes=[r_scr])
                P.act(scr, scr, AF.Sqrt, bias=1.0, scale=-1.0, reads=[r_scr], writes=[r_scr])
                P.tt("pool", Bd[d][:], Bd[d][:], XC[:], ALU.mult, reads=[r_XC], writes=[r_B[d]])
                P.tt("dve", Bd[d][:], Bd[d][:], scr, ALU.mult, reads=[r_scr], writes=[r_B[d]])
                if d == 0:
                    P.scan(TMP[:], Ad[0][:], Bd[0][:], reads=[r_A[0], r_B[0]], writes=[r_T])
                else:
                    P.scan(XC[:, ::-1], Ad[1][:, ::-1], Bd[1][:, ::-1], reads=[r_A[1], r_B[1]], writes=[r_XC])
            P.tt("dve", TMP[:], TMP[:], XC[:], ALU.add, reads=[r_XC], writes=[r_T])
            P.tt("dve", YA[:, ta, :], TMP[:], G[:], ALU.mult, reads=[r_T, r_G], writes=[r_YA[ta]])
        for ch in range(8):
            sl = slice(ch * 512, (ch + 1) * 512)
            for ta in range(3):
                P.act(sqb[:, ta, :], YA[:, ta, sl], AF.Square, reads=[r_YA[ta]], writes=[r_sq])
            j, r = pn_r.next()
            for ta in range(3):
                P.mm(pn[:, j, :], onb[:], sqb[:, ta, :], start=(ta == 0), stop=(ta == 2), reads=[r_on, r_sq],
                     writes=[r])
            P.act(rst[:], pn[:, j, :], AF.Sqrt, bias=EPS, scale=1.0 / D_A, reads=[r], writes=[r_rst])
            P.recip(rst[:], rst[:], reads=[r_rst], writes=[r_rst])
            for ta in range(3):
                oj, orr = ob_r.next()
                P.stt(ob[:, oj, :], YA[:, ta, sl], ppt[:, PP["gna"] + ta:PP["gna"] + ta + 1], rst[:], ALU.mult,
                      ALU.mult, reads=[r_YA[ta], r_rst, r_pp], writes=[orr])
                P.dma("pool", T["yT"][ta * 128:(ta + 1) * 128, sl], ob[:, oj, :], reads=[orr])
        P.barrier()
        P.emit_block()


def phase_attn(P, nc, T, l):
    with contextlib.ExitStack() as st:
        sb = _sb(nc, st)
        ppt = sb("ppt", [128, NPP], F32)
        idf = sb("idf", [128, 128], F32)
        idb = sb("idb", [128, 128], BF16)
        prf = sb("prf", [128, 128], F32)
        prb = sb("prb", [128, 128], BF16)
        mkf = sb("mkf", [128, 2, 128], F32)
        mkb = sb("mkb", [128, 2, 128], BF16)
        ct = sb("ct", [128, L], F32)
        stt_ = sb("st", [128, L], F32)
        S2 = sb("S", [128, 2, L], F32)
        XB2 = sb("XB", [128, 2, L], BF16)
        Q = sb("Q", [128, 3, L], BF16)
        KK = sb("KK", [128, 2, L], BF16)
        VA = sb("VA", [128, NT, 2, 65], BF16)
        t1 = sb("t1", [128, 2, 512], F32)
        t2 = sb("t2", [128, 2, 512], F32)
        PT = sb("PT", [128, 6, 6, 384], BF16)
        es = sb("es", [128, 6], F32)
        den = sb("den", [128, 2, 6], F32)
        yb = sb("yb", [128, 2, 384], F32)
        ybn = sb("ybn", [128, 3, 384], BF16)
        sqf = sb("sqf", [128, 384], F32)
        epsb = sb("epsb", [128, 1], F32)
        sm4 = sb("sm", [128, 4, 8], F32)
        YT = sb("YT", [128, 3, L], BF16)
        ps = st.enter_context(nc.psum_tensor(_un("ps"), [128, 4, 512], F32))
        po = st.enter_context(nc.psum_tensor(_un("po"), [128, 2, 512], F32))
        ptr = st.enter_context(nc.psum_tensor(_un("ptr"), [128, 2, 1024], BF16))
        r_pp, r_c, r_tab, r_S, r_XB, r_VA, r_es, r_sm, r_YT, r_sq = (Res("pp"), Res("c"), Res("tab"), Res("S"),
                                                                    Res("XB"), Res("VA"), Res("es"), Res("sm"),
                                                                    Res("YT"), Res("sq"))
        r_Q = [Res("Q%d" % i) for i in range(3)]
        r_K = [Res("K%d" % i) for i in range(2)]
        ps_r, po_r, ptr_r, t1_r, t2_r, PT_r = (Ring(4, "ps"), Ring(2, "po"), Ring(2, "ptr"), Ring(2, "t1"),
                                               Ring(2, "t2"), [Res("PT%d" % i) for i in range(6)])
        den_r, yb_r, ybn_r = Ring(2, "den"), Ring(2, "yb"), Ring(3, "ybn")
        P.dma("sp", ppt[:], T["pp"][l], writes=[r_pp])
        P.dma("sp", idf[:], T["ident"], writes=[r_c])
        P.dma("sp", prf[:], T["prot"], writes=[r_c])
        P.dma("sp", mkf[:, 0, :], T["maskn"], writes=[r_c])
        P.dma("sp", mkf[:, 1, :], T["maskp"], writes=[r_c])
        P.cp("dve", idb[:], idf[:], reads=[r_c], writes=[r_c])
        P.cp("dve", prb[:], prf[:], reads=[r_c], writes=[r_c])
        P.cp("dve", mkb[:], mkf[:], reads=[r_c], writes=[r_c])
        P.memset("pool", ct[:], 1.0, writes=[r_tab])
        P.memset("pool", stt_[:], 0.0, writes=[r_tab])
        for base in (0, 64):
            for off in (0, 8):
                P.dma("sp", ct[base + off:base + off + 8, :], T["rope"][0:8, :], writes=[r_tab])
                P.dma("sp", stt_[base + off:base + off + 8, :], T["rope"][8:16, :], writes=[r_tab])
        for base in (0, 64):
            P.ts("dve", stt_[base:base + 8, :], stt_[base:base + 8, :], -1.0, None, ALU.mult, writes=[r_tab])
        P.act(es[:], ppt[:, PP["sink"]:PP["sink"] + 6], AF.Exp, reads=[r_pp], writes=[r_es])
        P.memset("dve", epsb[:], EPS, writes=[r_es])
        S_r, XB_r = Ring(2, "S"), Ring(2, "XB")
        sj, sr = S_r.next()
        P.dma("sp", S2[:, sj, :].rearrange("p (n c) -> p n c", c=128), T["pV"].rearrange("(n p) c -> p n c", p=128),
              writes=[sr])
        P.memset("pool", VA[:], 1.0, writes=[r_VA])
        P.cp("dve", VA[:, :, :, 0:64], S2[:, sj, :].rearrange("p (n g c) -> p n g c", g=2, c=64), reads=[sr],
             writes=[r_VA])

        def rope(dst, dres, loads):
            sj, sr = S_r.next()
            for (pr, src) in loads:
                P.dma("sp", S2[pr, sj, :], src, writes=[sr])
            bj, br = XB_r.next()
            P.cp("act", XB2[:, bj, :], S2[:, sj, :], reads=[sr], writes=[br])
            for ch in range(8):
                sl = slice(ch * 512, (ch + 1) * 512)
                j, r = ps_r.next()
                P.mm(ps[:, j, :], prb[:], XB2[:, bj, sl], reads=[r_c, br], writes=[r])
                j1, r1 = t1_r.next()
                P.tt("dve", t1[:, j1, :], ps[:, j, :], stt_[:, sl], ALU.mult, reads=[r, r_tab], writes=[r1])
                j2, r2 = t2_r.next()
                P.tt("pool", t2[:, j2, :], S2[:, sj, sl], ct[:, sl], ALU.mult, reads=[sr, r_tab], writes=[r2])
                P.tt("dve", dst[:, sl], t1[:, j1, :], t2[:, j2, :], ALU.add, reads=[r1, r2], writes=[dres])

        for qt in range(3):
            rope(Q[:, qt, :], r_Q[qt], [(slice(0, 128), T["pQK"][qt * 128:(qt + 1) * 128, :])])
        rope(KK[:, 0, :], r_K[0], [(slice(0, 128), T["pQK"][384:512, :])])

        NPT = 6
        stB = {}
        sm_r = [Res("sm%d" % i) for i in range(4)]

        def stage_b(i):
            oj, orr = po_r.next()
            for h in range(6):
                g = h % 2
                kbs = [kb for kb in (i - 1, i, i + 1) if 0 <= kb < NT]
                for n, kb in enumerate(kbs):
                    pos = i - kb + 1
                    P.mm(po[:, oj, h * 65:(h + 1) * 65], PT[:, kb % NPT, h, pos * 128:(pos + 1) * 128],
                         VA[:, kb, g, :], start=(n == 0), stop=(n == len(kbs) - 1), reads=[PT_r[kb % NPT], r_VA],
                         writes=[orr])
            dj, dr = den_r.next()
            ov = po[:, oj, 0:390].rearrange("p (h c) -> p h c", c=65)
            P.tt("dve", den[:, dj, :], ov[:, :, 64], es[:], ALU.add, reads=[orr, r_es], writes=[dr])
            P.recip(den[:, dj, :], den[:, dj, :], reads=[dr], writes=[dr])
            yj, yr = yb_r.next()
            P.tt("dve", yb[:, yj, :].rearrange("p (h c) -> p h c", c=64), ov[:, :, 0:64],
                 den[:, dj, :].unsqueeze(2).broadcast_to([128, 6, 64]), ALU.mult, reads=[orr, dr], writes=[yr])
            k = i % 4
            smv, smr = sm4[:, k, :], sm_r[k]
            P.stt(sqf[:], yb[:, yj, :], 1.0, yb[:, yj, :], ALU.mult, ALU.mult, reads=[yr], writes=[r_sq, smr],
                  accum_out=smv[:, 0:1])
            P.act(smv[:, 2:3], smv[:, 0:1], AF.Ln, bias=epsb[:], scale=1.0 / D_B, reads=[smr, r_es], writes=[smr])
            P.act(smv[:, 3:4], smv[:, 2:3], AF.Exp, scale=-0.5, reads=[smr], writes=[smr])
            nj, nr = ybn_r.next()
            P.ts("dve", ybn[:, nj, :], yb[:, yj, :], smv[:, 3:4], None, ALU.mult, reads=[yr, smr], writes=[nr])
            stB[i] = (nj, nr)

        def stage_c(i):
            nj, nr = stB.pop(i)
            tj, trr = ptr_r.next()
            for tb in range(3):
                P.tr(ptr[:, tj, tb * 128:(tb + 1) * 128], ybn[:, nj, tb * 128:(tb + 1) * 128], idb[:],
                     reads=[nr, r_c], writes=[trr])
            for tb in range(3):
                P.ts("dve", YT[:, tb, i * 128:(i + 1) * 128], ptr[:, tj, tb * 128:(tb + 1) * 128],
                     ppt[:, PP["gnb"] + tb:PP["gnb"] + tb + 1], None, ALU.mult, reads=[trr, r_pp], writes=[r_YT])

        def stage_a(jb):
            lo_b, hi_b = max(jb - 1, 0), min(jb + 1, NT - 1)
            slot = jb % NPT
            for h in range(6):
                g, o, qt = h % 2, 64 * (h % 2), h // 2
                kt = 0 if o == 64 * g else 1
                c0 = (lo_b - jb + 1) * 128
                ncol = (hi_b - lo_b + 1) * 128
                j, r = ps_r.next()
                P.mm(ps[:, j, 0:ncol], KK[o:o + 64, kt, jb * 128:(jb + 1) * 128],
                     Q[o:o + 64, qt, lo_b * 128:(hi_b + 1) * 128], reads=[r_K[kt], r_Q[qt]], writes=[r])
                P.act(PT[:, slot, h, c0:c0 + ncol], ps[:, j, 0:ncol], AF.Exp, scale=0.125, reads=[r],
                      writes=[PT_r[slot]])
            if jb > 0:
                P.tt("dve", PT[:, slot, :, 0:128], PT[:, slot, :, 0:128], mkb[:, 0:1, :].broadcast_to([128, 6, 128]),
                     ALU.mult, reads=[r_c], writes=[PT_r[slot]])
            if jb < NT - 1:
                P.tt("dve", PT[:, slot, :, 256:384], PT[:, slot, :, 256:384],
                     mkb[:, 1:2, :].broadcast_to([128, 6, 128]), ALU.mult, reads=[r_c], writes=[PT_r[slot]])

        for step in range(NT + 4):
            if step < NT:
                stage_a(step)
            if 0 <= step - 2 < NT:
                stage_b(step - 2)
            if 0 <= step - 3 < NT:
                stage_c(step - 3)
        P.dma("sp", T["yT"][384:768, :].rearrange("(t p) n -> p t n", p=128), YT[:], reads=[r_YT])
        P.barrier()
        P.emit_block()


def phase_hyena_a(P, nc, T, l):
    with contextlib.ExitStack() as st:
        sb = _sb(nc, st)
        ppt = sb("ppt", [128, NPP], F32)
        cv = sb("cv", [128, 4], F32)
        hz = sb("hz", [33, L + 1], F32)
        w1 = sb("w1", [33, 64], F32)
        w2 = sb("w2", [64, 64], F32)
        w3 = sb("w3", [64, 512], F32)
        frb = sb("frb", [64, 2], F32)
        U1 = sb("U1", [64, 4, 512], F32)
        Tm = sb("Tm", [64, 4, 512], F32)
        H1 = sb("H1", [64, 5, 512], F32)
        H2 = sb("H2", [64, L + 1], F32)
        tp = sb("tp", [128, 2, 512], F32)
        dec = sb("dec", [128, 2, 512], F32)
        fo = sb("fo", [128, 2, 512], BF16)
        U = sb("U", [128, 2, L + 2], F32)
        Rv = sb("Rv", [128, 3, L], F32)
        zb = sb("zb", [128, L], BF16)
        x0b = sb("x0b", [128, L], BF16)
        UBh = sb("UBh", [128, 2, L + 2], BF16)
        DGh = sb("DGh", [128, 2, 3, 128], BF16)
        idfh = sb("idfh", [128, 128], F32)
        p1 = st.enter_context(nc.psum_tensor(_un("p1"), [128, 3, 512], F32))
        p3 = st.enter_context(nc.psum_tensor(_un("p3"), [128, 2, 512], F32))
        pcv = st.enter_context(nc.psum_tensor(_un("pcv"), [128, 2, 512], F32))
        r_pp, r_w, r_hz, r_frb = Res("pp"), Res("w"), Res("hz"), Res("frb")
        P.dma("sp", ppt[:], T["pp"][l], writes=[r_pp])
        P.dma("sp", cv[:], T["cvec"], writes=[r_pp])
        P.dma("sp", w1[:], T["hy_w1"][l], writes=[r_w])
        P.dma("sp", w2[:], T["hy_w2"][l], writes=[r_w])
        P.dma("sp", w3[:], T["hy_w3"][l], writes=[r_w])
        P.memset("pool", hz[:, L:L + 1], 0.0, writes=[r_hz])
        P.dma("sp", hz[:, 0:L], T["hyz"], writes=[r_hz])
        fr = ppt[0:64, PP["hfr"]:PP["hfr"] + 1]
        P.tt("dve", frb[:, 0:1], fr, ppt[0:64, PP["hb1"]:PP["hb1"] + 1], ALU.mult, reads=[r_pp], writes=[r_frb])
        P.tt("dve", frb[:, 1:2], fr, ppt[0:64, PP["hb2"]:PP["hb2"] + 1], ALU.mult, reads=[r_pp], writes=[r_frb])
        p1_r, p3_r, U1_r, H1_r, H2_r, tp_r, dec_r, fo_r = (Ring(3, "p1"), Ring(2, "p3"), Ring(4, "U1"), Ring(5, "H1"),
                                                           Ring(8, "H2"), Ring(2, "tp"), Ring(2, "dec"), Ring(2, "fo"))

        def sin_layer(psrc, pres, k, Hdst, hres):
            uj, ur = U1_r.next()
            u, t = U1[:, uj, :], Tm[:, uj, :]
            P.ts("dve", u, psrc, fr, frb[:, k:k + 1], ALU.mult, ALU.add, reads=[pres, r_pp, r_frb], writes=[ur])
            P.ts("dve", t, u, 1.0 / TWO_PI, MAGIC, ALU.mult, ALU.add, reads=[ur], writes=[ur])
            P.ts("dve", t, t, -MAGIC, -TWO_PI, ALU.add, ALU.mult, reads=[ur], writes=[ur])
            P.tt("dve", u, u, t, ALU.add, reads=[ur], writes=[ur])
            P.ts("dve", u, u, 3.14159, -3.14159, ALU.min, ALU.max, reads=[ur], writes=[ur])
            P.act(Hdst, u, AF.Sin, reads=[ur], writes=[hres])

        sH1, sH2 = {}, {}

        r_H2 = [Res("H2_%d" % i) for i in range(9)]
        P.memset("pool", H2[:, L:L + 1], 0.0, writes=[r_H2[8]])

        def fs1(c):
            rhs = hz[:, c * 512:(c + 1) * 512]
            j, r = p1_r.next()
            P.mm(p1[0:64, j, :], w1[:], rhs, reads=[r_w, r_hz], writes=[r])
            h1j, h1r = H1_r.next()
            sin_layer(p1[0:64, j, :], r, 0, H1[:, h1j, :], h1r)
            sH1[c] = (h1j, h1r)

        def fs2(c):
            h1j, h1r = sH1.pop(c)
            j, r = p1_r.next()
            P.mm(p1[0:64, j, :], w2[:], H1[:, h1j, :], reads=[r_w, h1r], writes=[r])
            sin_layer(p1[0:64, j, :], r, 1, H2[:, c * 512:(c + 1) * 512], r_H2[c])

        def fs3(c):
            if c < 8:
                h2v, h2rs = H2[:, c * 512:(c + 1) * 512], [r_H2[c]]
            else:
                s0 = L - (c - 8) * 512
                h2v, h2rs = H2[:, s0:s0 - 512:-1], r_H2
            tj, tr_ = tp_r.next()
            P.dma("sp", tp[:, tj, :], T["tpos"][0:1, c * 512:(c + 1) * 512].broadcast_to([128, 512]), writes=[tr_])
            half = 0 if c < 8 else 1
            for ctile in range(2):
                j, r = p3_r.next()
                P.mm(p3[:, j, :], w3[:, half * 256 + ctile * 128:half * 256 + (ctile + 1) * 128], h2v,
                     reads=[r_w] + h2rs, writes=[r])
                dj, dr = dec_r.next()
                P.act(dec[:, dj, :], tp[:, tj, :], AF.Exp, scale=cv[:, ctile:ctile + 1], reads=[tr_, r_pp], writes=[dr])
                fj, frr = fo_r.next()
                P.tt("dve", fo[:, fj, :], p3[:, j, :], dec[:, dj, :], ALU.mult, reads=[r, dr], writes=[frr])
                P.dma("pool", T["hcT"][ctile * 128:(ctile + 1) * 128, c * 512:(c + 1) * 512], fo[:, fj, :], reads=[frr])

        U_r = Ring(2, "U")
        UB_r, DG_r, pcv_r = Ring(2, "UBh"), Ring(2, "DGh"), Ring(2, "pcv")
        r_idf = Res("idfh")
        P.dma("sp", idfh[:], T["ident"], writes=[r_idf])
        r_R = [Res("R%d" % i) for i in range(3)]
        r_zb, r_x0 = Res("zb"), Res("x0b")

        def conv_tile(ctile, role):
            ti = role * 2 + ctile
            uj, ur = U_r.next()
            P.memset("pool", U[:, uj, 0:1], 0.0, writes=[ur])
            P.memset("pool", U[:, uj, L + 1:L + 2], 0.0, writes=[ur])
            P.dma("sp", U[:, uj, 1:L + 1], T["pC"][role * 256 + ctile * 128:role * 256 + (ctile + 1) * 128, :],
                  writes=[ur])
            wc = lambda k: ppt[:, PP["hcw"] + 6 * k + ti:PP["hcw"] + 6 * k + ti + 1]
            P.act(Rv[:, role, :], U[:, uj, 0:L], AF.Identity, scale=wc(0), bias=ppt[:, PP["hcb"] + ti:PP["hcb"] + ti + 1],
                  reads=[ur, r_pp], writes=[r_R[role]])
            for k in (1, 2):
                P.stt(Rv[:, role, :], U[:, uj, k:k + L], wc(k), Rv[:, role, :], ALU.mult, ALU.add,
                      reads=[ur, r_pp], writes=[r_R[role]])

        def conv_finish(ctile):
            P.tt("dve", zb[:], Rv[:, 2, :], Rv[:, 1, :], ALU.mult, reads=[r_R[1], r_R[2]], writes=[r_zb])
            P.cp("act", x0b[:], Rv[:, 0, :], reads=[r_R[0]], writes=[r_x0])
            rows = slice(ctile * 128, (ctile + 1) * 128)
            P.dma("pool", T["zT"][rows, :], zb[:], reads=[r_zb])
            P.dma("pool", T["zx"][0, rows, :], zb[:], reads=[r_zb])
            P.dma("pool", T["zx"][1, rows, :], x0b[:], reads=[r_x0])

        cq = []
        for ctile in range(2):
            for role in range(3):
                cq.append((lambda ct=ctile, ro=role: conv_tile(ct, ro)))
            cq.append((lambda ct=ctile: conv_finish(ct)))
        for gi in range(2):
            for c in range(4 * gi, 4 * gi + 4):
                fs1(c)
            for c in range(4 * gi, 4 * gi + 4):
                fs2(c)
            for _ in range(2):
                if cq:
                    cq.pop(0)()
        for gi in range(4):
            for c in range(4 * gi, 4 * gi + 4):
                fs3(c)
            for _ in range(2):
                if cq:
                    cq.pop(0)()
        while cq:
            cq.pop(0)()
        P.barrier()
        P.emit_block()


def phase_hyena_b(P, nc, T, l):
    with contextlib.ExitStack() as st:
        sb = _sb(nc, st)
        ppt = sb("ppt", [128, NPP], F32)
        mc = sb("mc", [128, 64, 128], BF16)
        ms = sb("ms", [128, 64, 128], BF16)
        msn = sb("msn", [128, 64, 128], BF16)
        irs = sb("irs", [128, 128, 32], BF16)
        cst = sb("cst", [128, 780], F32)
        e1zb = sb("e1zb", [128, 264], BF16)
        e1hb = sb("e1hb", [128, 132], BF16)
        g1b = sb("g1b", [128, 256], BF16)
        onb = sb("onb", [128, 128], BF16)
        BIG = sb("BIG", [128, 16384], BF16)
        BST = sb("BST", [128, 128, 64], BF16)
        ZHs = sb("ZHs", [128, 32, 128], BF16)
        ZZs = sb("ZZs", [128, 16, 128], BF16)
        Hs = sb("Hs", [128, 3, 2, 2, 64], F32)
        Y1 = sb("Y1", [128, 64, 2, 64], BF16)
        Y2 = sb("Y2", [128, 64, 2, 64], BF16)
        zg = sb("zg", [128, L], BF16)
        x0g = sb("x0g", [128, L], BF16)
        yc = sb("yc", [128, 2, L], BF16)
        tq = sb("tq", [128, 2, 2, 256], F32)
        gt = sb("gt", [128, 2, 512], F32)
        sqb = sb("sqb", [128, 2, 512], BF16)
        rst = sb("rst", [128, 512], F32)
        ob = sb("ob", [128, 2, 512], BF16)
        pb = st.enter_context(nc.psum_tensor(_un("pb"), [128, 8, 512], F32))
        r_pp, r_m, r_c = Res("pp"), Res("m"), Res("c")
        P.dma("sp", ppt[:], T["pp"][l], writes=[r_pp])
        P.dma("sp", mc[:].rearrange("p a b -> p (a b)"), T["mcb"], writes=[r_m])
        P.dma("sp", ms[:].rearrange("p a b -> p (a b)"), T["msb"], writes=[r_m])
        P.dma("sp", msn[:].rearrange("p a b -> p (a b)"), T["msnb"], writes=[r_m])
        P.dma("sp", irs[0:64, :, :].rearrange("p a b -> p (a b)"), T["irb"], writes=[r_m])
        P.dma("sp", irs[64:128, :, :].rearrange("p a b -> p (a b)"), T["iib"], writes=[r_m])
        P.dma("sp", cst[:, 0:264], T["e1z"], writes=[r_c])
        P.dma("sp", cst[:, 264:396], T["e1h"], writes=[r_c])
        P.dma("sp", cst[:, 396:652], T["g1"], writes=[r_c])
        P.dma("sp", cst[:, 652:780], T["ones"], writes=[r_c])
        P.cp("dve", e1zb[:], cst[:, 0:264], reads=[r_c], writes=[r_c])
        P.cp("dve", e1hb[:], cst[:, 264:396], reads=[r_c], writes=[r_c])
        P.cp("dve", g1b[:], cst[:, 396:652], reads=[r_c], writes=[r_c])
        P.cp("dve", onb[:], cst[:, 652:780], reads=[r_c], writes=[r_c])
        A = BIG[:, :].rearrange("p (r k s c) -> p r k s c", r=2, k=64, s=2)
        Bst = BST[:, :, :]
        r_bst = Res("bst")
        r_zgh, r_x0h = [Res("zg0"), Res("zg1")], [Res("x0g0"), Res("x0g1")]
        deferred = []
        stepc = [0]

        def bg_step():
            stepc[0] += 1
            if deferred and stepc[0] % 5 == 0:
                deferred.pop(0)()
        r_Y2 = Res("Y2")
        r_zh, r_zz, r_big, r_Y, r_zg, r_x0g = Res("zh"), Res("zz"), Res("big"), Res("Y"), Res("zg"), Res("x0g")
        r_lo = r_hi = r_big
        r_yc = [Res("yc0"), Res("yc1")]
        pb_r, tq_r, gt_r, hs_r = Ring(8, "pb"), Ring(2, "tq"), Ring(2, "gt"), Ring(3, "hs")
        P.memset("pool", Y1[:], 0.0, writes=[r_Y])
        P.memset("pool", Y2[:], 0.0, writes=[r_Y2])
        ecnt = [0]

        def f1_evac(j, r, which, c4):
            P.cp("act", A[:, :, 0:33, which, c4 * 4:c4 * 4 + 4],
                 pb[:, j, 0:264].rearrange("p (c r k) -> p r k c", c=4, r=2), reads=[r], writes=[r_big])

        for g in range(4):
            ctile, hp = g // 2, 64 * (g % 2)
            c0 = 64 * g
            for cl in range(2):
                P.dma("sp", ZHs[cl * 64:(cl + 1) * 64, :, :],
                      T["hcT"][c0 + cl:c0 + 64:2, :].rearrange("c (a b) -> a c b", b=128), writes=[r_zh])
            for cl in range(4):
                P.dma("sp", ZZs[cl * 32:(cl + 1) * 32, :, :],
                      T["zT"][c0 + cl:c0 + 64:4, :].rearrange("c (a b) -> a c b", b=128), writes=[r_zz])
            for c4 in range(16):
                j, r = pb_r.next()
                for mm_ in range(2):
                    P.mm(pb[:, j, mm_ * 132:(mm_ + 1) * 132], ZHs[:, c4 * 2 + mm_, :], e1hb[:], reads=[r_zh, r_c],
                         writes=[r])
                f1_evac(j, r, 0, c4)
                bg_step()
            for c4 in range(16):
                j, r = pb_r.next()
                P.mm(pb[:, j, 0:264], ZZs[:, c4, :], e1zb[:], reads=[r_zz, r_c], writes=[r])
                f1_evac(j, r, 1, c4)
                bg_step()
            for kq in range(17):
                j, r = pb_r.next()
                bv = pb[:, j, :].rearrange("p (k r s c) -> p k r s c", k=2, r=2, s=2)
                nk = 2 if kq < 16 else 1
                for kk in range(nk):
                    ka = kq * 2 + kk
                    a_re = A[:, 0, ka, :, :].rearrange("p s c -> p (s c)")
                    a_im = A[:, 1, ka, :, :].rearrange("p s c -> p (s c)")
                    o_re = bv[:, kk, 0, :, :].rearrange("p s c -> p (s c)")
                    o_im = bv[:, kk, 1, :, :].rearrange("p s c -> p (s c)")
                    P.mm(o_re, mc[:, ka, :], a_re, start=True, stop=False, reads=[r_m, r_big], writes=[r])
                    P.mm(o_re, ms[:, ka, :], a_im, start=False, stop=True, reads=[r_m, r_big], writes=[r])
                    P.mm(o_im, mc[:, ka, :], a_im, start=True, stop=False, reads=[r_m, r_big], writes=[r])
                    P.mm(o_im, msn[:, ka, :], a_re, start=False, stop=True, reads=[r_m, r_big], writes=[r])
                hj, hr = hs_r.next()
                P.act(Hs[:, hj, 0:nk, :, :], bv[:, 0:nk, :, 0, :], AF.Copy, scale=1.0 / 8192.0, reads=[r], writes=[hr])
                tj, tr_ = tq_r.next()
                ks = slice(kq * 2, kq * 2 + nk)
                ta = tq[:, tj, 0, :].rearrange("p (k r c) -> p k r c", k=2, r=2)[:, 0:nk]
                tb = tq[:, tj, 1, :].rearrange("p (k r c) -> p k r c", k=2, r=2)[:, 0:nk]
                xz = bv[:, 0:nk, :, 1, :]
                P.tt("dve", ta, xz, Hs[:, hj, 0:nk, 0:1, :].broadcast_to([128, nk, 2, 64]), ALU.mult, reads=[r, hr],
                     writes=[tr_])
                P.tt("dve", tb, xz, Hs[:, hj, 0:nk, 1:2, :].broadcast_to([128, nk, 2, 64]), ALU.mult, reads=[r, hr],
                     writes=[tr_])
                P.tt("pool", Y1[:, :, 0, ks].rearrange("p c k -> p k c"), ta[:, :, 0, :], tb[:, :, 1, :], ALU.subtract,
                     reads=[tr_], writes=[r_Y])
                P.tt("pool", Y1[:, :, 1, ks].rearrange("p c k -> p k c"), tb[:, :, 0, :], ta[:, :, 1, :], ALU.add,
                     reads=[tr_], writes=[r_Y])
                P.tt("pool", Y2[:, :, 1, ks].rearrange("p c k -> p k c"), ta[:, :, 0, :], tb[:, :, 1, :], ALU.subtract,
                     reads=[tr_], writes=[r_Y2])
                P.stt(Y2[:, :, 0, ks].rearrange("p c k -> p k c"), tb[:, :, 0, :], -1.0, ta[:, :, 1, :], ALU.mult,
                      ALU.subtract, reads=[tr_], writes=[r_Y2])
                bg_step()
            while deferred:
                deferred.pop(0)()
            r_zg, r_x0g = r_zgh[g % 2], r_x0h[g % 2]
            P.dma("sp", zg[hp:hp + 64, :], T["zx"][0, c0:c0 + 64, :], writes=[r_zg])
            P.dma("sp", x0g[hp:hp + 64, :], T["zx"][1, c0:c0 + 64, :], writes=[r_x0g])
            for c4 in range(16):
                j, r = pb_r.next()
                for cc in range(4):
                    c = c4 * 4 + cc
                    o = pb[:, j, cc * 128:(cc + 1) * 128]
                    P.mm(o, Y1[:, c, :, :].rearrange("p r k -> p (r k)"), g1b[:, 0:128], start=True, stop=False,
                         reads=[r_Y, r_c], writes=[r])
                    P.mm(o, Y2[:, c, :, :].rearrange("p r k -> p (r k)"), g1b[:, 128:256], start=False, stop=True,
                         reads=[r_Y2, r_c], writes=[r])
                eng = "act" if ecnt[0] % 4 else "dve"
                ecnt[0] += 1
                P.cp(eng, Bst[:, :, c4 * 4:c4 * 4 + 4], pb[:, j, :].rearrange("p (c n) -> p n c", c=4),
                     reads=[r], writes=[r_bst])
            zv = zg[hp:hp + 64, :].rearrange("p (a b) -> p b a", b=128)
            xv = x0g[hp:hp + 64, :].rearrange("p (a b) -> p b a", b=128)
            yv = yc[hp:hp + 64, ctile, :].rearrange("p (a b) -> p b a", b=128)
            bias = ppt[hp:hp + 64, PP["hbias"] + ctile:PP["hbias"] + ctile + 1]
            def i2_bank(nq, hp=hp, ctile=ctile, zv=zv, xv=xv, yv=yv, bias=bias, r_zg=r_zg, r_x0g=r_x0g):
                j, r = pb_r.next()
                for nn in range(16):
                    nb = nq * 16 + nn
                    o = pb[hp:hp + 64, j, nn * 32:(nn + 1) * 32]
                    P.mm(o, Bst[:, nb, :], irs[:, nb, :], start=True, stop=True, reads=[r_bst, r_m], writes=[r])
                gj, gr = gt_r.next()
                gv = gt[hp:hp + 64, gj, :].rearrange("p (b a) -> p b a", a=32)
                P.stt(gv, zv[:, nq * 16:(nq + 1) * 16, :], bias, pb[hp:hp + 64, j, :].rearrange("p (b a) -> p b a", a=32),
                      ALU.mult, ALU.add, reads=[r_zg, r_pp, r], writes=[gr])
                P.tt("dve", yv[:, nq * 16:(nq + 1) * 16, :], gv, xv[:, nq * 16:(nq + 1) * 16, :], ALU.mult,
                     reads=[gr, r_x0g], writes=[r_yc[ctile]])

            for nq in range(8):
                deferred.append(lambda nq=nq, f=i2_bank: f(nq))
        while deferred:
            deferred.pop(0)()
        r_sq, r_rst = Res("sq"), Res("rst")
        ob_r = Ring(2, "ob")
        for ch in range(8):
            sl = slice(ch * 512, (ch + 1) * 512)
            for ctile in range(2):
                P.act(sqb[:, ctile, :], yc[:, ctile, sl], AF.Square, reads=[r_yc[ctile]], writes=[r_sq])
            j, r = pb_r.next()
            for ctile in range(2):
                P.mm(pb[:, j, :], onb[:], sqb[:, ctile, :], start=(ctile == 0), stop=(ctile == 1), reads=[r_c, r_sq],
                     writes=[r])
            P.act(rst[:], pb[:, j, :], AF.Sqrt, bias=EPS, scale=1.0 / D_C, reads=[r], writes=[r_rst])
            P.recip(rst[:], rst[:], reads=[r_rst], writes=[r_rst])
            for ctile in range(2):
                oj, orr = ob_r.next()
                P.stt(ob[:, oj, :], yc[:, ctile, sl], ppt[:, PP["gnc"] + ctile:PP["gnc"] + ctile + 1], rst[:], ALU.mult,
                      ALU.mult, reads=[r_yc[ctile], r_rst, r_pp], writes=[orr])
                P.dma("pool", T["yT"][768 + ctile * 128:768 + (ctile + 1) * 128, sl], ob[:, oj, :], reads=[orr])
        P.barrier()
        P.emit_block()


SCRATCH = {"pA": ([768, L], F32), "pQK": ([512, L], F32), "pV": ([L, 128], F32), "pC": ([768, L], F32),
           "yT": ([D, L], BF16), "x1": ([L, D], F32), "h2T": ([D, L], BF16), "xs0": ([L, D], F32),
           "zT": ([D_C, L], BF16), "hcT": ([D_C, 2 * L], BF16), "zx": ([2, D_C, L], BF16),
           "mcb": ([128, 8192], BF16), "msb": ([128, 8192], BF16), "msnb": ([128, 8192], BF16),
           "irb": ([64, 4096], BF16), "iib": ([64, 4096], BF16)}
INPUTS = {"x": [L, D], "w_in": [DEPTH, D, D_IN], "w_out": [DEPTH, D, D], "w_up": [DEPTH, D, D_FF],
          "w_down": [DEPTH, D_FF, D], "pp": [DEPTH, 128, NPP], "gates": [DEPTH, 4, 3, 128, 128],
          "hy_w1": [DEPTH, 33, 64], "hy_w2": [DEPTH, 64, 64], "hy_w3": [DEPTH, 64, 512], "fng": [128, D],
          "cvec": [128, 4]}
CONST_SHAPES = {"ident": (128, 128), "ones": (128, 128), "rope": (16, L), "prot": (128, 128),
                "maskn": (128, 128), "maskp": (128, 128), "hyz": (33, L), "tpos": (1, 2 * L),
                "e1": (64, 128), "g1": (128, 256), "g2": (128, 256), "e1z": (128, 264), "e1h": (128, 132)}


def make_T(nc, need=None, ext_in=(), ext_out=(), wdepth=DEPTH):
    T = {}
    for n, s in list(INPUTS.items()) + list(CONST_SHAPES.items()):
        if need is None or n in need:
            s = list(s)
            if n in ("w_in", "w_out", "w_up", "w_down"):
                s[0] = wdepth
            T[n] = nc.dram_tensor(n, s, F32, kind="ExternalInput").ap()
    for n, (s, d) in SCRATCH.items():
        kind = "Internal"
        if n in ext_in:
            kind = "ExternalInput"
        if n in ext_out:
            kind = "ExternalOutput"
        T[n] = nc.dram_tensor(n, list(s), d, kind=kind).ap()
    T["out"] = nc.dram_tensor("out", [L, D], F32, kind="ExternalOutput").ap()
    return T


def build_program():
    nc = bass.Bass("TRN2", target_bir_lowering=False)
    T = make_T(nc)
    P = Prog(nc)
    with contextlib.ExitStack() as st:
        P.alloc_sems(st)
        for l in range(DEPTH):
            xsrc = T["x"] if l == 0 else T["xs0"]
            phase_inproj(P, nc, T, l, xsrc, with_prep=(l == 0))
            phase_lru(P, nc, T, l)
            phase_attn(P, nc, T, l)
            phase_hyena_a(P, nc, T, l)
            phase_hyena_b(P, nc, T, l)
            last = (l == DEPTH - 1)
            with contextlib.ExitStack() as wst:
                wu_p = wst.enter_context(nc.sbuf_tensor(_un("wuP"), [128, 8, D_FF], BF16))
                pre = {"wu": wu_p, "r_wuh": [[Res("wu") for _ in range(8)] for _ in range(4)]}
                phase_outproj(P, nc, T, l, xsrc, pre=pre)
                phase_mlp(P, nc, T, l, T["out"] if last else T["xs0"], last, pre=pre)
    return nc


def _perm_w_in(w):
    w = w.copy()
    w[:, :, 768:1152] = _perm_heads(w[:, :, 768:1152], 2)
    return np.ascontiguousarray(w)


def _perm_w_out(w):
    w = w.copy()
    w[:, 384:768, :] = _perm_heads(w[:, 384:768, :], 1)
    return np.ascontiguousarray(w)


def kernel(**inputs):
    I = {k: np.asarray(v) for k, v in inputs.items()}
    f32 = lambda a: np.ascontiguousarray(np.asarray(a, dtype=np.float32))
    shared = {
        "w_in": _perm_w_in(f32(I["w_in"])), "w_out": _perm_w_out(f32(I["w_out"])),
        "w_up": f32(I["w_up"]), "w_down": f32(I["w_down"]),
        "pp": np.stack([pack_params(I, l) for l in range(DEPTH)]),
        "gates": np.stack([pack_gates(I, l) for l in range(DEPTH)]),
        "hy_w1": f32(I["hy_w1"]), "hy_w2": f32(I["hy_w2"]), "hy_w3": f32(I["hy_w3"]),
        "fng": np.ascontiguousarray(np.broadcast_to(f32(I["final_norm_g"])[None, :], (128, D))),
    }
    shared.update(host_small_consts())
    x = f32(I["x"])
    nb = x.shape[0]
    n_cores = 8
    in_maps = []
    for c in range(n_cores):
        m = dict(shared)
        m["x"] = np.ascontiguousarray(x[c % nb])
        in_maps.append(m)
    nc = build_program()
    res = run_bass_kernel_spmd(nc, in_maps, core_ids=list(range(n_cores)))
    out = np.stack([np.asarray(res.results[b]["out"], dtype=np.float32) for b in range(nb)], 0)
    return out
```

```python
import contextlib
import math
import numpy as np
import concourse.bass as bass
import concourse.mybir as mybir
from concourse.bass_utils import run_bass_kernel_spmd

F32 = mybir.dt.float32
BF16 = mybir.dt.bfloat16
AF = mybir.ActivationFunctionType
ALU = mybir.AluOpType

D = 1024
L = 4096
DEPTH = 2
D_A = 384
D_B = 384
D_C = 256
D_IN = 2176
D_FF = 4096
EPS = 1e-6
NT = L // 128
ROPE_THETA = 500000.0
HY_BANDS = 16
HY_MAX_DECAY = math.log(1e-2) / 0.3
HY_MIN_DECAY = math.log(1e-2) / 1.5
MAGIC = 12582912.0
TWO_PI = 2.0 * math.pi

PP = {}
_c = 0
for _n, _w in [("g1", 8), ("g2", 8), ("caw", 12), ("cab", 3), ("ba", 6), ("bx", 6), ("lam", 6), ("gna", 3),
               ("gnb", 3), ("hcw", 18), ("hcb", 6), ("hbias", 2), ("gnc", 2), ("hb1", 1), ("hfr", 1), ("hb2", 1),
               ("sink", 6)]:
    PP[_n] = _c
    _c += _w
NPP = _c

ENGS = ("pe", "act", "dve", "pool", "sp")
NDSEM = 8


class Res:
    __slots__ = ("name", "writers", "readers")

    def __init__(self, name=""):
        self.name = name
        self.writers = {}
        self.readers = {}


class Prog:
    def __init__(self, nc):
        self.nc = nc
        self.ops = {e: [] for e in ENGS}
        self.cnt = {e: 0 for e in ENGS}
        self.waited = {e: {} for e in ENGS}
        self.dma_n = {e: 0 for e in ENGS}
        self.dsem_uses = {}
        self.sems = {}
        self.keys = list(ENGS) + [("d", q, i) for q in ("sp", "pool", "act") for i in range(NDSEM)]

    def alloc_sems(self, st):
        for k in self.keys:
            nm = k if isinstance(k, str) else "d_%s_%d" % (k[1], k[2])
            self.sems[k] = st.enter_context(self.nc.semaphore("s_" + nm))

    def _deps(self, eng, reads, writes):
        need = {}
        for r in reads:
            for k, v in r.writers.items():
                if need.get(k, 0) < v:
                    need[k] = v
        for w in writes:
            for k, v in w.writers.items():
                if need.get(k, 0) < v:
                    need[k] = v
            for k, v in w.readers.items():
                if need.get(k, 0) < v:
                    need[k] = v
        waits = []
        wd = self.waited[eng]
        for k, v in need.items():
            if eng == "pe" and k == "pe":
                continue
            if wd.get(k, 0) >= v:
                continue
            wd[k] = v
            waits.append((k, v))
        return waits

    def _mark(self, tok, reads, writes):
        k, v = tok
        for w in writes:
            w.writers = {k: v}
            w.readers = {}
        for r in reads:
            if r.readers.get(k, 0) < v:
                r.readers[k] = v

    def op(self, eng, fn, reads=(), writes=()):
        waits = self._deps(eng, reads, writes)
        self.cnt[eng] += 1
        tok = (eng, self.cnt[eng])
        self.ops[eng].append((waits, fn, tok, 1))
        self._mark(tok, reads, writes)
        return tok

    def dma(self, q, out, in_, reads=(), writes=()):
        n = self.dma_n[q]
        self.dma_n[q] += 1
        key = ("d", q, n % NDSEM)
        uses = self.dsem_uses.get(key, 0)
        waits = self._deps(q, reads, writes)
        if uses > 0 and self.waited[q].get(key, 0) < 16 * uses:
            self.waited[q][key] = 16 * uses
            waits.append((key, 16 * uses))
        self.dsem_uses[key] = uses + 1
        tok = (key, 16 * (uses + 1))
        self.ops[q].append((waits, (lambda e: e.dma_start(out=out, in_=in_)), tok, 16))
        self._mark(tok, reads, writes)
        return tok

    def barrier(self):
        cur = {e: self.cnt[e] for e in ENGS}
        for k, u in self.dsem_uses.items():
            cur[k] = 16 * u
        for e in ENGS:
            waits = []
            for k, v in cur.items():
                if k == e or v == 0:
                    continue
                if self.waited[e].get(k, 0) >= v:
                    continue
                self.waited[e][k] = v
                waits.append((k, v))
            self.ops[e].append((waits, None, None, 0))

    def emit_block(self):
        nc = self.nc
        sems = self.sems
        ops = self.ops

        def run(e, ename):
            for waits, fn, tok, inc in ops[ename]:
                for k, v in waits:
                    e.wait_ge(sems[k], v)
                if fn is not None:
                    fn(e).then_inc(sems[tok[0]], inc)

        with nc.Block() as block:
            @block.tensor
            def _(e):
                run(e, "pe")

            @block.scalar
            def _(e):
                run(e, "act")

            @block.vector
            def _(e):
                run(e, "dve")

            @block.gpsimd
            def _(e):
                run(e, "pool")

            @block.sync
            def _(e):
                run(e, "sp")
        self.ops = {e: [] for e in ENGS}

    def mm(self, out, lhsT, rhs, start=True, stop=True, reads=(), writes=()):
        return self.op("pe", lambda e: e.matmul(out, lhsT, rhs, start=start, stop=stop), reads, writes)

    def tr(self, out, in_, ident, reads=(), writes=()):
        return self.op("pe", lambda e: e.transpose(out, in_, ident), reads, writes)

    def act(self, out, in_, func, bias=None, scale=None, accum_out=None, reads=(), writes=()):
        kw = {}
        if bias is not None:
            kw["bias"] = bias
        if scale is not None:
            kw["scale"] = scale
        if accum_out is not None:
            kw["accum_out"] = accum_out
        return self.op("act", lambda e: e.activation(out=out, in_=in_, func=func, **kw), reads, writes)

    def ts(self, eng, out, in0, s1, s2, op0, op1=None, reads=(), writes=()):
        if op1 is None:
            return self.op(eng, lambda e: e.tensor_scalar(out=out, in0=in0, scalar1=s1, scalar2=None, op0=op0),
                           reads, writes)
        return self.op(eng, lambda e: e.tensor_scalar(out=out, in0=in0, scalar1=s1, scalar2=s2, op0=op0, op1=op1),
                       reads, writes)

    def tt(self, eng, out, in0, in1, op, reads=(), writes=()):
        return self.op(eng, lambda e: e.tensor_tensor(out=out, in0=in0, in1=in1, op=op), reads, writes)

    def stt(self, out, in0, scalar, in1, op0, op1, reads=(), writes=(), accum_out=None):
        if accum_out is not None:
            return self.op("dve", lambda e: e.scalar_tensor_tensor(out=out, in0=in0, scalar=scalar, in1=in1,
                                                                    op0=op0, op1=op1, accum_out=accum_out),
                           reads, writes)
        return self.op("dve", lambda e: e.scalar_tensor_tensor(out=out, in0=in0, scalar=scalar, in1=in1,
                                                                op0=op0, op1=op1), reads, writes)

    def cp(self, eng, out, in_, reads=(), writes=()):
        if eng == "act":
            return self.act(out, in_, AF.Copy, reads=reads, writes=writes)
        return self.op(eng, lambda e: e.tensor_copy(out=out, in_=in_), reads, writes)

    def memset(self, eng, ap, val, writes=()):
        return self.op(eng, lambda e: e.memset(ap, val), (), writes)

    def recip(self, out, in_, reads=(), writes=()):
        return self.op("dve", lambda e: e.reciprocal(out=out, in_=in_), reads, writes)

    def scan(self, out, d0, d1, reads=(), writes=()):
        return self.op("dve", lambda e: e.tensor_tensor_scan(out=out, data0=d0, data1=d1, initial=0.0,
                                                              op0=ALU.mult, op1=ALU.add), reads, writes)


_UID = [0]


def _un(n):
    _UID[0] += 1
    return "%s_%d" % (n, _UID[0])


def _sb(nc, st):
    return lambda n, s, d: st.enter_context(nc.sbuf_tensor(_un(n), s, d))


class Ring:
    def __init__(self, n, name):
        self.n = n
        self.i = 0
        self.res = [Res("%s%d" % (name, j)) for j in range(n)]

    def next(self):
        j = self.i % self.n
        self.i += 1
        return j, self.res[j]


def host_consts():
    C = {}
    C["ident"] = np.eye(128, dtype=np.float32)
    C["ones"] = np.ones((128, 128), np.float32)
    pos = np.arange(L, dtype=np.float32)
    inv = (np.float32(ROPE_THETA) ** (-np.arange(0, 16, 2, dtype=np.float32) / np.float32(16))).astype(np.float32)
    ang = (pos[:, None] * inv[None, :]).astype(np.float32).astype(np.float64)
    cos, sin = np.cos(ang).T, np.sin(ang).T
    ct = np.ones((128, L), np.float64)
    stb = np.zeros((128, L), np.float64)
    prot = np.zeros((128, 128), np.float32)
    for r in range(128):
        d = r % 64
        if d < 8:
            ct[r] = cos[d]
            stb[r] = -sin[d]
            prot[r + 8, r] = 1.0
        elif d < 16:
            ct[r] = cos[d - 8]
            stb[r] = sin[d - 8]
            prot[r - 8, r] = 1.0
    C["ropec"] = ct.astype(np.float32)
    C["ropes"] = stb.astype(np.float32)
    C["prot"] = prot
    j = np.arange(128)[:, None]
    q = np.arange(128)[None, :]
    C["maskn"] = (j <= q).astype(np.float32)
    C["maskp"] = (j >= q).astype(np.float32)
    t = np.linspace(0.0, 1.0, L)
    w = TWO_PI * np.arange(L) / L
    f = np.linspace(1e-4, HY_BANDS - 1, HY_BANDS)
    z = np.concatenate([t[None, :], np.cos(f[:, None] * w[None, :]), -np.sin(f[:, None] * w[None, :])], 0)
    deltas = np.abs(np.linspace(HY_MIN_DECAY, HY_MAX_DECAY, D_C))
    decay = np.exp(-t[None, :] * deltas[:, None])
    idx = (L - np.arange(L)) % L
    z2 = np.concatenate([z, z[:, idx]], 1)
    dec2 = np.concatenate([decay, decay[:, idx]], 1)
    dec2[:, L] = 0.0
    C["hyz"] = z2.astype(np.float32)
    C["hydec"] = dec2.astype(np.float32)
    N = 2 * L
    na = np.arange(64)[:, None]
    ka = np.arange(64)[None, :]
    a1 = TWO_PI * na * ka / 64.0
    C["e1"] = np.concatenate([np.cos(a1), -np.sin(a1)], 1).astype(np.float32)
    nb = np.arange(128)[:, None, None]
    kk = (np.arange(64)[None, :, None] + 64 * np.arange(128)[None, None, :])
    a2 = TWO_PI * ((nb * kk) % N) / N
    C["mc"] = np.cos(a2).reshape(128, 64 * 128).astype(np.float32)
    C["ms"] = np.sin(a2).reshape(128, 64 * 128).astype(np.float32)
    kb = np.arange(128)[:, None]
    nb2 = np.arange(128)[None, :]
    a3 = TWO_PI * ((kb * nb2) % 128) / 128.0
    C["g1"] = np.concatenate([np.cos(a3), np.sin(a3)], 1).astype(np.float32)
    C["g2"] = np.concatenate([-np.sin(a3), np.cos(a3)], 1).astype(np.float32)
    ka3 = np.arange(64)[:, None, None]
    nn = 128 * np.arange(32)[None, None, :] + np.arange(128)[None, :, None]
    a4 = TWO_PI * ((ka3 * nn) % N) / N
    C["ir"] = (np.cos(a4) / N).reshape(64, 128 * 32).astype(np.float32)
    C["ii"] = (-np.sin(a4) / N).reshape(64, 128 * 32).astype(np.float32)
    return C


def host_small_consts():
    C = host_consts()
    out = {k: C[k] for k in ("ident", "ones", "prot", "maskn", "maskp", "e1", "g1", "g2")}
    e1 = C["e1"]
    e1r = e1.reshape(64, 2, 64)[:, :, 0:33]
    e1z = np.zeros((128, 4, 2, 33), np.float32)
    for cl in range(4):
        e1z[cl * 32:(cl + 1) * 32, cl] = e1r[0:32]
    e1h = np.zeros((128, 2, 2, 33), np.float32)
    for cl in range(2):
        e1h[cl * 64:(cl + 1) * 64, cl] = e1r
    out["e1z"] = e1z.reshape(128, 264)
    out["e1h"] = e1h.reshape(128, 132)
    pos = np.arange(L, dtype=np.float32)
    inv = (np.float32(ROPE_THETA) ** (-np.arange(0, 16, 2, dtype=np.float32) / np.float32(16))).astype(np.float32)
    ang = (pos[:, None] * inv[None, :]).astype(np.float32).astype(np.float64)
    out["rope"] = np.concatenate([np.cos(ang).T, np.sin(ang).T], 0).astype(np.float32)
    out["hyz"] = C["hyz"][:, :L].copy()
    t = np.linspace(0.0, 1.0, L)
    idx = (L - np.arange(L)) % L
    tp = np.concatenate([t, t[idx]])
    tp[L] = 1.0e4
    out["tpos"] = tp[None, :].astype(np.float32)
    deltas = np.abs(np.linspace(HY_MIN_DECAY, HY_MAX_DECAY, D_C))
    cv = np.zeros((128, 4), np.float32)
    cv[:, 0] = -deltas[:128]
    cv[:, 1] = -deltas[128:]
    cv[0:33, 2] = 2.0
    cv[0, 2] = 1.0
    cv[32, 2] = 1.0
    out["cvec"] = cv
    return out


HEAD_PERM = [0, 3, 1, 4, 2, 5]


def _perm_heads(v, axis):
    v = np.asarray(v)
    idx = np.concatenate([np.arange(64) + 64 * h for h in HEAD_PERM])
    return np.take(v, idx, axis=axis)


def pack_params(I, l):
    pp = np.zeros((128, NPP), np.float32)

    def cols(name, vec, n):
        pp[:, PP[name]:PP[name] + n] = np.asarray(vec, np.float32).reshape(n, 128).T

    cols("g1", I["norm_mix_g"][l], 8)
    cols("g2", I["norm_mlp_g"][l], 8)
    for k in range(4):
        pp[:, PP["caw"] + 3 * k:PP["caw"] + 3 * k + 3] = np.asarray(I["conv_a_w"][l][k]).reshape(3, 128).T
    cols("cab", I["conv_a_b"][l], 3)
    for d in range(2):
        pp[:, PP["ba"] + 3 * d:PP["ba"] + 3 * d + 3] = np.asarray(I["lru_ba"][l][d]).reshape(3, 128).T
        pp[:, PP["bx"] + 3 * d:PP["bx"] + 3 * d + 3] = np.asarray(I["lru_bx"][l][d]).reshape(3, 128).T
        pp[:, PP["lam"] + 3 * d:PP["lam"] + 3 * d + 3] = np.asarray(I["lru_lambda"][l][d]).reshape(3, 128).T
    cols("gna", I["gnorm_a"][l], 3)
    cols("gnb", _perm_heads(I["gnorm_b"][l], 0), 3)
    for k in range(3):
        pp[:, PP["hcw"] + 6 * k:PP["hcw"] + 6 * k + 6] = np.asarray(I["hy_conv_w"][l][k]).reshape(6, 128).T
    cols("hcb", I["hy_conv_b"][l], 6)
    cols("hbias", I["hy_bias"][l], 2)
    cols("gnc", I["gnorm_c"][l], 2)
    pp[:64, PP["hb1"]] = I["hy_b1"][l]
    pp[:64, PP["hfr"]] = I["hy_freq"][l]
    pp[:64, PP["hb2"]] = I["hy_b2"][l]
    pp[:, PP["sink"]:PP["sink"] + 6] = np.asarray(I["attn_sink"][l])[HEAD_PERM][None, :]
    return pp


def pack_gates(I, l):
    g = np.zeros((4, 3, 128, 128), np.float32)
    for d in range(2):
        for wi, nm in enumerate(("lru_wa", "lru_wx")):
            W = np.asarray(I[nm][l][d])
            for blk in range(6):
                t, o = blk // 2, 64 * (blk % 2)
                g[2 * d + wi, t, o:o + 64, o:o + 64] = W[blk]
    return g


def phase_inproj(P, nc, T, l, xsrc, with_prep=False):
    with contextlib.ExitStack() as st:
        sb = _sb(nc, st)
        wi = sb("wi", [128, 8, D_IN], BF16)
        stg = sb("stg", [128, 2, D_IN], F32)
        ppt = sb("ppt", [128, NPP], F32)
        idf = sb("idf", [128, 128], F32)
        idb = sb("idb", [128, 128], BF16)
        xt = sb("xt", [128, 3, D], F32)
        sq2 = sb("sq", [128, 2, D], BF16)
        xn3 = sb("xn", [128, 3, D], BF16)
        hT = sb("hT", [128, 2, 8, 512], BF16)
        sm4 = sb("sm", [128, 4, 8], F32)
        ost = sb("ost", [128, 4, 512], F32)
        vst = sb("vst", [128, 2, 4, 128], F32)
        ptr = st.enter_context(nc.psum_tensor(_un("ptr"), [128, 2, 1024], BF16))
        pmm = st.enter_context(nc.psum_tensor(_un("pmm"), [128, 5, 512], F32))
        r_pp, r_id, r_wi = Res("pp"), Res("id"), Res("wi")
        P.dma("sp", ppt[:], T["pp"][l], writes=[r_pp])
        P.dma("sp", idf[:], T["ident"], writes=[r_id])
        P.cp("dve", idb[:], idf[:], reads=[r_id], writes=[r_id])
        stg_r = Ring(2, "stg")
        r_wik = [Res("wi%d" % i) for i in range(8)]
        wv = T["w_in"][l].rearrange("(kt p) n -> p kt n", p=128)
        for kt in range(8):
            j, r = stg_r.next()
            P.dma("sp", stg[:, j, :], wv[:, kt, :], writes=[r])
            if kt % 2:
                P.act(wi[:, kt, :], stg[:, j, :], AF.Copy, scale=ppt[:, PP["g1"] + kt:PP["g1"] + kt + 1],
                      reads=[r, r_pp], writes=[r_wik[kt]])
            else:
                P.ts("dve", wi[:, kt, :], stg[:, j, :], ppt[:, PP["g1"] + kt:PP["g1"] + kt + 1],
                     None, ALU.mult, reads=[r, r_pp], writes=[r_wik[kt]])
        xt_r, xn_r, hT_r, ptr_r, pmm_r = Ring(3, "xt"), Ring(3, "xn"), Ring(2, "hT"), Ring(2, "ptr"), Ring(5, "pmm")
        ost_r, vst_r = Ring(4, "ost"), Ring(2, "vst")
        sm_r = [Res("sm%d" % i) for i in range(4)]
        sq_r = Ring(2, "sq")
        xv = xsrc.rearrange("(n p) d -> n p d", p=128)
        st8 = {}
        hslot = {}
        hTres = [[Res("hT%d_%d" % (a_, b_)) for b_ in range(4)] for a_ in range(2)]
        cnt = [0]

        def s1(n):
            xj, xr = xt_r.next()
            P.dma("sp", xt[:, xj, :], xv[n], writes=[xr])
            k = n % 4
            smv, smr = sm4[:, k, :], sm_r[k]
            qj, qr = sq_r.next()
            P.act(sq2[:, qj, :], xt[:, xj, :], AF.Square, accum_out=smv[:, 0:1], reads=[xr], writes=[qr, smr])
            P.ts("dve", smv[:, 1:2], smv[:, 0:1], 1.0 / D, EPS, ALU.mult, ALU.add, reads=[smr], writes=[smr])
            P.act(smv[:, 2:3], smv[:, 1:2], AF.Sqrt, reads=[smr], writes=[smr])
            P.recip(smv[:, 3:4], smv[:, 2:3], reads=[smr], writes=[smr])
            nj, nr = xn_r.next()
            P.act(xn3[:, nj, :], xt[:, xj, :], AF.Copy, scale=smv[:, 3:4], reads=[xr, smr], writes=[nr])
            st8[n] = (nj, nr)

        def s2(n):
            nj, nr = st8.pop(n)
            ch, tt = n // 4, n % 4
            if tt == 0:
                hslot[ch] = hT_r.next()
            hj, hr = hslot[ch]
            hr = hTres[hj][tt]
            pj, pr = ptr_r.next()
            for kt in range(8):
                P.tr(ptr[:, pj, kt * 128:(kt + 1) * 128], xn3[:, nj, kt * 128:(kt + 1) * 128], idb[:],
                     reads=[nr, r_id], writes=[pr])
            P.cp("act" if tt % 2 else "dve", hT[:, hj, :, tt * 128:(tt + 1) * 128],
                 ptr[:, pj, :].rearrange("p (k t) -> p k t", k=8), reads=[pr], writes=[hr])

        def mgroup(ch, m):
            hj, hr = hslot[ch]
            hrs = hTres[hj]
            c0 = ch * 512
            if m == 10:
                vj, vr = vst_r.next()
                mj, mr = pmm_r.next()
                for tt in range(4):
                    for kt in range(8):
                        P.mm(pmm[:, mj, tt * 128:(tt + 1) * 128], hT[:, hj, kt, tt * 128:(tt + 1) * 128],
                             wi[:, kt, 1280:1408], start=(kt == 0), stop=(kt == 7), reads=[hrs[tt], r_wik[kt]],
                             writes=[mr])
                P.cp("act", vst[:, vj, :, :], pmm[:, mj, :].rearrange("p (t c) -> p t c", t=4),
                     reads=[mr], writes=[vr])
                P.dma("pool", T["pV"][c0:c0 + 512, :].rearrange("(t p) c -> p t c", p=128), vst[:, vj, :, :],
                      reads=[vr])
                return
            mj, mr = pmm_r.next()
            for kt in range(8):
                P.mm(pmm[:, mj, :], wi[:, kt, m * 128:(m + 1) * 128], hT[:, hj, kt, :], start=(kt == 0),
                     stop=(kt == 7), reads=hrs + [r_wik[kt]], writes=[mr])
            oj, orr = ost_r.next()
            P.cp("act" if cnt[0] % 2 else "dve", ost[:, oj, :], pmm[:, mj, :], reads=[mr], writes=[orr])
            cnt[0] += 1
            if m < 6:
                dst = T["pA"][m * 128:(m + 1) * 128, c0:c0 + 512]
            elif m < 10:
                dst = T["pQK"][(m - 6) * 128:(m - 5) * 128, c0:c0 + 512]
            else:
                dst = T["pC"][(m - 11) * 128:(m - 10) * 128, c0:c0 + 512]
            P.dma("pool", dst, ost[:, oj, :], reads=[orr])

        pending = []
        bgq = prep_ops(P, nc, T, sb) if with_prep else []
        for step in range(NT + 2):
            for _ in range(3):
                if bgq:
                    bgq.pop(0)()
            if step < NT:
                s1(step)
            if 0 <= step - 1 < NT:
                s2(step - 1)
                if (step - 1) % 4 == 3:
                    pending.extend([((step - 1) // 4, m) for m in range(17)])
            for _ in range(5):
                if pending:
                    mgroup(*pending.pop(0))
        while pending:
            mgroup(*pending.pop(0))
        while bgq:
            bgq.pop(0)()
        P.barrier()
        P.emit_block()


def phase_outproj(P, nc, T, l, xsrc, pre=None):
    with contextlib.ExitStack() as st:
        sb = _sb(nc, st)
        wo = sb("wo", [128, 8, D], BF16)
        stg = sb("stg", [128, 2, D], F32)
        idf = sb("idf", [128, 128], F32)
        idb = sb("idb", [128, 128], BF16)
        yc = sb("yc", [128, 2, 8, 512], BF16)
        xt = sb("xt", [128, 3, D], F32)
        x1 = sb("x1", [128, 3, D], F32)
        sq2 = sb("sq", [128, 2, D], BF16)
        xn3 = sb("xn", [128, 3, D], BF16)
        h2 = sb("h2", [128, 2, 8, 512], BF16)
        sm4 = sb("sm", [128, 4, 8], F32)
        pmm = st.enter_context(nc.psum_tensor(_un("pmm"), [128, 2, 1024], F32))
        ptr = st.enter_context(nc.psum_tensor(_un("ptr"), [128, 2, 1024], BF16))
        r_id, r_wo = Res("id"), Res("wo")
        P.dma("sp", idf[:], T["ident"], writes=[r_id])
        P.cp("dve", idb[:], idf[:], reads=[r_id], writes=[r_id])
        stg_r = Ring(2, "stg")
        r_wok = [Res("wo%d" % i) for i in range(8)]
        wv = T["w_out"][l].rearrange("(kt p) n -> p kt n", p=128)
        for kt in range(8):
            j, r = stg_r.next()
            P.dma("sp", stg[:, j, :], wv[:, kt, :], writes=[r])
            P.cp("act" if kt % 2 else "dve", wo[:, kt, :], stg[:, j, :], reads=[r], writes=[r_wok[kt]])
        yc_r, xt_r, x1_r, xn_r, h2_r = Ring(2, "yc"), Ring(3, "xt"), Ring(3, "x1"), Ring(3, "xn"), Ring(2, "h2")
        pmm_r, ptr_r = Ring(2, "pmm"), Ring(2, "ptr")
        sm_r = [Res("sm%d" % i) for i in range(4)]
        sq_r = Ring(2, "sq")
        xv = xsrc.rearrange("(n p) d -> n p d", p=128)
        x1v = T["x1"].rearrange("(n p) d -> n p d", p=128)
        yv = T["yT"].rearrange("(kt p) t -> p kt t", p=128)
        hv = T["h2T"].rearrange("(kt p) t -> p kt t", p=128)
        ycs, h2s, sA, sB = {}, {}, {}, {}

        def s1(n):
            ch, tt = n // 4, n % 4
            if tt == 0:
                yj, yr = yc_r.next()
                P.dma("sp", yc[:, yj, :, :], yv[:, :, ch * 512:(ch + 1) * 512], writes=[yr])
                ycs[ch] = (yj, yr)
            yj, yr = ycs[ch]
            xj, xr = xt_r.next()
            P.dma("sp", xt[:, xj, :], xv[n], writes=[xr])
            mj, mr = pmm_r.next()
            for half in range(2):
                for kt in range(8):
                    P.mm(pmm[:, mj, half * 512:(half + 1) * 512], yc[:, yj, kt, tt * 128:(tt + 1) * 128],
                         wo[:, kt, half * 512:(half + 1) * 512], start=(kt == 0), stop=(kt == 7),
                         reads=[yr, r_wok[kt]], writes=[mr])
            sA[n] = (xj, xr, mj, mr)

        def s2(n):
            xj, xr, mj, mr = sA.pop(n)
            oj, orr = x1_r.next()
            P.tt("dve", x1[:, oj, :], xt[:, xj, :], pmm[:, mj, :], ALU.add, reads=[xr, mr], writes=[orr])
            P.dma("pool", x1v[n], x1[:, oj, :], reads=[orr])
            k = n % 4
            smv, smr = sm4[:, k, :], sm_r[k]
            qj, qr = sq_r.next()
            P.act(sq2[:, qj, :], x1[:, oj, :], AF.Square, accum_out=smv[:, 0:1], reads=[orr], writes=[qr, smr])
            P.ts("dve", smv[:, 1:2], smv[:, 0:1], 1.0 / D, EPS, ALU.mult, ALU.add, reads=[smr], writes=[smr])
            P.act(smv[:, 2:3], smv[:, 1:2], AF.Sqrt, reads=[smr], writes=[smr])
            P.recip(smv[:, 3:4], smv[:, 2:3], reads=[smr], writes=[smr])
            nj, nr = xn_r.next()
            P.act(xn3[:, nj, :], x1[:, oj, :], AF.Copy, scale=smv[:, 3:4], reads=[orr, smr], writes=[nr])
            sB[n] = (nj, nr)

        def s3(n):
            nj, nr = sB.pop(n)
            ch, tt = n // 4, n % 4
            if tt == 0:
                h2s[ch] = h2_r.next()
            hj, hr = h2s[ch]
            pj, pr = ptr_r.next()
            for kt in range(8):
                P.tr(ptr[:, pj, kt * 128:(kt + 1) * 128], xn3[:, nj, kt * 128:(kt + 1) * 128], idb[:],
                     reads=[nr, r_id], writes=[pr])
            P.cp("act", h2[:, hj, :, tt * 128:(tt + 1) * 128],
                 ptr[:, pj, :].rearrange("p (k t) -> p k t", k=8), reads=[pr], writes=[hr])
            if tt == 3:
                P.dma("pool", hv[:, :, ch * 512:(ch + 1) * 512], h2[:, hj, :, :], reads=[hr])

        bgq = []
        if pre is not None:
            stgw = sb("stgw", [128, 2, 1024], F32)
            pp2 = sb("pp2", [128, NPP], F32)
            r_pp2 = Res("pp2")
            P.dma("sp", pp2[:], T["pp"][l], writes=[r_pp2])
            sw_r = Ring(2, "stgw")
            uv = T["w_up"][l].rearrange("(kt p) n -> p kt n", p=128)
            wu_p, r_wuh = pre["wu"], pre["r_wuh"]

            def wu_load(q4, kt, k):
                def f():
                    j, r = sw_r.next()
                    P.dma("sp", stgw[:, j, :], uv[:, kt, q4 * 1024:(q4 + 1) * 1024], writes=[r])
                    gcol = pp2[:, PP["g2"] + kt:PP["g2"] + kt + 1]
                    if k % 2:
                        P.act(wu_p[:, kt, q4 * 1024:(q4 + 1) * 1024], stgw[:, j, :], AF.Copy, scale=gcol,
                              reads=[r, r_pp2], writes=[r_wuh[q4][kt]])
                    else:
                        P.ts("dve", wu_p[:, kt, q4 * 1024:(q4 + 1) * 1024], stgw[:, j, :], gcol, None, ALU.mult,
                             reads=[r, r_pp2], writes=[r_wuh[q4][kt]])
                return f

            k = 0
            for q4 in range(4):
                for kt in range(8):
                    bgq.append(wu_load(q4, kt, k))
                    k += 1
        for step in range(NT + 2):
            if step < NT:
                s1(step)
            if 0 <= step - 1 < NT:
                s2(step - 1)
            if 0 <= step - 2 < NT:
                s3(step - 2)
            if bgq:
                bgq.pop(0)()
        while bgq:
            bgq.pop(0)()
        P.barrier()
        P.emit_block()


def phase_mlp(P, nc, T, l, xdst, last, pre=None):
    CH = 512
    with contextlib.ExitStack() as st:
        sb = _sb(nc, st)
        wu = pre["wu"] if pre is not None else sb("wu", [128, 8, D_FF], BF16)
        wd = sb("wd", [128, 32, D], BF16)
        stg = sb("stg", [128, 2, 1024], F32)
        ppt = sb("ppt", [128, NPP], F32)
        hc = sb("hc", [128, 2, 8, CH], BF16)
        aT = sb("aT", [128, 32, CH], BF16)
        rr = sb("rr", [128, 2, CH], F32)
        x1 = sb("x1", [128, 2, D], F32)
        sm = sb("sm", [128, 8], F32)
        if last:
            fg = sb("fg", [128, D], F32)
            sq = sb("sq", [128, D], BF16)
        pup = st.enter_context(nc.psum_tensor(_un("pup"), [128, 4, 512], F32))
        pdn = st.enter_context(nc.psum_tensor(_un("pdn"), [128, 2, 1024], F32))
        r_pp, r_wu, r_wd, r_fg = Res("pp"), Res("wu"), Res("wd"), Res("fg")
        P.dma("sp", ppt[:], T["pp"][l], writes=[r_pp])
        if last:
            P.dma("sp", fg[:], T["fng"], writes=[r_fg])
        stg_r = Ring(2, "stg")
        hc_r = Ring(2, "hc")
        hv = T["h2T"].rearrange("(kt p) t -> p kt t", p=128)
        hc0 = hc_r.next()
        P.dma("sp", hc[:, hc0[0], :, :], hv[:, :, 0:CH], writes=[hc0[1]])
        uv = T["w_up"][l].rearrange("(kt p) n -> p kt n", p=128)
        r_wuh = pre["r_wuh"] if pre is not None else [[Res("wu") for _ in range(8)] for _ in range(4)]
        r_wdf = [Res("wd%d" % i) for i in range(32)]
        bgq = []

        def wu_load(q4, kt, k):
            def f():
                j, r = stg_r.next()
                P.dma("sp", stg[:, j, :], uv[:, kt, q4 * 1024:(q4 + 1) * 1024], writes=[r])
                if k % 2:
                    P.act(wu[:, kt, q4 * 1024:(q4 + 1) * 1024], stg[:, j, :], AF.Copy,
                          scale=ppt[:, PP["g2"] + kt:PP["g2"] + kt + 1], reads=[r, r_pp], writes=[r_wuh[q4][kt]])
                else:
                    P.ts("dve", wu[:, kt, q4 * 1024:(q4 + 1) * 1024], stg[:, j, :],
                         ppt[:, PP["g2"] + kt:PP["g2"] + kt + 1], None, ALU.mult, reads=[r, r_pp],
                         writes=[r_wuh[q4][kt]])
            return f

        dv = T["w_down"][l].rearrange("(ft p) n -> p ft n", p=128)

        def wd_load(f2):
            def f():
                j, r = stg_r.next()
                P.dma("sp", stg[:, j, :], dv[:, f2, :], writes=[r])
                P.cp("act" if f2 % 2 else "dve", wd[:, f2, :], stg[:, j, :], reads=[r], writes=[r_wdf[f2]])
            return f

        k = 0
        for q4 in range(4):
            for kt in range(8):
                if pre is None:
                    if q4 < 2:
                        wu_load(q4, kt, k)()
                    else:
                        bgq.append(wu_load(q4, kt, k))
                k += 1
        for f2 in range(32):
            bgq.append(wd_load(f2))
        rr_r, x1_r, pup_r, pdn_r = (Ring(2, "rr"), Ring(2, "x1"), Ring(4, "pup"), Ring(2, "pdn"))
        r_aT, sm_r, r_sq = Res("aT"), Res("sm"), Res("sq")
        x1v = T["x1"].rearrange("(n p) d -> n p d", p=128)
        xov = xdst.rearrange("(n p) d -> n p d", p=128)
        for ch in range(L // CH):
            c0 = ch * CH
            if ch == 0:
                hj, hr = hc0
            else:
                hj, hr = hc_r.next()
                P.dma("sp", hc[:, hj, :, :], hv[:, :, c0:c0 + CH], writes=[hr])
            for ft in range(32):
                uj, ur = pup_r.next()
                for kt in range(8):
                    P.mm(pup[:, uj, 0:CH], wu[:, kt, ft * 128:(ft + 1) * 128], hc[:, hj, kt, :], start=(kt == 0),
                         stop=(kt == 7), reads=[hr, r_wuh[ft // 8][kt]], writes=[ur])
                rj, rres = rr_r.next()
                P.act(rr[:, rj, :], pup[:, uj, 0:CH], AF.Relu, reads=[ur], writes=[rres])
                P.tt("pool" if ft % 4 == 3 else "dve", aT[:, ft, :], rr[:, rj, :], rr[:, rj, :], ALU.mult,
                     reads=[rres], writes=[r_aT])
                for _ in range((1 if ft < 16 else 2) if pre is None else 1):
                    if bgq:
                        bgq.pop(0)()
            while bgq:
                bgq.pop(0)()
            for tt in range(CH // 128):
                n = ch * (CH // 128) + tt
                xj, xr = x1_r.next()
                P.dma("sp", x1[:, xj, :], x1v[n], writes=[xr])
                dj, dr = pdn_r.next()
                for half in range(2):
                    for ft in range(32):
                        P.mm(pdn[:, dj, half * 512:(half + 1) * 512], aT[:, ft, tt * 128:(tt + 1) * 128],
                             wd[:, ft, half * 512:(half + 1) * 512], start=(ft == 0), stop=(ft == 31),
                             reads=[r_aT, r_wdf[ft]], writes=[dr])
                P.tt("dve", x1[:, xj, :], x1[:, xj, :], pdn[:, dj, :], ALU.add, reads=[dr], writes=[xr])
                if last:
                    P.act(sq[:], x1[:, xj, :], AF.Square, accum_out=sm[:, 0:1], reads=[xr], writes=[r_sq, sm_r])
                    P.ts("dve", sm[:, 1:2], sm[:, 0:1], 1.0 / D, EPS, ALU.mult, ALU.add, reads=[sm_r], writes=[sm_r])
                    P.act(sm[:, 2:3], sm[:, 1:2], AF.Sqrt, reads=[sm_r], writes=[sm_r])
                    P.recip(sm[:, 3:4], sm[:, 2:3], reads=[sm_r], writes=[sm_r])
                    P.stt(x1[:, xj, :], x1[:, xj, :], sm[:, 3:4], fg[:], ALU.mult, ALU.mult,
                          reads=[sm_r, r_fg], writes=[xr])
                P.dma("pool", xov[n], x1[:, xj, :], reads=[xr])
        P.barrier()
        P.emit_block()


def dve_mod8192(P, X, Tm, res):
    P.ts("dve", Tm, X, 1.0 / 8192.0, -0.49999, ALU.mult, ALU.add, reads=[res], writes=[res])
    P.ts("dve", Tm, Tm, MAGIC, None, ALU.add, reads=[res], writes=[res])
    P.ts("dve", Tm, Tm, -MAGIC, -8192.0, ALU.add, ALU.mult, reads=[res], writes=[res])
    P.tt("dve", X, X, Tm, ALU.add, reads=[res], writes=[res])


def prep_ops(P, nc, T, sb):
    I32 = mybir.dt.int32
    W = 2048
    ki = sb("pki", [128, W], I32)
    X = sb("pX", [128, W], F32)
    Tm = sb("pTm", [128, W], F32)
    Yc = sb("pY", [128, W], F32)
    ob = sb("pob", [128, 3, W], BF16)
    pi = sb("ppi", [128, 1], I32)
    pf = sb("ppf", [128, 1], F32)
    pf_pi = sb("ppfpi", [128, 1], F32)
    pf_npi = sb("ppfnpi", [128, 1], F32)
    r = Res("prep")
    sc = TWO_PI / 8192.0
    ops = []
    A = ops.append
    A(lambda: P.memset("dve", pf_pi[:], math.pi, writes=[r]))
    A(lambda: P.memset("dve", pf_npi[:], -math.pi, writes=[r]))
    A(lambda: P.op("pool", lambda e: e.iota(pi[:], pattern=[[0, 1]], base=0, channel_multiplier=1), writes=[r]))
    A(lambda: P.cp("dve", pf[:], pi[:], reads=[r], writes=[r]))

    def gen(npart, pattern, base, dsts, col0, want, rowscale=None):
        x, t, y = X[:npart, :], Tm[:npart, :], Yc[:npart, :]
        A(lambda: P.op("pool", lambda e: e.iota(ki[:npart, :], pattern=pattern, base=base, channel_multiplier=0),
                       writes=[r]))
        A(lambda: P.cp("dve", x, ki[:npart, :], reads=[r], writes=[r]))
        A(lambda: P.ts("dve", x, x, pf[:npart, :], None, ALU.mult, reads=[r], writes=[r]))
        A(lambda: P.ts("dve", t, x, 1.0 / 8192.0, -0.49999, ALU.mult, ALU.add, reads=[r], writes=[r]))
        A(lambda: P.ts("dve", t, t, MAGIC, None, ALU.add, reads=[r], writes=[r]))
        A(lambda: P.ts("dve", t, t, -MAGIC, -8192.0, ALU.add, ALU.mult, reads=[r], writes=[r]))
        A(lambda: P.tt("dve", x, x, t, ALU.add, reads=[r], writes=[r]))
        if want[1]:
            A(lambda: P.act(ob[:npart, 1, :], x, AF.Sin, bias=pf_pi[:npart, :], scale=-sc, reads=[r], writes=[r]))
        A(lambda: P.act(ob[:npart, 2, :], x, AF.Sin, bias=pf_npi[:npart, :], scale=sc, reads=[r], writes=[r]))
        A(lambda: P.ts("dve", y, x, 6144.0, -8192.0, ALU.is_ge, ALU.mult, reads=[r], writes=[r]))
        A(lambda: P.stt(y, x, 2048.0, y, ALU.add, ALU.add, reads=[r], writes=[r]))
        A(lambda: P.act(ob[:npart, 0, :], y, AF.Sin, bias=pf_pi[:npart, :], scale=-sc, reads=[r], writes=[r]))
        for j, d in enumerate(dsts):
            if d is not None:
                if rowscale is not None:
                    A(lambda j=j: P.ts("dve", ob[:npart, j, :], ob[:npart, j, :], rowscale, None, ALU.mult,
                                       reads=[r], writes=[r]))
                A(lambda j=j, d=d: P.dma("pool", d[:, col0:col0 + W], ob[:npart, j, :], reads=[r]))

    for chn in range(4):
        gen(128, [[1, 16], [64, 128]], 16 * chn, [T["mcb"], T["msb"], T["msnb"]], chn * W, (1, 1, 1))
    wcol = sb("pwcol", [128, 4], F32)
    A(lambda: P.dma("sp", wcol[:], T["cvec"], writes=[r]))
    for chn in range(2):
        gen(64, [[1, 64], [128, 32]], 64 * chn, [T["irb"], None, T["iib"]], chn * W, (1, 0, 1),
            rowscale=wcol[0:64, 2:3])
    return ops


def phase_prep_only(P, nc, T):
    with contextlib.ExitStack() as st:
        sb = _sb(nc, st)
        for f in prep_ops(P, nc, T, sb):
            f()
        P.barrier()
        P.emit_block()


def phase_prep(P, nc, T):
    with contextlib.ExitStack() as st:
        sb = _sb(nc, st)
        I32 = mybir.dt.int32
        ki = sb("ki", [128, 8192], I32)
        X = sb("X", [128, 8192], F32)
        Tm = sb("Tm", [128, 8192], F32)
        Y = sb("Y", [128, 8192], F32)
        ob = sb("ob", [128, 3, 8192], BF16)
        pi = sb("pi", [128, 1], I32)
        pf = sb("pf", [128, 1], F32)
        r = Res("prep")
        s = TWO_PI / 8192.0
        P.op("pool", lambda e: e.iota(pi[:], pattern=[[0, 1]], base=0, channel_multiplier=1), writes=[r])
        P.cp("dve", pf[:], pi[:], reads=[r], writes=[r])

        def gen(npart, pattern, dsts):
            P.op("pool", lambda e: e.iota(ki[:npart, :], pattern=pattern, base=0, channel_multiplier=0), writes=[r])
            P.cp("dve", X[:npart, :], ki[:npart, :], reads=[r], writes=[r])
            P.ts("dve", X[:npart, :], X[:npart, :], pf[:npart, :], None, ALU.mult, reads=[r], writes=[r])
            dve_mod8192(P, X[:npart, :], Tm[:npart, :], r)
            P.act(ob[:npart, 1, :], X[:npart, :], AF.Sin, bias=pf_pi[:npart, :], scale=-s, reads=[r], writes=[r])
            P.act(ob[:npart, 2, :], X[:npart, :], AF.Sin, bias=pf_npi[:npart, :], scale=s, reads=[r], writes=[r])
            P.ts("dve", Y[:npart, :], X[:npart, :], 6144.0, -8192.0, ALU.is_ge, ALU.mult, reads=[r], writes=[r])
            P.stt(Y[:npart, :], X[:npart, :], 2048.0, Y[:npart, :], ALU.add, ALU.add, reads=[r], writes=[r])
            P.act(ob[:npart, 0, :], Y[:npart, :], AF.Sin, bias=pf_pi[:npart, :], scale=-s, reads=[r], writes=[r])
            for j, d in enumerate(dsts):
                if d is not None:
                    P.dma("sp", d, ob[:npart, j, 0:d.shape[1]], reads=[r])

        pf_pi = sb("pfpi", [128, 1], F32)
        pf_npi = sb("pfnpi", [128, 1], F32)
        P.memset("dve", pf_pi[:], math.pi, writes=[r])
        P.memset("dve", pf_npi[:], -math.pi, writes=[r])
        gen(128, [[128, 64], [1, 128]] if False else [[1, 64], [64, 128]], [T["mcb"], T["msb"], T["msnb"]])
        P.barrier()
        P.emit_block()
    with contextlib.ExitStack() as st:
        sb = _sb(nc, st)
        I32 = mybir.dt.int32
        ki = sb("ki", [64, 4096], I32)
        X = sb("X", [64, 4096], F32)
        Tm = sb("Tm", [64, 4096], F32)
        Y = sb("Y", [64, 4096], F32)
        ob = sb("ob", [64, 3, 4096], BF16)
        pi = sb("pi", [64, 1], I32)
        pf = sb("pf", [64, 1], F32)
        pf_pi = sb("pfpi", [64, 1], F32)
        pf_npi = sb("pfnpi", [64, 1], F32)
        r = Res("prep2")
        s = TWO_PI / 8192.0
        P.memset("dve", pf_pi[:], math.pi, writes=[r])
        P.memset("dve", pf_npi[:], -math.pi, writes=[r])
        P.op("pool", lambda e: e.iota(pi[:], pattern=[[0, 1]], base=0, channel_multiplier=1), writes=[r])
        P.cp("dve", pf[:], pi[:], reads=[r], writes=[r])
        P.op("pool", lambda e: e.iota(ki[:], pattern=[[1, 128], [128, 32]], base=0, channel_multiplier=0), writes=[r])
        P.cp("dve", X[:], ki[:], reads=[r], writes=[r])
        P.ts("dve", X[:], X[:], pf[:], None, ALU.mult, reads=[r], writes=[r])
        dve_mod8192(P, X[:], Tm[:], r)
        P.act(ob[:, 2, :], X[:], AF.Sin, bias=pf_npi[:], scale=s, reads=[r], writes=[r])
        P.ts("dve", Y[:], X[:], 6144.0, -8192.0, ALU.is_ge, ALU.mult, reads=[r], writes=[r])
        P.stt(Y[:], X[:], 2048.0, Y[:], ALU.add, ALU.add, reads=[r], writes=[r])
        P.act(ob[:, 0, :], Y[:], AF.Sin, bias=pf_pi[:], scale=-s, reads=[r], writes=[r])
        P.dma("sp", T["irb"], ob[:, 0, :], reads=[r])
        P.dma("sp", T["iib"], ob[:, 2, :], reads=[r])
        P.barrier()
        P.emit_block()


def phase_lru(P, nc, T, l):
    with contextlib.ExitStack() as st:
        sb = _sb(nc, st)
        ppt = sb("ppt", [128, NPP], F32)
        gst = sb("gst", [128, 4, 128], F32)
        gw = sb("gw", [128, 4, 128], BF16)
        U = sb("U", [128, L + 3], F32)
        XC = sb("XC", [128, L], F32)
        XCB = sb("XCB", [128, L], BF16)
        UB = sb("UB", [128, L + 3], BF16)
        DG = sb("DG", [128, 4, 128], BF16)
        idf = sb("idf", [128, 128], F32)
        r_UB, r_DG = Res("UB"), Res("DG")
        Ad = [sb("A%d" % d, [128, L], F32) for d in range(2)]
        Bd = [sb("B%d" % d, [128, L], F32) for d in range(2)]
        TMP = sb("TMP", [128, L], F32)
        G = sb("G", [128, L], F32)
        r_G = Res("G")
        YA = sb("YA", [128, 3, L], F32)
        cs = sb("cs", [128, 8], F32)
        onf = sb("onf", [128, 128], F32)
        onb = sb("onb", [128, 128], BF16)
        sqb = sb("sqb", [128, 3, 512], BF16)
        rst = sb("rst", [128, 512], F32)
        ob = sb("ob", [128, 2, 512], BF16)
        pg = st.enter_context(nc.psum_tensor(_un("pg"), [128, 4, 512], F32))
        pn = st.enter_context(nc.psum_tensor(_un("pn"), [128, 2, 512], F32))
        r_pp, r_on, r_gw, r_U, r_XC, r_XCB, r_T, r_cs = (Res("pp"), Res("on"), Res("gw"), Res("U"), Res("XC"),
                                                         Res("XCB"), Res("TMP"), Res("cs"))
        r_A = [Res("A0"), Res("A1")]
        r_B = [Res("B0"), Res("B1")]
        r_YA = [Res("YA%d" % i) for i in range(3)]
        pg_r, pn_r, ob_r = Ring(4, "pg"), Ring(2, "pn"), Ring(2, "ob")
        r_sq, r_rst, r_gst = Res("sq"), Res("rst"), Res("gst")
        P.dma("sp", ppt[:], T["pp"][l], writes=[r_pp])
        P.dma("sp", onf[:], T["ones"], writes=[r_on])
        P.dma("sp", idf[:], T["ident"], writes=[r_on])
        P.cp("dve", onb[:], onf[:], reads=[r_on], writes=[r_on])
        for ta in range(3):
            c = lambda nm, k=0: ppt[:, PP[nm] + k:PP[nm] + k + 1]
            P.dma("sp", gst[:], T["gates"][l, :, ta, :, :].rearrange("g p m -> p g m"), writes=[r_gst])
            P.cp("dve", gw[:], gst[:], reads=[r_gst], writes=[r_gw])
            for d in range(2):
                P.act(cs[:, d:d + 1], c("lam", 3 * d + ta), AF.Exp, scale=-1.0, reads=[r_pp], writes=[r_cs])
            for d in range(2):
                P.act(cs[:, d:d + 1], cs[:, d:d + 1], AF.Ln, bias=1.0, reads=[r_cs], writes=[r_cs])
            P.ts("dve", cs[:, 0:2], cs[:, 0:2], -8.0, None, ALU.mult, reads=[r_cs], writes=[r_cs])
            P.dma("sp", G[:], T["pA"][384 + ta * 128:384 + (ta + 1) * 128, :], writes=[r_G])
            P.act(G[:], G[:], AF.Gelu, reads=[r_G], writes=[r_G])
            P.memset("pool", U[:, 0:2], 0.0, writes=[r_U])
            P.memset("pool", U[:, L + 2:L + 3], 0.0, writes=[r_U])
            P.dma("sp", U[:, 2:L + 2], T["pA"][ta * 128:(ta + 1) * 128, :], writes=[r_U])
            P.cp("act", UB[:], U[:], reads=[r_U], writes=[r_UB])
            for k in range(4):
                P.ts("dve", DG[:, k, :], idf[:], c("caw", 3 * k + ta), None, ALU.mult, reads=[r_on, r_pp],
                     writes=[r_DG])
            for ch in range(8):
                j, r = pg_r.next()
                for k in range(4):
                    P.mm(pg[:, j, :], DG[:, k, :], UB[:, ch * 512 + k:ch * 512 + k + 512], start=(k == 0), stop=(k == 3),
                         reads=[r_DG, r_UB], writes=[r])
                P.act(XC[:, ch * 512:(ch + 1) * 512], pg[:, j, :], AF.Identity, bias=c("cab", ta), reads=[r, r_pp],
                      writes=[r_XC])
            P.cp("dve", XCB[:], XC[:], reads=[r_XC], writes=[r_XCB])
            for d in range(2):
                for ch in range(8):
                    sl = slice(ch * 512, (ch + 1) * 512)
                    j, r = pg_r.next()
                    P.mm(pg[:, j, :], gw[:, 2 * d, :], XCB[:, sl], reads=[r_gw, r_XCB], writes=[r])
                    P.act(Ad[d][:, sl], pg[:, j, :], AF.Sigmoid, bias=c("ba", 3 * d + ta), reads=[r, r_pp],
                          writes=[r_A[d]])
                    j, r = pg_r.next()
                    P.mm(pg[:, j, :], gw[:, 2 * d + 1, :], XCB[:, sl], reads=[r_gw, r_XCB], writes=[r])
                    P.act(Bd[d][:, sl], pg[:, j, :], AF.Sigmoid, bias=c("bx", 3 * d + ta), reads=[r, r_pp],
                          writes=[r_B[d]])
                scr, r_scr = (TMP[:], r_T) if d == 0 else (U[:, 0:L], r_U)
                P.act(Ad[d][:], Ad[d][:], AF.Exp, scale=cs[:, d:d + 1], reads=[r_cs], writes=[r_A[d]])
                P.act(scr, Ad[d][:], AF.Square, reads=[r_A[d]], writes=[r_scr])
                P.act(scr, scr, AF.Sqrt, bias=1.0, scale=-1.0, reads=[r_scr], writes=[r_scr])
                P.tt("pool", Bd[d][:], Bd[d][:], XC[:], ALU.mult, reads=[r_XC], writes=[r_B[d]])
                P.tt("dve", Bd[d][:], Bd[d][:], scr, ALU.mult, reads=[r_scr], writes=[r_B[d]])
                if d == 0:
                    P.scan(TMP[:], Ad[0][:], Bd[0][:], reads=[r_A[0], r_B[0]], writes=[r_T])
                else:
                    P.scan(XC[:, ::-1], Ad[1][:, ::-1], Bd[1][:, ::-1], reads=[r_A[1], r_B[1]], writes=[r_XC])
            P.tt("dve", TMP[:], TMP[:], XC[:], ALU.add, reads=[r_XC], writes=[r_T])
            P.tt("dve", YA[:, ta, :], TMP[:], G[:], ALU.mult, reads=[r_T, r_G], writes=[r_YA[ta]])
        for ch in range(8):
            sl = slice(ch * 512, (ch + 1) * 512)
            for ta in range(3):
                P.act(sqb[:, ta, :], YA[:, ta, sl], AF.Square, reads=[r_YA[ta]], writes=[r_sq])
            j, r = pn_r.next()
            for ta in range(3):
                P.mm(pn[:, j, :], onb[:], sqb[:, ta, :], start=(ta == 0), stop=(ta == 2), reads=[r_on, r_sq],
                     writes=[r])
            P.act(rst[:], pn[:, j, :], AF.Sqrt, bias=EPS, scale=1.0 / D_A, reads=[r], writes=[r_rst])
            P.recip(rst[:], rst[:], reads=[r_rst], writes=[r_rst])
            for ta in range(3):
                oj, orr = ob_r.next()
                P.stt(ob[:, oj, :], YA[:, ta, sl], ppt[:, PP["gna"] + ta:PP["gna"] + ta + 1], rst[:], ALU.mult,
                      ALU.mult, reads=[r_YA[ta], r_rst, r_pp], writes=[orr])
                P.dma("pool", T["yT"][ta * 128:(ta + 1) * 128, sl], ob[:, oj, :], reads=[orr])
        P.barrier()
        P.emit_block()


def phase_attn(P, nc, T, l):
    with contextlib.ExitStack() as st:
        sb = _sb(nc, st)
        ppt = sb("ppt", [128, NPP], F32)
        idf = sb("idf", [128, 128], F32)
        idb = sb("idb", [128, 128], BF16)
        prf = sb("prf", [128, 128], F32)
        prb = sb("prb", [128, 128], BF16)
        mkf = sb("mkf", [128, 2, 128], F32)
        mkb = sb("mkb", [128, 2, 128], BF16)
        ct = sb("ct", [128, L], F32)
        stt_ = sb("st", [128, L], F32)
        S2 = sb("S", [128, 2, L], F32)
        XB2 = sb("XB", [128, 2, L], BF16)
        Q = sb("Q", [128, 3, L], BF16)
        KK = sb("KK", [128, 2, L], BF16)
        VA = sb("VA", [128, NT, 2, 65], BF16)
        t1 = sb("t1", [128, 2, 512], F32)
        t2 = sb("t2", [128, 2, 512], F32)
        PT = sb("PT", [128, 6, 6, 384], BF16)
        es = sb("es", [128, 6], F32)
        den = sb("den", [128, 2, 6], F32)
        yb = sb("yb", [128, 2, 384], F32)
        ybn = sb("ybn", [128, 3, 384], BF16)
        sqf = sb("sqf", [128, 384], F32)
        epsb = sb("epsb", [128, 1], F32)
        sm4 = sb("sm", [128, 4, 8], F32)
        YT = sb("YT", [128, 3, L], BF16)
        ps = st.enter_context(nc.psum_tensor(_un("ps"), [128, 4, 512], F32))
        po = st.enter_context(nc.psum_tensor(_un("po"), [128, 2, 512], F32))
        ptr = st.enter_context(nc.psum_tensor(_un("ptr"), [128, 2, 1024], BF16))
        r_pp, r_c, r_tab, r_S, r_XB, r_VA, r_es, r_sm, r_YT, r_sq = (Res("pp"), Res("c"), Res("tab"), Res("S"),
                                                                    Res("XB"), Res("VA"), Res("es"), Res("sm"),
                                                                    Res("YT"), Res("sq"))
        r_Q = [Res("Q%d" % i) for i in range(3)]
        r_K = [Res("K%d" % i) for i in range(2)]
        ps_r, po_r, ptr_r, t1_r, t2_r, PT_r = (Ring(4, "ps"), Ring(2, "po"), Ring(2, "ptr"), Ring(2, "t1"),
                                               Ring(2, "t2"), [Res("PT%d" % i) for i in range(6)])
        den_r, yb_r, ybn_r = Ring(2, "den"), Ring(2, "yb"), Ring(3, "ybn")
        P.dma("sp", ppt[:], T["pp"][l], writes=[r_pp])
        P.dma("sp", idf[:], T["ident"], writes=[r_c])
        P.dma("sp", prf[:], T["prot"], writes=[r_c])
        P.dma("sp", mkf[:, 0, :], T["maskn"], writes=[r_c])
        P.dma("sp", mkf[:, 1, :], T["maskp"], writes=[r_c])
        P.cp("dve", idb[:], idf[:], reads=[r_c], writes=[r_c])
        P.cp("dve", prb[:], prf[:], reads=[r_c], writes=[r_c])
        P.cp("dve", mkb[:], mkf[:], reads=[r_c], writes=[r_c])
        P.memset("pool", ct[:], 1.0, writes=[r_tab])
        P.memset("pool", stt_[:], 0.0, writes=[r_tab])
        for base in (0, 64):
            for off in (0, 8):
                P.dma("sp", ct[base + off:base + off + 8, :], T["rope"][0:8, :], writes=[r_tab])
                P.dma("sp", stt_[base + off:base + off + 8, :], T["rope"][8:16, :], writes=[r_tab])
        for base in (0, 64):
            P.ts("dve", stt_[base:base + 8, :], stt_[base:base + 8, :], -1.0, None, ALU.mult, writes=[r_tab])
        P.act(es[:], ppt[:, PP["sink"]:PP["sink"] + 6], AF.Exp, reads=[r_pp], writes=[r_es])
        P.memset("dve", epsb[:], EPS, writes=[r_es])
        S_r, XB_r = Ring(2, "S"), Ring(2, "XB")
        sj, sr = S_r.next()
        P.dma("sp", S2[:, sj, :].rearrange("p (n c) -> p n c", c=128), T["pV"].rearrange("(n p) c -> p n c", p=128),
              writes=[sr])
        P.memset("pool", VA[:], 1.0, writes=[r_VA])
        P.cp("dve", VA[:, :, :, 0:64], S2[:, sj, :].rearrange("p (n g c) -> p n g c", g=2, c=64), reads=[sr],
             writes=[r_VA])

        def rope(dst, dres, loads):
            sj, sr = S_r.next()
            for (pr, src) in loads:
                P.dma("sp", S2[pr, sj, :], src, writes=[sr])
            bj, br = XB_r.next()
            P.cp("act", XB2[:, bj, :], S2[:, sj, :], reads=[sr], writes=[br])
            for ch in range(8):
                sl = slice(ch * 512, (ch + 1) * 512)
                j, r = ps_r.next()
                P.mm(ps[:, j, :], prb[:], XB2[:, bj, sl], reads=[r_c, br], writes=[r])
                j1, r1 = t1_r.next()
                P.tt("dve", t1[:, j1, :], ps[:, j, :], stt_[:, sl], ALU.mult, reads=[r, r_tab], writes=[r1])
                j2, r2 = t2_r.next()
                P.tt("pool", t2[:, j2, :], S2[:, sj, sl], ct[:, sl], ALU.mult, reads=[sr, r_tab], writes=[r2])
                P.tt("dve", dst[:, sl], t1[:, j1, :], t2[:, j2, :], ALU.add, reads=[r1, r2], writes=[dres])

        for qt in range(3):
            rope(Q[:, qt, :], r_Q[qt], [(slice(0, 128), T["pQK"][qt * 128:(qt + 1) * 128, :])])
        rope(KK[:, 0, :], r_K[0], [(slice(0, 128), T["pQK"][384:512, :])])

        NPT = 6
        stB = {}
        sm_r = [Res("sm%d" % i) for i in range(4)]

        def stage_b(i):
            oj, orr = po_r.next()
            for h in range(6):
                g = h % 2
                kbs = [kb for kb in (i - 1, i, i + 1) if 0 <= kb < NT]
                for n, kb in enumerate(kbs):
                    pos = i - kb + 1
                    P.mm(po[:, oj, h * 65:(h + 1) * 65], PT[:, kb % NPT, h, pos * 128:(pos + 1) * 128],
                         VA[:, kb, g, :], start=(n == 0), stop=(n == len(kbs) - 1), reads=[PT_r[kb % NPT], r_VA],
                         writes=[orr])
            dj, dr = den_r.next()
            ov = po[:, oj, 0:390].rearrange("p (h c) -> p h c", c=65)
            P.tt("dve", den[:, dj, :], ov[:, :, 64], es[:], ALU.add, reads=[orr, r_es], writes=[dr])
            P.recip(den[:, dj, :], den[:, dj, :], reads=[dr], writes=[dr])
            yj, yr = yb_r.next()
            P.tt("dve", yb[:, yj, :].rearrange("p (h c) -> p h c", c=64), ov[:, :, 0:64],
                 den[:, dj, :].unsqueeze(2).broadcast_to([128, 6, 64]), ALU.mult, reads=[orr, dr], writes=[yr])
            k = i % 4
            smv, smr = sm4[:, k, :], sm_r[k]
            P.stt(sqf[:], yb[:, yj, :], 1.0, yb[:, yj, :], ALU.mult, ALU.mult, reads=[yr], writes=[r_sq, smr],
                  accum_out=smv[:, 0:1])
            P.act(smv[:, 2:3], smv[:, 0:1], AF.Ln, bias=epsb[:], scale=1.0 / D_B, reads=[smr, r_es], writes=[smr])
            P.act(smv[:, 3:4], smv[:, 2:3], AF.Exp, scale=-0.5, reads=[smr], writes=[smr])
            nj, nr = ybn_r.next()
            P.ts("dve", ybn[:, nj, :], yb[:, yj, :], smv[:, 3:4], None, ALU.mult, reads=[yr, smr], writes=[nr])
            stB[i] = (nj, nr)

        def stage_c(i):
            nj, nr = stB.pop(i)
            tj, trr = ptr_r.next()
            for tb in range(3):
                P.tr(ptr[:, tj, tb * 128:(tb + 1) * 128], ybn[:, nj, tb * 128:(tb + 1) * 128], idb[:],
                     reads=[nr, r_c], writes=[trr])
            for tb in range(3):
                P.ts("dve", YT[:, tb, i * 128:(i + 1) * 128], ptr[:, tj, tb * 128:(tb + 1) * 128],
                     ppt[:, PP["gnb"] + tb:PP["gnb"] + tb + 1], None, ALU.mult, reads=[trr, r_pp], writes=[r_YT])

        def stage_a(jb):
            lo_b, hi_b = max(jb - 1, 0), min(jb + 1, NT - 1)
            slot = jb % NPT
            for h in range(6):
                g, o, qt = h % 2, 64 * (h % 2), h // 2
                kt = 0 if o == 64 * g else 1
                c0 = (lo_b - jb + 1) * 128
                ncol = (hi_b - lo_b + 1) * 128
                j, r = ps_r.next()
                P.mm(ps[:, j, 0:ncol], KK[o:o + 64, kt, jb * 128:(jb + 1) * 128],
                     Q[o:o + 64, qt, lo_b * 128:(hi_b + 1) * 128], reads=[r_K[kt], r_Q[qt]], writes=[r])
                P.act(PT[:, slot, h, c0:c0 + ncol], ps[:, j, 0:ncol], AF.Exp, scale=0.125, reads=[r],
                      writes=[PT_r[slot]])
            if jb > 0:
                P.tt("dve", PT[:, slot, :, 0:128], PT[:, slot, :, 0:128], mkb[:, 0:1, :].broadcast_to([128, 6, 128]),
                     ALU.mult, reads=[r_c], writes=[PT_r[slot]])
            if jb < NT - 1:
                P.tt("dve", PT[:, slot, :, 256:384], PT[:, slot, :, 256:384],
                     mkb[:, 1:2, :].broadcast_to([128, 6, 128]), ALU.mult, reads=[r_c], writes=[PT_r[slot]])

        for step in range(NT + 4):
            if step < NT:
                stage_a(step)
            if 0 <= step - 2 < NT:
                stage_b(step - 2)
            if 0 <= step - 3 < NT:
                stage_c(step - 3)
        P.dma("sp", T["yT"][384:768, :].rearrange("(t p) n -> p t n", p=128), YT[:], reads=[r_YT])
        P.barrier()
        P.emit_block()


def phase_hyena_a(P, nc, T, l):
    with contextlib.ExitStack() as st:
        sb = _sb(nc, st)
        ppt = sb("ppt", [128, NPP], F32)
        cv = sb("cv", [128, 4], F32)
        hz = sb("hz", [33, L + 1], F32)
        w1 = sb("w1", [33, 64], F32)
        w2 = sb("w2", [64, 64], F32)
        w3 = sb("w3", [64, 512], F32)
        frb = sb("frb", [64, 2], F32)
        U1 = sb("U1", [64, 4, 512], F32)
        Tm = sb("Tm", [64, 4, 512], F32)
        H1 = sb("H1", [64, 5, 512], F32)
        H2 = sb("H2", [64, L + 1], F32)
        tp = sb("tp", [128, 2, 512], F32)
        dec = sb("dec", [128, 2, 512], F32)
        fo = sb("fo", [128, 2, 512], BF16)
        U = sb("U", [128, 2, L + 2], F32)
        Rv = sb("Rv", [128, 3, L], F32)
        zb = sb("zb", [128, L], BF16)
        x0b = sb("x0b", [128, L], BF16)
        UBh = sb("UBh", [128, 2, L + 2], BF16)
        DGh = sb("DGh", [128, 2, 3, 128], BF16)
        idfh = sb("idfh", [128, 128], F32)
        p1 = st.enter_context(nc.psum_tensor(_un("p1"), [128, 3, 512], F32))
        p3 = st.enter_context(nc.psum_tensor(_un("p3"), [128, 2, 512], F32))
        pcv = st.enter_context(nc.psum_tensor(_un("pcv"), [128, 2, 512], F32))
        r_pp, r_w, r_hz, r_frb = Res("pp"), Res("w"), Res("hz"), Res("frb")
        P.dma("sp", ppt[:], T["pp"][l], writes=[r_pp])
        P.dma("sp", cv[:], T["cvec"], writes=[r_pp])
        P.dma("sp", w1[:], T["hy_w1"][l], writes=[r_w])
        P.dma("sp", w2[:], T["hy_w2"][l], writes=[r_w])
        P.dma("sp", w3[:], T["hy_w3"][l], writes=[r_w])
        P.memset("pool", hz[:, L:L + 1], 0.0, writes=[r_hz])
        P.dma("sp", hz[:, 0:L], T["hyz"], writes=[r_hz])
        fr = ppt[0:64, PP["hfr"]:PP["hfr"] + 1]
        P.tt("dve", frb[:, 0:1], fr, ppt[0:64, PP["hb1"]:PP["hb1"] + 1], ALU.mult, reads=[r_pp], writes=[r_frb])
        P.tt("dve", frb[:, 1:2], fr, ppt[0:64, PP["hb2"]:PP["hb2"] + 1], ALU.mult, reads=[r_pp], writes=[r_frb])
        p1_r, p3_r, U1_r, H1_r, H2_r, tp_r, dec_r, fo_r = (Ring(3, "p1"), Ring(2, "p3"), Ring(4, "U1"), Ring(5, "H1"),
                                                           Ring(8, "H2"), Ring(2, "tp"), Ring(2, "dec"), Ring(2, "fo"))

        def sin_layer(psrc, pres, k, Hdst, hres):
            uj, ur = U1_r.next()
            u, t = U1[:, uj, :], Tm[:, uj, :]
            P.ts("dve", u, psrc, fr, frb[:, k:k + 1], ALU.mult, ALU.add, reads=[pres, r_pp, r_frb], writes=[ur])
            P.ts("dve", t, u, 1.0 / TWO_PI, MAGIC, ALU.mult, ALU.add, reads=[ur], writes=[ur])
            P.ts("dve", t, t, -MAGIC, -TWO_PI, ALU.add, ALU.mult, reads=[ur], writes=[ur])
            P.tt("dve", u, u, t, ALU.add, reads=[ur], writes=[ur])
            P.ts("dve", u, u, 3.14159, -3.14159, ALU.min, ALU.max, reads=[ur], writes=[ur])
            P.act(Hdst, u, AF.Sin, reads=[ur], writes=[hres])

        sH1, sH2 = {}, {}

        r_H2 = [Res("H2_%d" % i) for i in range(9)]
        P.memset("pool", H2[:, L:L + 1], 0.0, writes=[r_H2[8]])

        def fs1(c):
            rhs = hz[:, c * 512:(c + 1) * 512]
            j, r = p1_r.next()
            P.mm(p1[0:64, j, :], w1[:], rhs, reads=[r_w, r_hz], writes=[r])
            h1j, h1r = H1_r.next()
            sin_layer(p1[0:64, j, :], r, 0, H1[:, h1j, :], h1r)
            sH1[c] = (h1j, h1r)

        def fs2(c):
            h1j, h1r = sH1.pop(c)
            j, r = p1_r.next()
            P.mm(p1[0:64, j, :], w2[:], H1[:, h1j, :], reads=[r_w, h1r], writes=[r])
            sin_layer(p1[0:64, j, :], r, 1, H2[:, c * 512:(c + 1) * 512], r_H2[c])

        def fs3(c):
            if c < 8:
                h2v, h2rs = H2[:, c * 512:(c + 1) * 512], [r_H2[c]]
            else:
                s0 = L - (c - 8) * 512
                h2v, h2rs = H2[:, s0:s0 - 512:-1], r_H2
            tj, tr_ = tp_r.next()
            P.dma("sp", tp[:, tj, :], T["tpos"][0:1, c * 512:(c + 1) * 512].broadcast_to([128, 512]), writes=[tr_])
            half = 0 if c < 8 else 1
            for ctile in range(2):
                j, r = p3_r.next()
                P.mm(p3[:, j, :], w3[:, half * 256 + ctile * 128:half * 256 + (ctile + 1) * 128], h2v,
                     reads=[r_w] + h2rs, writes=[r])
                dj, dr = dec_r.next()
                P.act(dec[:, dj, :], tp[:, tj, :], AF.Exp, scale=cv[:, ctile:ctile + 1], reads=[tr_, r_pp], writes=[dr])
                fj, frr = fo_r.next()
                P.tt("dve", fo[:, fj, :], p3[:, j, :], dec[:, dj, :], ALU.mult, reads=[r, dr], writes=[frr])
                P.dma("pool", T["hcT"][ctile * 128:(ctile + 1) * 128, c * 512:(c + 1) * 512], fo[:, fj, :], reads=[frr])

        U_r = Ring(2, "U")
        UB_r, DG_r, pcv_r = Ring(2, "UBh"), Ring(2, "DGh"), Ring(2, "pcv")
        r_idf = Res("idfh")
        P.dma("sp", idfh[:], T["ident"], writes=[r_idf])
        r_R = [Res("R%d" % i) for i in range(3)]
        r_zb, r_x0 = Res("zb"), Res("x0b")

        def conv_tile(ctile, role):
            ti = role * 2 + ctile
            uj, ur = U_r.next()
            P.memset("pool", U[:, uj, 0:1], 0.0, writes=[ur])
            P.memset("pool", U[:, uj, L + 1:L + 2], 0.0, writes=[ur])
            P.dma("sp", U[:, uj, 1:L + 1], T["pC"][role * 256 + ctile * 128:role * 256 + (ctile + 1) * 128, :],
                  writes=[ur])
            wc = lambda k: ppt[:, PP["hcw"] + 6 * k + ti:PP["hcw"] + 6 * k + ti + 1]
            P.act(Rv[:, role, :], U[:, uj, 0:L], AF.Identity, scale=wc(0), bias=ppt[:, PP["hcb"] + ti:PP["hcb"] + ti + 1],
                  reads=[ur, r_pp], writes=[r_R[role]])
            for k in (1, 2):
                P.stt(Rv[:, role, :], U[:, uj, k:k + L], wc(k), Rv[:, role, :], ALU.mult, ALU.add,
                      reads=[ur, r_pp], writes=[r_R[role]])

        def conv_finish(ctile):
            P.tt("dve", zb[:], Rv[:, 2, :], Rv[:, 1, :], ALU.mult, reads=[r_R[1], r_R[2]], writes=[r_zb])
            P.cp("act", x0b[:], Rv[:, 0, :], reads=[r_R[0]], writes=[r_x0])
            rows = slice(ctile * 128, (ctile + 1) * 128)
            P.dma("pool", T["zT"][rows, :], zb[:], reads=[r_zb])
            P.dma("pool", T["zx"][0, rows, :], zb[:], reads=[r_zb])
            P.dma("pool", T["zx"][1, rows, :], x0b[:], reads=[r_x0])

        cq = []
        for ctile in range(2):
            for role in range(3):
                cq.append((lambda ct=ctile, ro=role: conv_tile(ct, ro)))
            cq.append((lambda ct=ctile: conv_finish(ct)))
        for gi in range(2):
            for c in range(4 * gi, 4 * gi + 4):
                fs1(c)
            for c in range(4 * gi, 4 * gi + 4):
                fs2(c)
            for _ in range(2):
                if cq:
                    cq.pop(0)()
        for gi in range(4):
            for c in range(4 * gi, 4 * gi + 4):
                fs3(c)
            for _ in range(2):
                if cq:
                    cq.pop(0)()
        while cq:
            cq.pop(0)()
        P.barrier()
        P.emit_block()


def phase_hyena_b(P, nc, T, l):
    with contextlib.ExitStack() as st:
        sb = _sb(nc, st)
        ppt = sb("ppt", [128, NPP], F32)
        mc = sb("mc", [128, 64, 128], BF16)
        ms = sb("ms", [128, 64, 128], BF16)
        msn = sb("msn", [128, 64, 128], BF16)
        irs = sb("irs", [128, 128, 32], BF16)
        cst = sb("cst", [128, 780], F32)
        e1zb = sb("e1zb", [128, 264], BF16)
        e1hb = sb("e1hb", [128, 132], BF16)
        g1b = sb("g1b", [128, 256], BF16)
        onb = sb("onb", [128, 128], BF16)
        BIG = sb("BIG", [128, 16384], BF16)
        BST = sb("BST", [128, 128, 64], BF16)
        ZHs = sb("ZHs", [128, 32, 128], BF16)
        ZZs = sb("ZZs", [128, 16, 128], BF16)
        Hs = sb("Hs", [128, 3, 2, 2, 64], F32)
        Y1 = sb("Y1", [128, 64, 2, 64], BF16)
        Y2 = sb("Y2", [128, 64, 2, 64], BF16)
        zg = sb("zg", [128, L], BF16)
        x0g = sb("x0g", [128, L], BF16)
        yc = sb("yc", [128, 2, L], BF16)
        tq = sb("tq", [128, 2, 2, 256], F32)
        gt = sb("gt", [128, 2, 512], F32)
        sqb = sb("sqb", [128, 2, 512], BF16)
        rst = sb("rst", [128, 512], F32)
        ob = sb("ob", [128, 2, 512], BF16)
        pb = st.enter_context(nc.psum_tensor(_un("pb"), [128, 8, 512], F32))
        r_pp, r_m, r_c = Res("pp"), Res("m"), Res("c")
        P.dma("sp", ppt[:], T["pp"][l], writes=[r_pp])
        P.dma("sp", mc[:].rearrange("p a b -> p (a b)"), T["mcb"], writes=[r_m])
        P.dma("sp", ms[:].rearrange("p a b -> p (a b)"), T["msb"], writes=[r_m])
        P.dma("sp", msn[:].rearrange("p a b -> p (a b)"), T["msnb"], writes=[r_m])
        P.dma("sp", irs[0:64, :, :].rearrange("p a b -> p (a b)"), T["irb"], writes=[r_m])
        P.dma("sp", irs[64:128, :, :].rearrange("p a b -> p (a b)"), T["iib"], writes=[r_m])
        P.dma("sp", cst[:, 0:264], T["e1z"], writes=[r_c])
        P.dma("sp", cst[:, 264:396], T["e1h"], writes=[r_c])
        P.dma("sp", cst[:, 396:652], T["g1"], writes=[r_c])
        P.dma("sp", cst[:, 652:780], T["ones"], writes=[r_c])
        P.cp("dve", e1zb[:], cst[:, 0:264], reads=[r_c], writes=[r_c])
        P.cp("dve", e1hb[:], cst[:, 264:396], reads=[r_c], writes=[r_c])
        P.cp("dve", g1b[:], cst[:, 396:652], reads=[r_c], writes=[r_c])
        P.cp("dve", onb[:], cst[:, 652:780], reads=[r_c], writes=[r_c])
        A = BIG[:, :].rearrange("p (r k s c) -> p r k s c", r=2, k=64, s=2)
        Bst = BST[:, :, :]
        r_bst = Res("bst")
        r_zgh, r_x0h = [Res("zg0"), Res("zg1")], [Res("x0g0"), Res("x0g1")]
        deferred = []
        stepc = [0]

        def bg_step():
            stepc[0] += 1
            if deferred and stepc[0] % 5 == 0:
                deferred.pop(0)()
        r_Y2 = Res("Y2")
        r_zh, r_zz, r_big, r_Y, r_zg, r_x0g = Res("zh"), Res("zz"), Res("big"), Res("Y"), Res("zg"), Res("x0g")
        r_lo = r_hi = r_big
        r_yc = [Res("yc0"), Res("yc1")]
        pb_r, tq_r, gt_r, hs_r = Ring(8, "pb"), Ring(2, "tq"), Ring(2, "gt"), Ring(3, "hs")
        P.memset("pool", Y1[:], 0.0, writes=[r_Y])
        P.memset("pool", Y2[:], 0.0, writes=[r_Y2])
        ecnt = [0]

        def f1_evac(j, r, which, c4):
            P.cp("act", A[:, :, 0:33, which, c4 * 4:c4 * 4 + 4],
                 pb[:, j, 0:264].rearrange("p (c r k) -> p r k c", c=4, r=2), reads=[r], writes=[r_big])

        for g in range(4):
            ctile, hp = g // 2, 64 * (g % 2)
            c0 = 64 * g
            for cl in range(2):
                P.dma("sp", ZHs[cl * 64:(cl + 1) * 64, :, :],
                      T["hcT"][c0 + cl:c0 + 64:2, :].rearrange("c (a b) -> a c b", b=128), writes=[r_zh])
            for cl in range(4):
                P.dma("sp", ZZs[cl * 32:(cl + 1) * 32, :, :],
                      T["zT"][c0 + cl:c0 + 64:4, :].rearrange("c (a b) -> a c b", b=128), writes=[r_zz])
            for c4 in range(16):
                j, r = pb_r.next()
                for mm_ in range(2):
                    P.mm(pb[:, j, mm_ * 132:(mm_ + 1) * 132], ZHs[:, c4 * 2 + mm_, :], e1hb[:], reads=[r_zh, r_c],
                         writes=[r])
                f1_evac(j, r, 0, c4)
                bg_step()
            for c4 in range(16):
                j, r = pb_r.next()
                P.mm(pb[:, j, 0:264], ZZs[:, c4, :], e1zb[:], reads=[r_zz, r_c], writes=[r])
                f1_evac(j, r, 1, c4)
                bg_step()
            for kq in range(17):
                j, r = pb_r.next()
                bv = pb[:, j, :].rearrange("p (k r s c) -> p k r s c", k=2, r=2, s=2)
                nk = 2 if kq < 16 else 1
                for kk in range(nk):
                    ka = kq * 2 + kk
                    a_re = A[:, 0, ka, :, :].rearrange("p s c -> p (s c)")
                    a_im = A[:, 1, ka, :, :].rearrange("p s c -> p (s c)")
                    o_re = bv[:, kk, 0, :, :].rearrange("p s c -> p (s c)")
                    o_im = bv[:, kk, 1, :, :].rearrange("p s c -> p (s c)")
                    P.mm(o_re, mc[:, ka, :], a_re, start=True, stop=False, reads=[r_m, r_big], writes=[r])
                    P.mm(o_re, ms[:, ka, :], a_im, start=False, stop=True, reads=[r_m, r_big], writes=[r])
                    P.mm(o_im, mc[:, ka, :], a_im, start=True, stop=False, reads=[r_m, r_big], writes=[r])
                    P.mm(o_im, msn[:, ka, :], a_re, start=False, stop=True, reads=[r_m, r_big], writes=[r])
                hj, hr = hs_r.next()
                P.act(Hs[:, hj, 0:nk, :, :], bv[:, 0:nk, :, 0, :], AF.Copy, scale=1.0 / 8192.0, reads=[r], writes=[hr])
                tj, tr_ = tq_r.next()
                ks = slice(kq * 2, kq * 2 + nk)
                ta = tq[:, tj, 0, :].rearrange("p (k r c) -> p k r c", k=2, r=2)[:, 0:nk]
                tb = tq[:, tj, 1, :].rearrange("p (k r c) -> p k r c", k=2, r=2)[:, 0:nk]
                xz = bv[:, 0:nk, :, 1, :]
                P.tt("dve", ta, xz, Hs[:, hj, 0:nk, 0:1, :].broadcast_to([128, nk, 2, 64]), ALU.mult, reads=[r, hr],
                     writes=[tr_])
                P.tt("dve", tb, xz, Hs[:, hj, 0:nk, 1:2, :].broadcast_to([128, nk, 2, 64]), ALU.mult, reads=[r, hr],
                     writes=[tr_])
                P.tt("pool", Y1[:, :, 0, ks].rearrange("p c k -> p k c"), ta[:, :, 0, :], tb[:, :, 1, :], ALU.subtract,
                     reads=[tr_], writes=[r_Y])
                P.tt("pool", Y1[:, :, 1, ks].rearrange("p c k -> p k c"), tb[:, :, 0, :], ta[:, :, 1, :], ALU.add,
                     reads=[tr_], writes=[r_Y])
                P.tt("dve", Y2[:, :, 1, ks].rearrange("p c k -> p k c"), ta[:, :, 0, :], tb[:, :, 1, :], ALU.subtract,
                     reads=[tr_], writes=[r_Y2])
                P.stt(Y2[:, :, 0, ks].rearrange("p c k -> p k c"), tb[:, :, 0, :], -1.0, ta[:, :, 1, :], ALU.mult,
                      ALU.subtract, reads=[tr_], writes=[r_Y2])
                bg_step()
            while deferred:
                deferred.pop(0)()
            r_zg, r_x0g = r_zgh[g % 2], r_x0h[g % 2]
            P.dma("sp", zg[hp:hp + 64, :], T["zx"][0, c0:c0 + 64, :], writes=[r_zg])
            P.dma("sp", x0g[hp:hp + 64, :], T["zx"][1, c0:c0 + 64, :], writes=[r_x0g])
            for c4 in range(16):
                j, r = pb_r.next()
                for cc in range(4):
                    c = c4 * 4 + cc
                    o = pb[:, j, cc * 128:(cc + 1) * 128]
                    P.mm(o, Y1[:, c, :, :].rearrange("p r k -> p (r k)"), g1b[:, 0:128], start=True, stop=False,
                         reads=[r_Y, r_c], writes=[r])
                    P.mm(o, Y2[:, c, :, :].rearrange("p r k -> p (r k)"), g1b[:, 128:256], start=False, stop=True,
                         reads=[r_Y2, r_c], writes=[r])
                eng = "act"
                P.cp(eng, Bst[:, :, c4 * 4:c4 * 4 + 4], pb[:, j, :].rearrange("p (c n) -> p n c", c=4),
                     reads=[r], writes=[r_bst])
            zv = zg[hp:hp + 64, :].rearrange("p (a b) -> p b a", b=128)
            xv = x0g[hp:hp + 64, :].rearrange("p (a b) -> p b a", b=128)
            yv = yc[hp:hp + 64, ctile, :].rearrange("p (a b) -> p b a", b=128)
            bias = ppt[hp:hp + 64, PP["hbias"] + ctile:PP["hbias"] + ctile + 1]
            def i2_bank(nq, hp=hp, ctile=ctile, zv=zv, xv=xv, yv=yv, bias=bias, r_zg=r_zg, r_x0g=r_x0g):
                j, r = pb_r.next()
                for nn in range(16):
                    nb = nq * 16 + nn
                    o = pb[hp:hp + 64, j, nn * 32:(nn + 1) * 32]
                    P.mm(o, Bst[:, nb, :], irs[:, nb, :], start=True, stop=True, reads=[r_bst, r_m], writes=[r])
                gj, gr = gt_r.next()
                gv = gt[hp:hp + 64, gj, :].rearrange("p (b a) -> p b a", a=32)
                P.stt(gv, zv[:, nq * 16:(nq + 1) * 16, :], bias, pb[hp:hp + 64, j, :].rearrange("p (b a) -> p b a", a=32),
                      ALU.mult, ALU.add, reads=[r_zg, r_pp, r], writes=[gr])
                P.tt("dve", yv[:, nq * 16:(nq + 1) * 16, :], gv, xv[:, nq * 16:(nq + 1) * 16, :], ALU.mult,
                     reads=[gr, r_x0g], writes=[r_yc[ctile]])

            for nq in range(8):
                deferred.append(lambda nq=nq, f=i2_bank: f(nq))
        while deferred:
            deferred.pop(0)()
        r_sq, r_rst = Res("sq"), Res("rst")
        ob_r = Ring(2, "ob")
        for ch in range(8):
            sl = slice(ch * 512, (ch + 1) * 512)
            for ctile in range(2):
                P.act(sqb[:, ctile, :], yc[:, ctile, sl], AF.Square, reads=[r_yc[ctile]], writes=[r_sq])
            j, r = pb_r.next()
            for ctile in range(2):
                P.mm(pb[:, j, :], onb[:], sqb[:, ctile, :], start=(ctile == 0), stop=(ctile == 1), reads=[r_c, r_sq],
                     writes=[r])
            P.act(rst[:], pb[:, j, :], AF.Sqrt, bias=EPS, scale=1.0 / D_C, reads=[r], writes=[r_rst])
            P.recip(rst[:], rst[:], reads=[r_rst], writes=[r_rst])
            for ctile in range(2):
                oj, orr = ob_r.next()
                P.stt(ob[:, oj, :], yc[:, ctile, sl], ppt[:, PP["gnc"] + ctile:PP["gnc"] + ctile + 1], rst[:], ALU.mult,
                      ALU.mult, reads=[r_yc[ctile], r_rst, r_pp], writes=[orr])
                P.dma("pool", T["yT"][768 + ctile * 128:768 + (ctile + 1) * 128, sl], ob[:, oj, :], reads=[orr])
        P.barrier()
        P.emit_block()


SCRATCH = {"pA": ([768, L], F32), "pQK": ([512, L], F32), "pV": ([L, 128], F32), "pC": ([768, L], F32),
           "yT": ([D, L], BF16), "x1": ([L, D], F32), "h2T": ([D, L], BF16), "xs0": ([L, D], F32),
           "zT": ([D_C, L], BF16), "hcT": ([D_C, 2 * L], BF16), "zx": ([2, D_C, L], BF16),
           "mcb": ([128, 8192], BF16), "msb": ([128, 8192], BF16), "msnb": ([128, 8192], BF16),
           "irb": ([64, 4096], BF16), "iib": ([64, 4096], BF16)}
INPUTS = {"x": [L, D], "w_in": [DEPTH, D, D_IN], "w_out": [DEPTH, D, D], "w_up": [DEPTH, D, D_FF],
          "w_down": [DEPTH, D_FF, D], "pp": [DEPTH, 128, NPP], "gates": [DEPTH, 4, 3, 128, 128],
          "hy_w1": [DEPTH, 33, 64], "hy_w2": [DEPTH, 64, 64], "hy_w3": [DEPTH, 64, 512], "fng": [128, D],
          "cvec": [128, 4]}
CONST_SHAPES = {"ident": (128, 128), "ones": (128, 128), "rope": (16, L), "prot": (128, 128),
                "maskn": (128, 128), "maskp": (128, 128), "hyz": (33, L), "tpos": (1, 2 * L),
                "e1": (64, 128), "g1": (128, 256), "g2": (128, 256), "e1z": (128, 264), "e1h": (128, 132)}


def make_T(nc, need=None, ext_in=(), ext_out=(), wdepth=DEPTH):
    T = {}
    for n, s in list(INPUTS.items()) + list(CONST_SHAPES.items()):
        if need is None or n in need:
            s = list(s)
            if n in ("w_in", "w_out", "w_up", "w_down"):
                s[0] = wdepth
            T[n] = nc.dram_tensor(n, s, F32, kind="ExternalInput").ap()
    for n, (s, d) in SCRATCH.items():
        kind = "Internal"
        if n in ext_in:
            kind = "ExternalInput"
        if n in ext_out:
            kind = "ExternalOutput"
        T[n] = nc.dram_tensor(n, list(s), d, kind=kind).ap()
    T["out"] = nc.dram_tensor("out", [L, D], F32, kind="ExternalOutput").ap()
    return T


def build_program():
    nc = bass.Bass("TRN2", target_bir_lowering=False)
    T = make_T(nc)
    P = Prog(nc)
    with contextlib.ExitStack() as st:
        P.alloc_sems(st)
        for l in range(DEPTH):
            xsrc = T["x"] if l == 0 else T["xs0"]
            phase_inproj(P, nc, T, l, xsrc, with_prep=(l == 0))
            phase_lru(P, nc, T, l)
            phase_attn(P, nc, T, l)
            phase_hyena_a(P, nc, T, l)
            phase_hyena_b(P, nc, T, l)
            last = (l == DEPTH - 1)
            with contextlib.ExitStack() as wst:
                wu_p = wst.enter_context(nc.sbuf_tensor(_un("wuP"), [128, 8, D_FF], BF16))
                pre = {"wu": wu_p, "r_wuh": [[Res("wu") for _ in range(8)] for _ in range(4)]}
                phase_outproj(P, nc, T, l, xsrc, pre=pre)
                phase_mlp(P, nc, T, l, T["out"] if last else T["xs0"], last, pre=pre)
    return nc


def _perm_w_in(w):
    w = w.copy()
    w[:, :, 768:1152] = _perm_heads(w[:, :, 768:1152], 2)
    return np.ascontiguousarray(w)


def _perm_w_out(w):
    w = w.copy()
    w[:, 384:768, :] = _perm_heads(w[:, 384:768, :], 1)
    return np.ascontiguousarray(w)


def kernel(**inputs):
    I = {k: np.asarray(v) for k, v in inputs.items()}
    f32 = lambda a: np.ascontiguousarray(np.asarray(a, dtype=np.float32))
    shared = {
        "w_in": _perm_w_in(f32(I["w_in"])), "w_out": _perm_w_out(f32(I["w_out"])),
        "w_up": f32(I["w_up"]), "w_down": f32(I["w_down"]),
        "pp": np.stack([pack_params(I, l) for l in range(DEPTH)]),
        "gates": np.stack([pack_gates(I, l) for l in range(DEPTH)]),
        "hy_w1": f32(I["hy_w1"]), "hy_w2": f32(I["hy_w2"]), "hy_w3": f32(I["hy_w3"]),
        "fng": np.ascontiguousarray(np.broadcast_to(f32(I["final_norm_g"])[None, :], (128, D))),
    }
    shared.update(host_small_consts())
    x = f32(I["x"])
    nb = x.shape[0]
    n_cores = 8
    in_maps = []
    for c in range(n_cores):
        m = dict(shared)
        m["x"] = np.ascontiguousarray(x[c % nb])
        in_maps.append(m)
    nc = build_program()
    res = run_bass_kernel_spmd(nc, in_maps, core_ids=list(range(n_cores)))
    out = np.stack([np.asarray(res.results[b]["out"], dtype=np.float32) for b in range(nb)], 0)
    return out
```

```python
import contextlib
import math
import numpy as np
import concourse.bass as bass
import concourse.mybir as mybir
from concourse.bass_utils import run_bass_kernel_spmd

F32 = mybir.dt.float32
BF16 = mybir.dt.bfloat16
AF = mybir.ActivationFunctionType
ALU = mybir.AluOpType

D = 1024
L = 4096
DEPTH = 2
D_A = 384
D_B = 384
D_C = 256
D_IN = 2176
D_FF = 4096
EPS = 1e-6
NT = L // 128
ROPE_THETA = 500000.0
HY_BANDS = 16
HY_MAX_DECAY = math.log(1e-2) / 0.3
HY_MIN_DECAY = math.log(1e-2) / 1.5
MAGIC = 12582912.0
TWO_PI = 2.0 * math.pi

PP = {}
_c = 0
for _n, _w in [("g1", 8), ("g2", 8), ("caw", 12), ("cab", 3), ("ba", 6), ("bx", 6), ("lam", 6), ("gna", 3),
               ("gnb", 3), ("hcw", 18), ("hcb", 6), ("hbias", 2), ("gnc", 2), ("hb1", 1), ("hfr", 1), ("hb2", 1),
               ("sink", 6)]:
    PP[_n] = _c
    _c += _w
NPP = _c

ENGS = ("pe", "act", "dve", "pool", "sp")
NDSEM = 8


class Res:
    __slots__ = ("name", "writers", "readers")

    def __init__(self, name=""):
        self.name = name
        self.writers = {}
        self.readers = {}


class Prog:
    def __init__(self, nc):
        self.nc = nc
        self.ops = {e: [] for e in ENGS}
        self.cnt = {e: 0 for e in ENGS}
        self.waited = {e: {} for e in ENGS}
        self.dma_n = {e: 0 for e in ENGS}
        self.dsem_uses = {}
        self.sems = {}
        self.keys = list(ENGS) + [("d", q, i) for q in ("sp", "pool", "act") for i in range(NDSEM)]

    def alloc_sems(self, st):
        for k in self.keys:
            nm = k if isinstance(k, str) else "d_%s_%d" % (k[1], k[2])
            self.sems[k] = st.enter_context(self.nc.semaphore("s_" + nm))

    def _deps(self, eng, reads, writes):
        need = {}
        for r in reads:
            for k, v in r.writers.items():
                if need.get(k, 0) < v:
                    need[k] = v
        for w in writes:
            for k, v in w.writers.items():
                if need.get(k, 0) < v:
                    need[k] = v
            for k, v in w.readers.items():
                if need.get(k, 0) < v:
                    need[k] = v
        waits = []
        wd = self.waited[eng]
        for k, v in need.items():
            if eng == "pe" and k == "pe":
                continue
            if wd.get(k, 0) >= v:
                continue
            wd[k] = v
            waits.append((k, v))
        return waits

    def _mark(self, tok, reads, writes):
        k, v = tok
        for w in writes:
            w.writers = {k: v}
            w.readers = {}
        for r in reads:
            if r.readers.get(k, 0) < v:
                r.readers[k] = v

    def op(self, eng, fn, reads=(), writes=()):
        waits = self._deps(eng, reads, writes)
        self.cnt[eng] += 1
        tok = (eng, self.cnt[eng])
        self.ops[eng].append((waits, fn, tok, 1))
        self._mark(tok, reads, writes)
        return tok

    def dma(self, q, out, in_, reads=(), writes=()):
        n = self.dma_n[q]
        self.dma_n[q] += 1
        key = ("d", q, n % NDSEM)
        uses = self.dsem_uses.get(key, 0)
        waits = self._deps(q, reads, writes)
        if uses > 0 and self.waited[q].get(key, 0) < 16 * uses:
            self.waited[q][key] = 16 * uses
            waits.append((key, 16 * uses))
        self.dsem_uses[key] = uses + 1
        tok = (key, 16 * (uses + 1))
        self.ops[q].append((waits, (lambda e: e.dma_start(out=out, in_=in_)), tok, 16))
        self._mark(tok, reads, writes)
        return tok

    def barrier(self):
        cur = {e: self.cnt[e] for e in ENGS}
        for k, u in self.dsem_uses.items():
            cur[k] = 16 * u
        for e in ENGS:
            waits = []
            for k, v in cur.items():
                if k == e or v == 0:
                    continue
                if self.waited[e].get(k, 0) >= v:
                    continue
                self.waited[e][k] = v
                waits.append((k, v))
            self.ops[e].append((waits, None, None, 0))

    def emit_block(self):
        nc = self.nc
        sems = self.sems
        ops = self.ops

        def run(e, ename):
            for waits, fn, tok, inc in ops[ename]:
                for k, v in waits:
                    e.wait_ge(sems[k], v)
                if fn is not None:
                    fn(e).then_inc(sems[tok[0]], inc)

        with nc.Block() as block:
            @block.tensor
            def _(e):
                run(e, "pe")

            @block.scalar
            def _(e):
                run(e, "act")

            @block.vector
            def _(e):
                run(e, "dve")

            @block.gpsimd
            def _(e):
                run(e, "pool")

            @block.sync
            def _(e):
                run(e, "sp")
        self.ops = {e: [] for e in ENGS}

    def mm(self, out, lhsT, rhs, start=True, stop=True, reads=(), writes=()):
        return self.op("pe", lambda e: e.matmul(out, lhsT, rhs, start=start, stop=stop), reads, writes)

    def tr(self, out, in_, ident, reads=(), writes=()):
        return self.op("pe", lambda e: e.transpose(out, in_, ident), reads, writes)

    def act(self, out, in_, func, bias=None, scale=None, accum_out=None, reads=(), writes=()):
        kw = {}
        if bias is not None:
            kw["bias"] = bias
        if scale is not None:
            kw["scale"] = scale
        if accum_out is not None:
            kw["accum_out"] = accum_out
        return self.op("act", lambda e: e.activation(out=out, in_=in_, func=func, **kw), reads, writes)

    def ts(self, eng, out, in0, s1, s2, op0, op1=None, reads=(), writes=()):
        if op1 is None:
            return self.op(eng, lambda e: e.tensor_scalar(out=out, in0=in0, scalar1=s1, scalar2=None, op0=op0),
                           reads, writes)
        return self.op(eng, lambda e: e.tensor_scalar(out=out, in0=in0, scalar1=s1, scalar2=s2, op0=op0, op1=op1),
                       reads, writes)

    def tt(self, eng, out, in0, in1, op, reads=(), writes=()):
        return self.op(eng, lambda e: e.tensor_tensor(out=out, in0=in0, in1=in1, op=op), reads, writes)

    def stt(self, out, in0, scalar, in1, op0, op1, reads=(), writes=(), accum_out=None):
        if accum_out is not None:
            return self.op("dve", lambda e: e.scalar_tensor_tensor(out=out, in0=in0, scalar=scalar, in1=in1,
                                                                    op0=op0, op1=op1, accum_out=accum_out),
                           reads, writes)
        return self.op("dve", lambda e: e.scalar_tensor_tensor(out=out, in0=in0, scalar=scalar, in1=in1,
                                                                op0=op0, op1=op1), reads, writes)

    def cp(self, eng, out, in_, reads=(), writes=()):
        if eng == "act":
            return self.act(out, in_, AF.Copy, reads=reads, writes=writes)
        return self.op(eng, lambda e: e.tensor_copy(out=out, in_=in_), reads, writes)

    def memset(self, eng, ap, val, writes=()):
        return self.op(eng, lambda e: e.memset(ap, val), (), writes)

    def recip(self, out, in_, reads=(), writes=()):
        return self.op("dve", lambda e: e.reciprocal(out=out, in_=in_), reads, writes)

    def scan(self, out, d0, d1, reads=(), writes=()):
        return self.op("dve", lambda e: e.tensor_tensor_scan(out=out, data0=d0, data1=d1, initial=0.0,
                                                              op0=ALU.mult, op1=ALU.add), reads, writes)


_UID = [0]


def _un(n):
    _UID[0] += 1
    return "%s_%d" % (n, _UID[0])


def _sb(nc, st):
    return lambda n, s, d: st.enter_context(nc.sbuf_tensor(_un(n), s, d))


class Ring:
    def __init__(self, n, name):
        self.n = n
        self.i = 0
        self.res = [Res("%s%d" % (name, j)) for j in range(n)]

    def next(self):
        j = self.i % self.n
        self.i += 1
        return j, self.res[j]


def host_consts():
    C = {}
    C["ident"] = np.eye(128, dtype=np.float32)
    C["ones"] = np.ones((128, 128), np.float32)
    pos = np.arange(L, dtype=np.float32)
    inv = (np.float32(ROPE_THETA) ** (-np.arange(0, 16, 2, dtype=np.float32) / np.float32(16))).astype(np.float32)
    ang = (pos[:, None] * inv[None, :]).astype(np.float32).astype(np.float64)
    cos, sin = np.cos(ang).T, np.sin(ang).T
    ct = np.ones((128, L), np.float64)
    stb = np.zeros((128, L), np.float64)
    prot = np.zeros((128, 128), np.float32)
    for r in range(128):
        d = r % 64
        if d < 8:
            ct[r] = cos[d]
            stb[r] = -sin[d]
            prot[r + 8, r] = 1.0
        elif d < 16:
            ct[r] = cos[d - 8]
            stb[r] = sin[d - 8]
            prot[r - 8, r] = 1.0
    C["ropec"] = ct.astype(np.float32)
    C["ropes"] = stb.astype(np.float32)
    C["prot"] = prot
    j = np.arange(128)[:, None]
    q = np.arange(128)[None, :]
    C["maskn"] = (j <= q).astype(np.float32)
    C["maskp"] = (j >= q).astype(np.float32)
    t = np.linspace(0.0, 1.0, L)
    w = TWO_PI * np.arange(L) / L
    f = np.linspace(1e-4, HY_BANDS - 1, HY_BANDS)
    z = np.concatenate([t[None, :], np.cos(f[:, None] * w[None, :]), -np.sin(f[:, None] * w[None, :])], 0)
    deltas = np.abs(np.linspace(HY_MIN_DECAY, HY_MAX_DECAY, D_C))
    decay = np.exp(-t[None, :] * deltas[:, None])
    idx = (L - np.arange(L)) % L
    z2 = np.concatenate([z, z[:, idx]], 1)
    dec2 = np.concatenate([decay, decay[:, idx]], 1)
    dec2[:, L] = 0.0
    C["hyz"] = z2.astype(np.float32)
    C["hydec"] = dec2.astype(np.float32)
    N = 2 * L
    na = np.arange(64)[:, None]
    ka = np.arange(64)[None, :]
    a1 = TWO_PI * na * ka / 64.0
    C["e1"] = np.concatenate([np.cos(a1), -np.sin(a1)], 1).astype(np.float32)
    nb = np.arange(128)[:, None, None]
    kk = (np.arange(64)[None, :, None] + 64 * np.arange(128)[None, None, :])
    a2 = TWO_PI * ((nb * kk) % N) / N
    C["mc"] = np.cos(a2).reshape(128, 64 * 128).astype(np.float32)
    C["ms"] = np.sin(a2).reshape(128, 64 * 128).astype(np.float32)
    kb = np.arange(128)[:, None]
    nb2 = np.arange(128)[None, :]
    a3 = TWO_PI * ((kb * nb2) % 128) / 128.0
    C["g1"] = np.concatenate([np.cos(a3), np.sin(a3)], 1).astype(np.float32)
    C["g2"] = np.concatenate([-np.sin(a3), np.cos(a3)], 1).astype(np.float32)
    ka3 = np.arange(64)[:, None, None]
    nn = 128 * np.arange(32)[None, None, :] + np.arange(128)[None, :, None]
    a4 = TWO_PI * ((ka3 * nn) % N) / N
    C["ir"] = (np.cos(a4) / N).reshape(64, 128 * 32).astype(np.float32)
    C["ii"] = (-np.sin(a4) / N).reshape(64, 128 * 32).astype(np.float32)
    return C


def host_small_consts():
    C = host_consts()
    out = {k: C[k] for k in ("ident", "ones", "prot", "maskn", "maskp", "e1", "g1", "g2")}
    e1 = C["e1"]
    e1r = e1.reshape(64, 2, 64)[:, :, 0:33]
    e1z = np.zeros((128, 4, 2, 33), np.float32)
    for cl in range(4):
        e1z[cl * 32:(cl + 1) * 32, cl] = e1r[0:32]
    e1h = np.zeros((128, 2, 2, 33), np.float32)
    for cl in range(2):
        e1h[cl * 64:(cl + 1) * 64, cl] = e1r
    out["e1z"] = e1z.reshape(128, 264)
    out["e1h"] = e1h.reshape(128, 132)
    pos = np.arange(L, dtype=np.float32)
    inv = (np.float32(ROPE_THETA) ** (-np.arange(0, 16, 2, dtype=np.float32) / np.float32(16))).astype(np.float32)
    ang = (pos[:, None] * inv[None, :]).astype(np.float32).astype(np.float64)
    out["rope"] = np.concatenate([np.cos(ang).T, np.sin(ang).T], 0).astype(np.float32)
    out["hyz"] = C["hyz"][:, :L].copy()
    t = np.linspace(0.0, 1.0, L)
    idx = (L - np.arange(L)) % L
    tp = np.concatenate([t, t[idx]])
    tp[L] = 1.0e4
    out["tpos"] = tp[None, :].astype(np.float32)
    deltas = np.abs(np.linspace(HY_MIN_DECAY, HY_MAX_DECAY, D_C))
    cv = np.zeros((128, 4), np.float32)
    cv[:, 0] = -deltas[:128]
    cv[:, 1] = -deltas[128:]
    cv[0:33, 2] = 2.0
    cv[0, 2] = 1.0
    cv[32, 2] = 1.0
    out["cvec"] = cv
    return out


HEAD_PERM = [0, 3, 1, 4, 2, 5]


def _perm_heads(v, axis):
    v = np.asarray(v)
    idx = np.concatenate([np.arange(64) + 64 * h for h in HEAD_PERM])
    return np.take(v, idx, axis=axis)


def pack_params(I, l):
    pp = np.zeros((128, NPP), np.float32)

    def cols(name, vec, n):
        pp[:, PP[name]:PP[name] + n] = np.asarray(vec, np.float32).reshape(n, 128).T

    cols("g1", I["norm_mix_g"][l], 8)
    cols("g2", I["norm_mlp_g"][l], 8)
    for k in range(4):
        pp[:, PP["caw"] + 3 * k:PP["caw"] + 3 * k + 3] = np.asarray(I["conv_a_w"][l][k]).reshape(3, 128).T
    cols("cab", I["conv_a_b"][l], 3)
    for d in range(2):
        pp[:, PP["ba"] + 3 * d:PP["ba"] + 3 * d + 3] = np.asarray(I["lru_ba"][l][d]).reshape(3, 128).T
        pp[:, PP["bx"] + 3 * d:PP["bx"] + 3 * d + 3] = np.asarray(I["lru_bx"][l][d]).reshape(3, 128).T
        pp[:, PP["lam"] + 3 * d:PP["lam"] + 3 * d + 3] = np.asarray(I["lru_lambda"][l][d]).reshape(3, 128).T
    cols("gna", I["gnorm_a"][l], 3)
    cols("gnb", _perm_heads(I["gnorm_b"][l], 0), 3)
    for k in range(3):
        pp[:, PP["hcw"] + 6 * k:PP["hcw"] + 6 * k + 6] = np.asarray(I["hy_conv_w"][l][k]).reshape(6, 128).T
    cols("hcb", I["hy_conv_b"][l], 6)
    cols("hbias", I["hy_bias"][l], 2)
    cols("gnc", I["gnorm_c"][l], 2)
    pp[:64, PP["hb1"]] = I["hy_b1"][l]
    pp[:64, PP["hfr"]] = I["hy_freq"][l]
    pp[:64, PP["hb2"]] = I["hy_b2"][l]
    pp[:, PP["sink"]:PP["sink"] + 6] = np.asarray(I["attn_sink"][l])[HEAD_PERM][None, :]
    return pp


def pack_gates(I, l):
    g = np.zeros((4, 3, 128, 128), np.float32)
    for d in range(2):
        for wi, nm in enumerate(("lru_wa", "lru_wx")):
            W = np.asarray(I[nm][l][d])
            for blk in range(6):
                t, o = blk // 2, 64 * (blk % 2)
                g[2 * d + wi, t, o:o + 64, o:o + 64] = W[blk]
    return g


def phase_inproj(P, nc, T, l, xsrc, with_prep=False):
    with contextlib.ExitStack() as st:
        sb = _sb(nc, st)
        wi = sb("wi", [128, 8, D_IN], BF16)
        stg = sb("stg", [128, 2, D_IN], F32)
        ppt = sb("ppt", [128, NPP], F32)
        idf = sb("idf", [128, 128], F32)
        idb = sb("idb", [128, 128], BF16)
        xt = sb("xt", [128, 3, D], F32)
        sq2 = sb("sq", [128, 2, D], BF16)
        xn3 = sb("xn", [128, 3, D], BF16)
        hT = sb("hT", [128, 2, 8, 512], BF16)
        sm4 = sb("sm", [128, 4, 8], F32)
        ost = sb("ost", [128, 4, 512], F32)
        vst = sb("vst", [128, 2, 4, 128], F32)
        ptr = st.enter_context(nc.psum_tensor(_un("ptr"), [128, 2, 1024], BF16))
        pmm = st.enter_context(nc.psum_tensor(_un("pmm"), [128, 5, 512], F32))
        r_pp, r_id, r_wi = Res("pp"), Res("id"), Res("wi")
        P.dma("sp", ppt[:], T["pp"][l], writes=[r_pp])
        P.dma("sp", idf[:], T["ident"], writes=[r_id])
        P.cp("dve", idb[:], idf[:], reads=[r_id], writes=[r_id])
        stg_r = Ring(2, "stg")
        r_wik = [Res("wi%d" % i) for i in range(8)]
        wv = T["w_in"][l].rearrange("(kt p) n -> p kt n", p=128)
        for kt in range(8):
            j, r = stg_r.next()
            P.dma("sp", stg[:, j, :], wv[:, kt, :], writes=[r])
            if kt % 2:
                P.act(wi[:, kt, :], stg[:, j, :], AF.Copy, scale=ppt[:, PP["g1"] + kt:PP["g1"] + kt + 1],
                      reads=[r, r_pp], writes=[r_wik[kt]])
            else:
                P.ts("dve", wi[:, kt, :], stg[:, j, :], ppt[:, PP["g1"] + kt:PP["g1"] + kt + 1],
                     None, ALU.mult, reads=[r, r_pp], writes=[r_wik[kt]])
        xt_r, xn_r, hT_r, ptr_r, pmm_r = Ring(3, "xt"), Ring(3, "xn"), Ring(2, "hT"), Ring(2, "ptr"), Ring(5, "pmm")
        ost_r, vst_r = Ring(4, "ost"), Ring(2, "vst")
        sm_r = [Res("sm%d" % i) for i in range(4)]
        sq_r = Ring(2, "sq")
        xv = xsrc.rearrange("(n p) d -> n p d", p=128)
        st8 = {}
        hslot = {}
        hTres = [[Res("hT%d_%d" % (a_, b_)) for b_ in range(4)] for a_ in range(2)]
        cnt = [0]

        def s1(n):
            xj, xr = xt_r.next()
            P.dma("sp", xt[:, xj, :], xv[n], writes=[xr])
            k = n % 4
            smv, smr = sm4[:, k, :], sm_r[k]
            qj, qr = sq_r.next()
            P.act(sq2[:, qj, :], xt[:, xj, :], AF.Square, accum_out=smv[:, 0:1], reads=[xr], writes=[qr, smr])
            P.ts("dve", smv[:, 1:2], smv[:, 0:1], 1.0 / D, EPS, ALU.mult, ALU.add, reads=[smr], writes=[smr])
            P.act(smv[:, 2:3], smv[:, 1:2], AF.Sqrt, reads=[smr], writes=[smr])
            P.recip(smv[:, 3:4], smv[:, 2:3], reads=[smr], writes=[smr])
            nj, nr = xn_r.next()
            P.act(xn3[:, nj, :], xt[:, xj, :], AF.Copy, scale=smv[:, 3:4], reads=[xr, smr], writes=[nr])
            st8[n] = (nj, nr)

        def s2(n):
            nj, nr = st8.pop(n)
            ch, tt = n // 4, n % 4
            if tt == 0:
                hslot[ch] = hT_r.next()
            hj, hr = hslot[ch]
            hr = hTres[hj][tt]
            pj, pr = ptr_r.next()
            for kt in range(8):
                P.tr(ptr[:, pj, kt * 128:(kt + 1) * 128], xn3[:, nj, kt * 128:(kt + 1) * 128], idb[:],
                     reads=[nr, r_id], writes=[pr])
            P.cp("act" if tt % 2 else "dve", hT[:, hj, :, tt * 128:(tt + 1) * 128],
                 ptr[:, pj, :].rearrange("p (k t) -> p k t", k=8), reads=[pr], writes=[hr])

        def mgroup(ch, m):
            hj, hr = hslot[ch]
            hrs = hTres[hj]
            c0 = ch * 512
            if m == 10:
                vj, vr = vst_r.next()
                mj, mr = pmm_r.next()
                for tt in range(4):
                    for kt in range(8):
                        P.mm(pmm[:, mj, tt * 128:(tt + 1) * 128], hT[:, hj, kt, tt * 128:(tt + 1) * 128],
                             wi[:, kt, 1280:1408], start=(kt == 0), stop=(kt == 7), reads=[hrs[tt], r_wik[kt]],
                             writes=[mr])
                P.cp("act", vst[:, vj, :, :], pmm[:, mj, :].rearrange("p (t c) -> p t c", t=4),
                     reads=[mr], writes=[vr])
                P.dma("pool", T["pV"][c0:c0 + 512, :].rearrange("(t p) c -> p t c", p=128), vst[:, vj, :, :],
                      reads=[vr])
                return
            mj, mr = pmm_r.next()
            for kt in range(8):
                P.mm(pmm[:, mj, :], wi[:, kt, m * 128:(m + 1) * 128], hT[:, hj, kt, :], start=(kt == 0),
                     stop=(kt == 7), reads=hrs + [r_wik[kt]], writes=[mr])
            oj, orr = ost_r.next()
            P.cp("act" if cnt[0] % 2 else "dve", ost[:, oj, :], pmm[:, mj, :], reads=[mr], writes=[orr])
            cnt[0] += 1
            if m < 6:
                dst = T["pA"][m * 128:(m + 1) * 128, c0:c0 + 512]
            elif m < 10:
                dst = T["pQK"][(m - 6) * 128:(m - 5) * 128, c0:c0 + 512]
            else:
                dst = T["pC"][(m - 11) * 128:(m - 10) * 128, c0:c0 + 512]
            P.dma("pool", dst, ost[:, oj, :], reads=[orr])

        pending = []
        bgq = prep_ops(P, nc, T, sb) if with_prep else []
        for step in range(NT + 2):
            for _ in range(3):
                if bgq:
                    bgq.pop(0)()
            if step < NT:
                s1(step)
            if 0 <= step - 1 < NT:
                s2(step - 1)
                if (step - 1) % 4 == 3:
                    pending.extend([((step - 1) // 4, m) for m in range(17)])
            for _ in range(5):
                if pending:
                    mgroup(*pending.pop(0))
        while pending:
            mgroup(*pending.pop(0))
        while bgq:
            bgq.pop(0)()
        P.barrier()
        P.emit_block()


def phase_outproj(P, nc, T, l, xsrc, pre=None):
    with contextlib.ExitStack() as st:
        sb = _sb(nc, st)
        wo = sb("wo", [128, 8, D], BF16)
        stg = sb("stg", [128, 2, D], F32)
        idf = sb("idf", [128, 128], F32)
        idb = sb("idb", [128, 128], BF16)
        yc = sb("yc", [128, 2, 8, 512], BF16)
        xt = sb("xt", [128, 3, D], F32)
        x1 = sb("x1", [128, 3, D], F32)
        sq2 = sb("sq", [128, 2, D], BF16)
        xn3 = sb("xn", [128, 3, D], BF16)
        h2 = sb("h2", [128, 2, 8, 512], BF16)
        sm4 = sb("sm", [128, 4, 8], F32)
        pmm = st.enter_context(nc.psum_tensor(_un("pmm"), [128, 2, 1024], F32))
        ptr = st.enter_context(nc.psum_tensor(_un("ptr"), [128, 2, 1024], BF16))
        r_id, r_wo = Res("id"), Res("wo")
        P.dma("sp", idf[:], T["ident"], writes=[r_id])
        P.cp("dve", idb[:], idf[:], reads=[r_id], writes=[r_id])
        stg_r = Ring(2, "stg")
        r_wok = [Res("wo%d" % i) for i in range(8)]
        wv = T["w_out"][l].rearrange("(kt p) n -> p kt n", p=128)
        for kt in range(8):
            j, r = stg_r.next()
            P.dma("sp", stg[:, j, :], wv[:, kt, :], writes=[r])
            P.cp("act" if kt % 2 else "dve", wo[:, kt, :], stg[:, j, :], reads=[r], writes=[r_wok[kt]])
        yc_r, xt_r, x1_r, xn_r, h2_r = Ring(2, "yc"), Ring(3, "xt"), Ring(3, "x1"), Ring(3, "xn"), Ring(2, "h2")
        pmm_r, ptr_r = Ring(2, "pmm"), Ring(2, "ptr")
        sm_r = [Res("sm%d" % i) for i in range(4)]
        sq_r = Ring(2, "sq")
        xv = xsrc.rearrange("(n p) d -> n p d", p=128)
        x1v = T["x1"].rearrange("(n p) d -> n p d", p=128)
        yv = T["yT"].rearrange("(kt p) t -> p kt t", p=128)
        hv = T["h2T"].rearrange("(kt p) t -> p kt t", p=128)
        ycs, h2s, sA, sB = {}, {}, {}, {}

        def s1(n):
            ch, tt = n // 4, n % 4
            if tt == 0:
                yj, yr = yc_r.next()
                P.dma("sp", yc[:, yj, :, :], yv[:, :, ch * 512:(ch + 1) * 512], writes=[yr])
                ycs[ch] = (yj, yr)
            yj, yr = ycs[ch]
            xj, xr = xt_r.next()
            P.dma("sp", xt[:, xj, :], xv[n], writes=[xr])
            mj, mr = pmm_r.next()
            for half in range(2):
                for kt in range(8):
                    P.mm(pmm[:, mj, half * 512:(half + 1) * 512], yc[:, yj, kt, tt * 128:(tt + 1) * 128],
                         wo[:, kt, half * 512:(half + 1) * 512], start=(kt == 0), stop=(kt == 7),
                         reads=[yr, r_wok[kt]], writes=[mr])
            sA[n] = (xj, xr, mj, mr)

        def s2(n):
            xj, xr, mj, mr = sA.pop(n)
            oj, orr = x1_r.next()
            P.tt("dve", x1[:, oj, :], xt[:, xj, :], pmm[:, mj, :], ALU.add, reads=[xr, mr], writes=[orr])
            P.dma("pool", x1v[n], x1[:, oj, :], reads=[orr])
            k = n % 4
            smv, smr = sm4[:, k, :], sm_r[k]
            qj, qr = sq_r.next()
            P.act(sq2[:, qj, :], x1[:, oj, :], AF.Square, accum_out=smv[:, 0:1], reads=[orr], writes=[qr, smr])
            P.ts("dve", smv[:, 1:2], smv[:, 0:1], 1.0 / D, EPS, ALU.mult, ALU.add, reads=[smr], writes=[smr])
            P.act(smv[:, 2:3], smv[:, 1:2], AF.Sqrt, reads=[smr], writes=[smr])
            P.recip(smv[:, 3:4], smv[:, 2:3], reads=[smr], writes=[smr])
            nj, nr = xn_r.next()
            P.act(xn3[:, nj, :], x1[:, oj, :], AF.Copy, scale=smv[:, 3:4], reads=[orr, smr], writes=[nr])
            sB[n] = (nj, nr)

        def s3(n):
            nj, nr = sB.pop(n)
            ch, tt = n // 4, n % 4
            if tt == 0:
                h2s[ch] = h2_r.next()
            hj, hr = h2s[ch]
            pj, pr = ptr_r.next()
            for kt in range(8):
                P.tr(ptr[:, pj, kt * 128:(kt + 1) * 128], xn3[:, nj, kt * 128:(kt + 1) * 128], idb[:],
                     reads=[nr, r_id], writes=[pr])
            P.cp("act", h2[:, hj, :, tt * 128:(tt + 1) * 128],
                 ptr[:, pj, :].rearrange("p (k t) -> p k t", k=8), reads=[pr], writes=[hr])
            if tt == 3:
                P.dma("pool", hv[:, :, ch * 512:(ch + 1) * 512], h2[:, hj, :, :], reads=[hr])

        bgq = []
        if pre is not None:
            stgw = sb("stgw", [128, 4, 1024], F32)
            pp2 = sb("pp2", [128, NPP], F32)
            r_pp2 = Res("pp2")
            P.dma("sp", pp2[:], T["pp"][l], writes=[r_pp2])
            sw_r = Ring(4, "stgw")
            uv = T["w_up"][l].rearrange("(kt p) n -> p kt n", p=128)
            wu_p, r_wuh = pre["wu"], pre["r_wuh"]

            def wu_load(q4, kt, k):
                def f():
                    j, r = sw_r.next()
                    P.dma("sp", stgw[:, j, :], uv[:, kt, q4 * 1024:(q4 + 1) * 1024], writes=[r])
                    gcol = pp2[:, PP["g2"] + kt:PP["g2"] + kt + 1]
                    if k % 2:
                        P.act(wu_p[:, kt, q4 * 1024:(q4 + 1) * 1024], stgw[:, j, :], AF.Copy, scale=gcol,
                              reads=[r, r_pp2], writes=[r_wuh[q4][kt]])
                    else:
                        P.ts("dve", wu_p[:, kt, q4 * 1024:(q4 + 1) * 1024], stgw[:, j, :], gcol, None, ALU.mult,
                             reads=[r, r_pp2], writes=[r_wuh[q4][kt]])
                return f

            k = 0
            for q4 in range(4):
                for kt in range(8):
                    bgq.append(wu_load(q4, kt, k))
                    k += 1
        for step in range(NT + 2):
            if step < NT:
                s1(step)
            if 0 <= step - 1 < NT:
                s2(step - 1)
            if 0 <= step - 2 < NT:
                s3(step - 2)
            if bgq:
                bgq.pop(0)()
        while bgq:
            bgq.pop(0)()
        P.barrier()
        P.emit_block()


def phase_mlp(P, nc, T, l, xdst, last, pre=None):
    CH = 512
    with contextlib.ExitStack() as st:
        sb = _sb(nc, st)
        wu = pre["wu"] if pre is not None else sb("wu", [128, 8, D_FF], BF16)
        wd = sb("wd", [128, 32, D], BF16)
        stg = sb("stg", [128, 2, 1024], F32)
        ppt = sb("ppt", [128, NPP], F32)
        hc = sb("hc", [128, 2, 8, CH], BF16)
        aT = sb("aT", [128, 32, CH], BF16)
        rr = sb("rr", [128, 2, CH], F32)
        x1 = sb("x1", [128, 2, D], F32)
        sm = sb("sm", [128, 8], F32)
        if last:
            fg = sb("fg", [128, D], F32)
            sq = sb("sq", [128, D], BF16)
        pup = st.enter_context(nc.psum_tensor(_un("pup"), [128, 4, 512], F32))
        pdn = st.enter_context(nc.psum_tensor(_un("pdn"), [128, 2, 1024], F32))
        r_pp, r_wu, r_wd, r_fg = Res("pp"), Res("wu"), Res("wd"), Res("fg")
        P.dma("sp", ppt[:], T["pp"][l], writes=[r_pp])
        if last:
            P.dma("sp", fg[:], T["fng"], writes=[r_fg])
        stg_r = Ring(2, "stg")
        hc_r = Ring(2, "hc")
        hv = T["h2T"].rearrange("(kt p) t -> p kt t", p=128)
        hc0 = hc_r.next()
        P.dma("sp", hc[:, hc0[0], :, :], hv[:, :, 0:CH], writes=[hc0[1]])
        uv = T["w_up"][l].rearrange("(kt p) n -> p kt n", p=128)
        r_wuh = pre["r_wuh"] if pre is not None else [[Res("wu") for _ in range(8)] for _ in range(4)]
        r_wdf = [Res("wd%d" % i) for i in range(32)]
        bgq = []

        def wu_load(q4, kt, k):
            def f():
                j, r = stg_r.next()
                P.dma("sp", stg[:, j, :], uv[:, kt, q4 * 1024:(q4 + 1) * 1024], writes=[r])
                if k % 2:
                    P.act(wu[:, kt, q4 * 1024:(q4 + 1) * 1024], stg[:, j, :], AF.Copy,
                          scale=ppt[:, PP["g2"] + kt:PP["g2"] + kt + 1], reads=[r, r_pp], writes=[r_wuh[q4][kt]])
                else:
                    P.ts("dve", wu[:, kt, q4 * 1024:(q4 + 1) * 1024], stg[:, j, :],
                         ppt[:, PP["g2"] + kt:PP["g2"] + kt + 1], None, ALU.mult, reads=[r, r_pp],
                         writes=[r_wuh[q4][kt]])
            return f

        dv = T["w_down"][l].rearrange("(ft p) n -> p ft n", p=128)

        def wd_load(f2):
            def f():
                j, r = stg_r.next()
                P.dma("sp", stg[:, j, :], dv[:, f2, :], writes=[r])
                P.cp("act" if f2 % 2 else "dve", wd[:, f2, :], stg[:, j, :], reads=[r], writes=[r_wdf[f2]])
            return f

        k = 0
        for q4 in range(4):
            for kt in range(8):
                if pre is None:
                    if q4 < 2:
                        wu_load(q4, kt, k)()
                    else:
                        bgq.append(wu_load(q4, kt, k))
                k += 1
        for f2 in range(32):
            bgq.append(wd_load(f2))
        rr_r, x1_r, pup_r, pdn_r = (Ring(2, "rr"), Ring(2, "x1"), Ring(4, "pup"), Ring(2, "pdn"))
        r_aT, sm_r, r_sq = Res("aT"), Res("sm"), Res("sq")
        x1v = T["x1"].rearrange("(n p) d -> n p d", p=128)
        xov = xdst.rearrange("(n p) d -> n p d", p=128)
        for ch in range(L // CH):
            c0 = ch * CH
            if ch == 0:
                hj, hr = hc0
            else:
                hj, hr = hc_r.next()
                P.dma("sp", hc[:, hj, :, :], hv[:, :, c0:c0 + CH], writes=[hr])
            for ft in range(32):
                uj, ur = pup_r.next()
                for kt in range(8):
                    P.mm(pup[:, uj, 0:CH], wu[:, kt, ft * 128:(ft + 1) * 128], hc[:, hj, kt, :], start=(kt == 0),
                         stop=(kt == 7), reads=[hr, r_wuh[ft // 8][kt]], writes=[ur])
                rj, rres = rr_r.next()
                P.act(rr[:, rj, :], pup[:, uj, 0:CH], AF.Relu, reads=[ur], writes=[rres])
                P.tt("pool" if ft % 4 == 3 else "dve", aT[:, ft, :], rr[:, rj, :], rr[:, rj, :], ALU.mult,
                     reads=[rres], writes=[r_aT])
                for _ in range((1 if ft < 16 else 2) if pre is None else 1):
                    if bgq:
                        bgq.pop(0)()
            while bgq:
                bgq.pop(0)()
            for tt in range(CH // 128):
                n = ch * (CH // 128) + tt
                xj, xr = x1_r.next()
                P.dma("sp", x1[:, xj, :], x1v[n], writes=[xr])
                dj, dr = pdn_r.next()
                for half in range(2):
                    for ft in range(32):
                        P.mm(pdn[:, dj, half * 512:(half + 1) * 512], aT[:, ft, tt * 128:(tt + 1) * 128],
                             wd[:, ft, half * 512:(half + 1) * 512], start=(ft == 0), stop=(ft == 31),
                             reads=[r_aT, r_wdf[ft]], writes=[dr])
                P.tt("dve", x1[:, xj, :], x1[:, xj, :], pdn[:, dj, :], ALU.add, reads=[dr], writes=[xr])
                if last:
                    P.act(sq[:], x1[:, xj, :], AF.Square, accum_out=sm[:, 0:1], reads=[xr], writes=[r_sq, sm_r])
                    P.ts("dve", sm[:, 1:2], sm[:, 0:1], 1.0 / D, EPS, ALU.mult, ALU.add, reads=[sm_r], writes=[sm_r])
                    P.act(sm[:, 2:3], sm[:, 1:2], AF.Sqrt, reads=[sm_r], writes=[sm_r])
                    P.recip(sm[:, 3:4], sm[:, 2:3], reads=[sm_r], writes=[sm_r])
                    P.stt(x1[:, xj, :], x1[:, xj, :], sm[:, 3:4], fg[:], ALU.mult, ALU.mult,
                          reads=[sm_r, r_fg], writes=[xr])
                P.dma("pool", xov[n], x1[:, xj, :], reads=[xr])
        P.barrier()
        P.emit_block()


def dve_mod8192(P, X, Tm, res):
    P.ts("dve", Tm, X, 1.0 / 8192.0, -0.49999, ALU.mult, ALU.add, reads=[res], writes=[res])
    P.ts("dve", Tm, Tm, MAGIC, None, ALU.add, reads=[res], writes=[res])
    P.ts("dve", Tm, Tm, -MAGIC, -8192.0, ALU.add, ALU.mult, reads=[res], writes=[res])
    P.tt("dve", X, X, Tm, ALU.add, reads=[res], writes=[res])


def prep_ops(P, nc, T, sb):
    I32 = mybir.dt.int32
    W = 2048
    ki = sb("pki", [128, W], I32)
    X = sb("pX", [128, W], F32)
    Tm = sb("pTm", [128, W], F32)
    Yc = sb("pY", [128, W], F32)
    ob = sb("pob", [128, 3, W], BF16)
    pi = sb("ppi", [128, 1], I32)
    pf = sb("ppf", [128, 1], F32)
    pf_pi = sb("ppfpi", [128, 1], F32)
    pf_npi = sb("ppfnpi", [128, 1], F32)
    r = Res("prep")
    sc = TWO_PI / 8192.0
    ops = []
    A = ops.append
    A(lambda: P.memset("dve", pf_pi[:], math.pi, writes=[r]))
    A(lambda: P.memset("dve", pf_npi[:], -math.pi, writes=[r]))
    A(lambda: P.op("pool", lambda e: e.iota(pi[:], pattern=[[0, 1]], base=0, channel_multiplier=1), writes=[r]))
    A(lambda: P.cp("dve", pf[:], pi[:], reads=[r], writes=[r]))

    def gen(npart, pattern, base, dsts, col0, want, rowscale=None):
        x, t, y = X[:npart, :], Tm[:npart, :], Yc[:npart, :]
        A(lambda: P.op("pool", lambda e: e.iota(ki[:npart, :], pattern=pattern, base=base, channel_multiplier=0),
                       writes=[r]))
        A(lambda: P.cp("dve", x, ki[:npart, :], reads=[r], writes=[r]))
        A(lambda: P.ts("dve", x, x, pf[:npart, :], None, ALU.mult, reads=[r], writes=[r]))
        A(lambda: P.ts("dve", t, x, 1.0 / 8192.0, -0.49999, ALU.mult, ALU.add, reads=[r], writes=[r]))
        A(lambda: P.ts("dve", t, t, MAGIC, None, ALU.add, reads=[r], writes=[r]))
        A(lambda: P.ts("dve", t, t, -MAGIC, -8192.0, ALU.add, ALU.mult, reads=[r], writes=[r]))
        A(lambda: P.tt("dve", x, x, t, ALU.add, reads=[r], writes=[r]))
        if want[1]:
            A(lambda: P.act(ob[:npart, 1, :], x, AF.Sin, bias=pf_pi[:npart, :], scale=-sc, reads=[r], writes=[r]))
        A(lambda: P.act(ob[:npart, 2, :], x, AF.Sin, bias=pf_npi[:npart, :], scale=sc, reads=[r], writes=[r]))
        A(lambda: P.ts("dve", y, x, 6144.0, -8192.0, ALU.is_ge, ALU.mult, reads=[r], writes=[r]))
        A(lambda: P.stt(y, x, 2048.0, y, ALU.add, ALU.add, reads=[r], writes=[r]))
        A(lambda: P.act(ob[:npart, 0, :], y, AF.Sin, bias=pf_pi[:npart, :], scale=-sc, reads=[r], writes=[r]))
        for j, d in enumerate(dsts):
            if d is not None:
                if rowscale is not None:
                    A(lambda j=j: P.ts("dve", ob[:npart, j, :], ob[:npart, j, :], rowscale, None, ALU.mult,
                                       reads=[r], writes=[r]))
                A(lambda j=j, d=d: P.dma("pool", d[:, col0:col0 + W], ob[:npart, j, :], reads=[r]))

    for chn in range(4):
        gen(128, [[1, 16], [64, 128]], 16 * chn, [T["mcb"], T["msb"], T["msnb"]], chn * W, (1, 1, 1))
    wcol = sb("pwcol", [128, 4], F32)
    A(lambda: P.dma("sp", wcol[:], T["cvec"], writes=[r]))
    for chn in range(2):
        gen(64, [[1, 64], [128, 32]], 64 * chn, [T["irb"], None, T["iib"]], chn * W, (1, 0, 1),
            rowscale=wcol[0:64, 2:3])
    return ops


def phase_prep_only(P, nc, T):
    with contextlib.ExitStack() as st:
        sb = _sb(nc, st)
        for f in prep_ops(P, nc, T, sb):
            f()
        P.barrier()
        P.emit_block()


def phase_prep(P, nc, T):
    with contextlib.ExitStack() as st:
        sb = _sb(nc, st)
        I32 = mybir.dt.int32
        ki = sb("ki", [128, 8192], I32)
        X = sb("X", [128, 8192], F32)
        Tm = sb("Tm", [128, 8192], F32)
        Y = sb("Y", [128, 8192], F32)
        ob = sb("ob", [128, 3, 8192], BF16)
        pi = sb("pi", [128, 1], I32)
        pf = sb("pf", [128, 1], F32)
        r = Res("prep")
        s = TWO_PI / 8192.0
        P.op("pool", lambda e: e.iota(pi[:], pattern=[[0, 1]], base=0, channel_multiplier=1), writes=[r])
        P.cp("dve", pf[:], pi[:], reads=[r], writes=[r])

        def gen(npart, pattern, dsts):
            P.op("pool", lambda e: e.iota(ki[:npart, :], pattern=pattern, base=0, channel_multiplier=0), writes=[r])
            P.cp("dve", X[:npart, :], ki[:npart, :], reads=[r], writes=[r])
            P.ts("dve", X[:npart, :], X[:npart, :], pf[:npart, :], None, ALU.mult, reads=[r], writes=[r])
            dve_mod8192(P, X[:npart, :], Tm[:npart, :], r)
            P.act(ob[:npart, 1, :], X[:npart, :], AF.Sin, bias=pf_pi[:npart, :], scale=-s, reads=[r], writes=[r])
            P.act(ob[:npart, 2, :], X[:npart, :], AF.Sin, bias=pf_npi[:npart, :], scale=s, reads=[r], writes=[r])
            P.ts("dve", Y[:npart, :], X[:npart, :], 6144.0, -8192.0, ALU.is_ge, ALU.mult, reads=[r], writes=[r])
            P.stt(Y[:npart, :], X[:npart, :], 2048.0, Y[:npart, :], ALU.add, ALU.add, reads=[r], writes=[r])
            P.act(ob[:npart, 0, :], Y[:npart, :], AF.Sin, bias=pf_pi[:npart, :], scale=-s, reads=[r], writes=[r])
            for j, d in enumerate(dsts):
                if d is not None:
                    P.dma("sp", d, ob[:npart, j, 0:d.shape[1]], reads=[r])

        pf_pi = sb("pfpi", [128, 1], F32)
        pf_npi = sb("pfnpi", [128, 1], F32)
        P.memset("dve", pf_pi[:], math.pi, writes=[r])
        P.memset("dve", pf_npi[:], -math.pi, writes=[r])
        gen(128, [[128, 64], [1, 128]] if False else [[1, 64], [64, 128]], [T["mcb"], T["msb"], T["msnb"]])
        P.barrier()
        P.emit_block()
    with contextlib.ExitStack() as st:
        sb = _sb(nc, st)
        I32 = mybir.dt.int32
        ki = sb("ki", [64, 4096], I32)
        X = sb("X", [64, 4096], F32)
        Tm = sb("Tm", [64, 4096], F32)
        Y = sb("Y", [64, 4096], F32)
        ob = sb("ob", [64, 3, 4096], BF16)
        pi = sb("pi", [64, 1], I32)
        pf = sb("pf", [64, 1], F32)
        pf_pi = sb("pfpi", [64, 1], F32)
        pf_npi = sb("pfnpi", [64, 1], F32)
        r = Res("prep2")
        s = TWO_PI / 8192.0
        P.memset("dve", pf_pi[:], math.pi, writes=[r])
        P.memset("dve", pf_npi[:], -math.pi, writes=[r])
        P.op("pool", lambda e: e.iota(pi[:], pattern=[[0, 1]], base=0, channel_multiplier=1), writes=[r])
        P.cp("dve", pf[:], pi[:], reads=[r], writes=[r])
        P.op("pool", lambda e: e.iota(ki[:], pattern=[[1, 128], [128, 32]], base=0, channel_multiplier=0), writes=[r])
        P.cp("dve", X[:], ki[:], reads=[r], writes=[r])
        P.ts("dve", X[:], X[:], pf[:], None, ALU.mult, reads=[r], writes=[r])
        dve_mod8192(P, X[:], Tm[:], r)
        P.act(ob[:, 2, :], X[:], AF.Sin, bias=pf_npi[:], scale=s, reads=[r], writes=[r])
        P.ts("dve", Y[:], X[:], 6144.0, -8192.0, ALU.is_ge, ALU.mult, reads=[r], writes=[r])
        P.stt(Y[:], X[:], 2048.0, Y[:], ALU.add, ALU.add, reads=[r], writes=[r])
        P.act(ob[:, 0, :], Y[:], AF.Sin, bias=pf_pi[:], scale=-s, reads=[r], writes=[r])
        P.dma("sp", T["irb"], ob[:, 0, :], reads=[r])
        P.dma("sp", T["iib"], ob[:, 2, :], reads=[r])
        P.barrier()
        P.emit_block()


def phase_lru(P, nc, T, l):
    with contextlib.ExitStack() as st:
        sb = _sb(nc, st)
        ppt = sb("ppt", [128, NPP], F32)
        gst = sb("gst", [128, 4, 128], F32)
        gw = sb("gw", [128, 4, 128], BF16)
        U = sb("U", [128, L + 3], F32)
        XC = sb("XC", [128, L], F32)
        XCB = sb("XCB", [128, L], BF16)
        UB = sb("UB", [128, L + 3], BF16)
        DG = sb("DG", [128, 4, 128], BF16)
        idf = sb("idf", [128, 128], F32)
        r_UB, r_DG = Res("UB"), Res("DG")
        Ad = [sb("A%d" % d, [128, L], F32) for d in range(2)]
        Bd = [sb("B%d" % d, [128, L], F32) for d in range(2)]
        TMP = sb("TMP", [128, L], F32)
        G = sb("G", [128, L], F32)
        r_G = Res("G")
        YA = sb("YA", [128, 3, L], F32)
        cs = sb("cs", [128, 8], F32)
        onf = sb("onf", [128, 128], F32)
        onb = sb("onb", [128, 128], BF16)
        sqb = sb("sqb", [128, 3, 512], BF16)
        rst = sb("rst", [128, 512], F32)
        ob = sb("ob", [128, 2, 512], BF16)
        pg = st.enter_context(nc.psum_tensor(_un("pg"), [128, 4, 512], F32))
        pn = st.enter_context(nc.psum_tensor(_un("pn"), [128, 2, 512], F32))
        r_pp, r_on, r_gw, r_U, r_XC, r_XCB, r_T, r_cs = (Res("pp"), Res("on"), Res("gw"), Res("U"), Res("XC"),
                                                         Res("XCB"), Res("TMP"), Res("cs"))
        r_A = [Res("A0"), Res("A1")]
        r_B = [Res("B0"), Res("B1")]
        r_YA = [Res("YA%d" % i) for i in range(3)]
        pg_r, pn_r, ob_r = Ring(4, "pg"), Ring(2, "pn"), Ring(2, "ob")
        r_sq, r_rst, r_gst = Res("sq"), Res("rst"), Res("gst")
        P.dma("sp", ppt[:], T["pp"][l], writes=[r_pp])
        P.dma("sp", onf[:], T["ones"], writes=[r_on])
        P.dma("sp", idf[:], T["ident"], writes=[r_on])
        P.cp("dve", onb[:], onf[:], reads=[r_on], writes=[r_on])
        for ta in range(3):
            c = lambda nm, k=0: ppt[:, PP[nm] + k:PP[nm] + k + 1]
            P.dma("sp", gst[:], T["gates"][l, :, ta, :, :].rearrange("g p m -> p g m"), writes=[r_gst])
            P.cp("dve", gw[:], gst[:], reads=[r_gst], writes=[r_gw])
            for d in range(2):
                P.act(cs[:, d:d + 1], c("lam", 3 * d + ta), AF.Exp, scale=-1.0, reads=[r_pp], writes=[r_cs])
            for d in range(2):
                P.act(cs[:, d:d + 1], cs[:, d:d + 1], AF.Ln, bias=1.0, reads=[r_cs], writes=[r_cs])
            P.ts("dve", cs[:, 0:2], cs[:, 0:2], -8.0, None, ALU.mult, reads=[r_cs], writes=[r_cs])
            P.dma("sp", G[:], T["pA"][384 + ta * 128:384 + (ta + 1) * 128, :], writes=[r_G])
            P.act(G[:], G[:], AF.Gelu, reads=[r_G], writes=[r_G])
            P.memset("pool", U[:, 0:2], 0.0, writes=[r_U])
            P.memset("pool", U[:, L + 2:L + 3], 0.0, writes=[r_U])
            P.dma("sp", U[:, 2:L + 2], T["pA"][ta * 128:(ta + 1) * 128, :], writes=[r_U])
            P.cp("act", UB[:], U[:], reads=[r_U], writes=[r_UB])
            for k in range(4):
                P.ts("dve", DG[:, k, :], idf[:], c("caw", 3 * k + ta), None, ALU.mult, reads=[r_on, r_pp],
                     writes=[r_DG])
            for ch in range(8):
                j, r = pg_r.next()
                for k in range(4):
                    P.mm(pg[:, j, :], DG[:, k, :], UB[:, ch * 512 + k:ch * 512 + k + 512], start=(k == 0), stop=(k == 3),
                         reads=[r_DG, r_UB], writes=[r])
                P.act(XC[:, ch * 512:(ch + 1) * 512], pg[:, j, :], AF.Identity, bias=c("cab", ta), reads=[r, r_pp],
                      writes=[r_XC])
            P.cp("dve", XCB[:], XC[:], reads=[r_XC], writes=[r_XCB])
            for d in range(2):
                for ch in range(8):
                    sl = slice(ch * 512, (ch + 1) * 512)
                    j, r = pg_r.next()
                    P.mm(pg[:, j, :], gw[:, 2 * d, :], XCB[:, sl], reads=[r_gw, r_XCB], writes=[r])
                    P.act(Ad[d][:, sl], pg[:, j, :], AF.Sigmoid, bias=c("ba", 3 * d + ta), reads=[r, r_pp],
                          writes=[r_A[d]])
                    j, r = pg_r.next()
                    P.mm(pg[:, j, :], gw[:, 2 * d + 1, :], XCB[:, sl], reads=[r_gw, r_XCB], writes=[r])
                    P.act(Bd[d][:, sl], pg[:, j, :], AF.Sigmoid, bias=c("bx", 3 * d + ta), reads=[r, r_pp],
                          writes=[r_B[d]])
                scr, r_scr = (TMP[:], r_T) if d == 0 else (U[:, 0:L], r_U)
                P.act(Ad[d][:], Ad[d][:], AF.Exp, scale=cs[:, d:d + 1], reads=[r_cs], writes=[r_A[d]])
                P.act(scr, Ad[d][:], AF.Square, reads=[r_A[d]], writes=[r_scr])
                P.act(scr, scr, AF.Sqrt, bias=1.0, scale=-1.0, reads=[r_scr], writes=[r_scr])
                P.tt("pool", Bd[d][:], Bd[d][:], XC[:], ALU.mult, reads=[r_XC], writes=[r_B[d]])
                P.tt("dve", Bd[d][:], Bd[d][:], scr, ALU.mult, reads=[r_scr], writes=[r_B[d]])
                if d == 0:
                    P.scan(TMP[:], Ad[0][:], Bd[0][:], reads=[r_A[0], r_B[0]], writes=[r_T])
                else:
                    P.scan(XC[:, ::-1], Ad[1][:, ::-1], Bd[1][:, ::-1], reads=[r_A[1], r_B[1]], writes=[r_XC])
            P.tt("dve", TMP[:], TMP[:], XC[:], ALU.add, reads=[r_XC], writes=[r_T])
            P.tt("dve", YA[:, ta, :], TMP[:], G[:], ALU.mult, reads=[r_T, r_G], writes=[r_YA[ta]])
        for ch in range(8):
            sl = slice(ch * 512, (ch + 1) * 512)
            for ta in range(3):
                P.act(sqb[:, ta, :], YA[:, ta, sl], AF.Square, reads=[r_YA[ta]], writes=[r_sq])
            j, r = pn_r.next()
            for ta in range(3):
                P.mm(pn[:, j, :], onb[:], sqb[:, ta, :], start=(ta == 0), stop=(ta == 2), reads=[r_on, r_sq],
                     writes=[r])
            P.act(rst[:], pn[:, j, :], AF.Sqrt, bias=EPS, scale=1.0 / D_A, reads=[r], writes=[r_rst])
            P.recip(rst[:], rst[:], reads=[r_rst], writes=[r_rst])
            for ta in range(3):
                oj, orr = ob_r.next()
                P.stt(ob[:, oj, :], YA[:, ta, sl], ppt[:, PP["gna"] + ta:PP["gna"] + ta + 1], rst[:], ALU.mult,
                      ALU.mult, reads=[r_YA[ta], r_rst, r_pp], writes=[orr])
                P.dma("pool", T["yT"][ta * 128:(ta + 1) * 128, sl], ob[:, oj, :], reads=[orr])
        P.barrier()
        P.emit_block()


def phase_attn(P, nc, T, l):
    with contextlib.ExitStack() as st:
        sb = _sb(nc, st)
        ppt = sb("ppt", [128, NPP], F32)
        idf = sb("idf", [128, 128], F32)
        idb = sb("idb", [128, 128], BF16)
        prf = sb("prf", [128, 128], F32)
        prb = sb("prb", [128, 128], BF16)
        mkf = sb("mkf", [128, 2, 128], F32)
        mkb = sb("mkb", [128, 2, 128], BF16)
        ct = sb("ct", [128, L], F32)
        stt_ = sb("st", [128, L], F32)
        S2 = sb("S", [128, 2, L], F32)
        XB2 = sb("XB", [128, 2, L], BF16)
        Q = sb("Q", [128, 3, L], BF16)
        KK = sb("KK", [128, 2, L], BF16)
        VA = sb("VA", [128, NT, 2, 65], BF16)
        t1 = sb("t1", [128, 2, 512], F32)
        t2 = sb("t2", [128, 2, 512], F32)
        PT = sb("PT", [128, 6, 6, 384], BF16)
        es = sb("es", [128, 6], F32)
        den = sb("den", [128, 2, 6], F32)
        yb = sb("yb", [128, 2, 384], F32)
        ybn = sb("ybn", [128, 3, 384], BF16)
        sqf = sb("sqf", [128, 384], F32)
        epsb = sb("epsb", [128, 1], F32)
        sm4 = sb("sm", [128, 4, 8], F32)
        YT = sb("YT", [128, 3, L], BF16)
        ps = st.enter_context(nc.psum_tensor(_un("ps"), [128, 4, 512], F32))
        po = st.enter_context(nc.psum_tensor(_un("po"), [128, 2, 512], F32))
        ptr = st.enter_context(nc.psum_tensor(_un("ptr"), [128, 2, 1024], BF16))
        r_pp, r_c, r_tab, r_S, r_XB, r_VA, r_es, r_sm, r_YT, r_sq = (Res("pp"), Res("c"), Res("tab"), Res("S"),
                                                                    Res("XB"), Res("VA"), Res("es"), Res("sm"),
                                                                    Res("YT"), Res("sq"))
        r_Q = [Res("Q%d" % i) for i in range(3)]
        r_K = [Res("K%d" % i) for i in range(2)]
        ps_r, po_r, ptr_r, t1_r, t2_r, PT_r = (Ring(4, "ps"), Ring(2, "po"), Ring(2, "ptr"), Ring(2, "t1"),
                                               Ring(2, "t2"), [Res("PT%d" % i) for i in range(6)])
        den_r, yb_r, ybn_r = Ring(2, "den"), Ring(2, "yb"), Ring(3, "ybn")
        P.dma("sp", ppt[:], T["pp"][l], writes=[r_pp])
        P.dma("sp", idf[:], T["ident"], writes=[r_c])
        P.dma("sp", prf[:], T["prot"], writes=[r_c])
        P.dma("sp", mkf[:, 0, :], T["maskn"], writes=[r_c])
        P.dma("sp", mkf[:, 1, :], T["maskp"], writes=[r_c])
        P.cp("dve", idb[:], idf[:], reads=[r_c], writes=[r_c])
        P.cp("dve", prb[:], prf[:], reads=[r_c], writes=[r_c])
        P.cp("dve", mkb[:], mkf[:], reads=[r_c], writes=[r_c])
        P.memset("pool", ct[:], 1.0, writes=[r_tab])
        P.memset("pool", stt_[:], 0.0, writes=[r_tab])
        for base in (0, 64):
            for off in (0, 8):
                P.dma("sp", ct[base + off:base + off + 8, :], T["rope"][0:8, :], writes=[r_tab])
                P.dma("sp", stt_[base + off:base + off + 8, :], T["rope"][8:16, :], writes=[r_tab])
        for base in (0, 64):
            P.ts("dve", stt_[base:base + 8, :], stt_[base:base + 8, :], -1.0, None, ALU.mult, writes=[r_tab])
        P.act(es[:], ppt[:, PP["sink"]:PP["sink"] + 6], AF.Exp, reads=[r_pp], writes=[r_es])
        P.memset("dve", epsb[:], EPS, writes=[r_es])
        S_r, XB_r = Ring(2, "S"), Ring(2, "XB")
        sj, sr = S_r.next()
        P.dma("sp", S2[:, sj, :].rearrange("p (n c) -> p n c", c=128), T["pV"].rearrange("(n p) c -> p n c", p=128),
              writes=[sr])
        P.memset("pool", VA[:], 1.0, writes=[r_VA])
        P.cp("dve", VA[:, :, :, 0:64], S2[:, sj, :].rearrange("p (n g c) -> p n g c", g=2, c=64), reads=[sr],
             writes=[r_VA])

        def rope(dst, dres, loads):
            sj, sr = S_r.next()
            for (pr, src) in loads:
                P.dma("sp", S2[pr, sj, :], src, writes=[sr])
            bj, br = XB_r.next()
            P.cp("act", XB2[:, bj, :], S2[:, sj, :], reads=[sr], writes=[br])
            for ch in range(8):
                sl = slice(ch * 512, (ch + 1) * 512)
                j, r = ps_r.next()
                P.mm(ps[:, j, :], prb[:], XB2[:, bj, sl], reads=[r_c, br], writes=[r])
                j1, r1 = t1_r.next()
                P.tt("dve", t1[:, j1, :], ps[:, j, :], stt_[:, sl], ALU.mult, reads=[r, r_tab], writes=[r1])
                j2, r2 = t2_r.next()
                P.tt("pool", t2[:, j2, :], S2[:, sj, sl], ct[:, sl], ALU.mult, reads=[sr, r_tab], writes=[r2])
                P.tt("dve", dst[:, sl], t1[:, j1, :], t2[:, j2, :], ALU.add, reads=[r1, r2], writes=[dres])

        for qt in range(3):
            rope(Q[:, qt, :], r_Q[qt], [(slice(0, 128), T["pQK"][qt * 128:(qt + 1) * 128, :])])
        rope(KK[:, 0, :], r_K[0], [(slice(0, 128), T["pQK"][384:512, :])])

        NPT = 6
        stB = {}
        sm_r = [Res("sm%d" % i) for i in range(4)]

        def stage_b(i):
            oj, orr = po_r.next()
            for h in range(6):
                g = h % 2
                kbs = [kb for kb in (i - 1, i, i + 1) if 0 <= kb < NT]
                for n, kb in enumerate(kbs):
                    pos = i - kb + 1
                    P.mm(po[:, oj, h * 65:(h + 1) * 65], PT[:, kb % NPT, h, pos * 128:(pos + 1) * 128],
                         VA[:, kb, g, :], start=(n == 0), stop=(n == len(kbs) - 1), reads=[PT_r[kb % NPT], r_VA],
                         writes=[orr])
            dj, dr = den_r.next()
            ov = po[:, oj, 0:390].rearrange("p (h c) -> p h c", c=65)
            P.tt("dve", den[:, dj, :], ov[:, :, 64], es[:], ALU.add, reads=[orr, r_es], writes=[dr])
            P.recip(den[:, dj, :], den[:, dj, :], reads=[dr], writes=[dr])
            yj, yr = yb_r.next()
            P.tt("dve", yb[:, yj, :].rearrange("p (h c) -> p h c", c=64), ov[:, :, 0:64],
                 den[:, dj, :].unsqueeze(2).broadcast_to([128, 6, 64]), ALU.mult, reads=[orr, dr], writes=[yr])
            k = i % 4
            smv, smr = sm4[:, k, :], sm_r[k]
            P.stt(sqf[:], yb[:, yj, :], 1.0, yb[:, yj, :], ALU.mult, ALU.mult, reads=[yr], writes=[r_sq, smr],
                  accum_out=smv[:, 0:1])
            P.act(smv[:, 2:3], smv[:, 0:1], AF.Ln, bias=epsb[:], scale=1.0 / D_B, reads=[smr, r_es], writes=[smr])
            P.act(smv[:, 3:4], smv[:, 2:3], AF.Exp, scale=-0.5, reads=[smr], writes=[smr])
            nj, nr = ybn_r.next()
            P.ts("dve", ybn[:, nj, :], yb[:, yj, :], smv[:, 3:4], None, ALU.mult, reads=[yr, smr], writes=[nr])
            stB[i] = (nj, nr)

        def stage_c(i):
            nj, nr = stB.pop(i)
            tj, trr = ptr_r.next()
            for tb in range(3):
                P.tr(ptr[:, tj, tb * 128:(tb + 1) * 128], ybn[:, nj, tb * 128:(tb + 1) * 128], idb[:],
                     reads=[nr, r_c], writes=[trr])
            for tb in range(3):
                P.ts("dve", YT[:, tb, i * 128:(i + 1) * 128], ptr[:, tj, tb * 128:(tb + 1) * 128],
                     ppt[:, PP["gnb"] + tb:PP["gnb"] + tb + 1], None, ALU.mult, reads=[trr, r_pp], writes=[r_YT])

        def stage_a(jb):
            lo_b, hi_b = max(jb - 1, 0), min(jb + 1, NT - 1)
            slot = jb % NPT
            for h in range(6):
                g, o, qt = h % 2, 64 * (h % 2), h // 2
                kt = 0 if o == 64 * g else 1
                c0 = (lo_b - jb + 1) * 128
                ncol = (hi_b - lo_b + 1) * 128
                j, r = ps_r.next()
                P.mm(ps[:, j, 0:ncol], KK[o:o + 64, kt, jb * 128:(jb + 1) * 128],
                     Q[o:o + 64, qt, lo_b * 128:(hi_b + 1) * 128], reads=[r_K[kt], r_Q[qt]], writes=[r])
                P.act(PT[:, slot, h, c0:c0 + ncol], ps[:, j, 0:ncol], AF.Exp, scale=0.125, reads=[r],
                      writes=[PT_r[slot]])
            if jb > 0:
                P.tt("dve", PT[:, slot, :, 0:128], PT[:, slot, :, 0:128], mkb[:, 0:1, :].broadcast_to([128, 6, 128]),
                     ALU.mult, reads=[r_c], writes=[PT_r[slot]])
            if jb < NT - 1:
                P.tt("dve", PT[:, slot, :, 256:384], PT[:, slot, :, 256:384],
                     mkb[:, 1:2, :].broadcast_to([128, 6, 128]), ALU.mult, reads=[r_c], writes=[PT_r[slot]])

        for step in range(NT + 4):
            if step < NT:
                stage_a(step)
            if 0 <= step - 2 < NT:
                stage_b(step - 2)
            if 0 <= step - 3 < NT:
                stage_c(step - 3)
        P.dma("sp", T["yT"][384:768, :].rearrange("(t p) n -> p t n", p=128), YT[:], reads=[r_YT])
        P.barrier()
        P.emit_block()


def phase_hyena_a(P, nc, T, l):
    with contextlib.ExitStack() as st:
        sb = _sb(nc, st)
        ppt = sb("ppt", [128, NPP], F32)
        cv = sb("cv", [128, 4], F32)
        hz = sb("hz", [33, L + 1], F32)
        w1 = sb("w1", [33, 64], F32)
        w2 = sb("w2", [64, 64], F32)
        w3 = sb("w3", [64, 512], F32)
        frb = sb("frb", [64, 2], F32)
        U1 = sb("U1", [64, 4, 512], F32)
        Tm = sb("Tm", [64, 4, 512], F32)
        H1 = sb("H1", [64, 5, 512], F32)
        H2 = sb("H2", [64, L + 1], F32)
        tp = sb("tp", [128, 2, 512], F32)
        dec = sb("dec", [128, 2, 512], F32)
        fo = sb("fo", [128, 2, 512], BF16)
        U = sb("U", [128, 2, L + 2], F32)
        Rv = sb("Rv", [128, 3, L], F32)
        zb = sb("zb", [128, L], BF16)
        x0b = sb("x0b", [128, L], BF16)
        UBh = sb("UBh", [128, 2, L + 2], BF16)
        DGh = sb("DGh", [128, 2, 3, 128], BF16)
        idfh = sb("idfh", [128, 128], F32)
        p1 = st.enter_context(nc.psum_tensor(_un("p1"), [128, 3, 512], F32))
        p3 = st.enter_context(nc.psum_tensor(_un("p3"), [128, 2, 512], F32))
        pcv = st.enter_context(nc.psum_tensor(_un("pcv"), [128, 2, 512], F32))
        r_pp, r_w, r_hz, r_frb = Res("pp"), Res("w"), Res("hz"), Res("frb")
        P.dma("sp", ppt[:], T["pp"][l], writes=[r_pp])
        P.dma("sp", cv[:], T["cvec"], writes=[r_pp])
        P.dma("sp", w1[:], T["hy_w1"][l], writes=[r_w])
        P.dma("sp", w2[:], T["hy_w2"][l], writes=[r_w])
        P.dma("sp", w3[:], T["hy_w3"][l], writes=[r_w])
        P.memset("pool", hz[:, L:L + 1], 0.0, writes=[r_hz])
        P.dma("sp", hz[:, 0:L], T["hyz"], writes=[r_hz])
        fr = ppt[0:64, PP["hfr"]:PP["hfr"] + 1]
        P.tt("dve", frb[:, 0:1], fr, ppt[0:64, PP["hb1"]:PP["hb1"] + 1], ALU.mult, reads=[r_pp], writes=[r_frb])
        P.tt("dve", frb[:, 1:2], fr, ppt[0:64, PP["hb2"]:PP["hb2"] + 1], ALU.mult, reads=[r_pp], writes=[r_frb])
        p1_r, p3_r, U1_r, H1_r, H2_r, tp_r, dec_r, fo_r = (Ring(3, "p1"), Ring(2, "p3"), Ring(4, "U1"), Ring(5, "H1"),
                                                           Ring(8, "H2"), Ring(2, "tp"), Ring(2, "dec"), Ring(2, "fo"))

        def sin_layer(psrc, pres, k, Hdst, hres):
            uj, ur = U1_r.next()
            u, t = U1[:, uj, :], Tm[:, uj, :]
            P.ts("dve", u, psrc, fr, frb[:, k:k + 1], ALU.mult, ALU.add, reads=[pres, r_pp, r_frb], writes=[ur])
            P.ts("dve", t, u, 1.0 / TWO_PI, MAGIC, ALU.mult, ALU.add, reads=[ur], writes=[ur])
            P.ts("dve", t, t, -MAGIC, -TWO_PI, ALU.add, ALU.mult, reads=[ur], writes=[ur])
            P.tt("dve", u, u, t, ALU.add, reads=[ur], writes=[ur])
            P.ts("dve", u, u, 3.14159, -3.14159, ALU.min, ALU.max, reads=[ur], writes=[ur])
            P.act(Hdst, u, AF.Sin, reads=[ur], writes=[hres])

        sH1, sH2 = {}, {}

        r_H2 = [Res("H2_%d" % i) for i in range(9)]
        P.memset("pool", H2[:, L:L + 1], 0.0, writes=[r_H2[8]])

        def fs1(c):
            rhs = hz[:, c * 512:(c + 1) * 512]
            j, r = p1_r.next()
            P.mm(p1[0:64, j, :], w1[:], rhs, reads=[r_w, r_hz], writes=[r])
            h1j, h1r = H1_r.next()
            sin_layer(p1[0:64, j, :], r, 0, H1[:, h1j, :], h1r)
            sH1[c] = (h1j, h1r)

        def fs2(c):
            h1j, h1r = sH1.pop(c)
            j, r = p1_r.next()
            P.mm(p1[0:64, j, :], w2[:], H1[:, h1j, :], reads=[r_w, h1r], writes=[r])
            sin_layer(p1[0:64, j, :], r, 1, H2[:, c * 512:(c + 1) * 512], r_H2[c])

        def fs3(c):
            if c < 8:
                h2v, h2rs = H2[:, c * 512:(c + 1) * 512], [r_H2[c]]
            else:
                s0 = L - (c - 8) * 512
                h2v, h2rs = H2[:, s0:s0 - 512:-1], r_H2
            tj, tr_ = tp_r.next()
            P.dma("sp", tp[:, tj, :], T["tpos"][0:1, c * 512:(c + 1) * 512].broadcast_to([128, 512]), writes=[tr_])
            half = 0 if c < 8 else 1
            for ctile in range(2):
                j, r = p3_r.next()
                P.mm(p3[:, j, :], w3[:, half * 256 + ctile * 128:half * 256 + (ctile + 1) * 128], h2v,
                     reads=[r_w] + h2rs, writes=[r])
                dj, dr = dec_r.next()
                P.act(dec[:, dj, :], tp[:, tj, :], AF.Exp, scale=cv[:, ctile:ctile + 1], reads=[tr_, r_pp], writes=[dr])
                fj, frr = fo_r.next()
                P.tt("dve", fo[:, fj, :], p3[:, j, :], dec[:, dj, :], ALU.mult, reads=[r, dr], writes=[frr])
                P.dma("pool", T["hcT"][ctile * 128:(ctile + 1) * 128, c * 512:(c + 1) * 512], fo[:, fj, :], reads=[frr])

        U_r = Ring(2, "U")
        UB_r, DG_r, pcv_r = Ring(2, "UBh"), Ring(2, "DGh"), Ring(2, "pcv")
        r_idf = Res("idfh")
        P.dma("sp", idfh[:], T["ident"], writes=[r_idf])
        r_R = [Res("R%d" % i) for i in range(3)]
        r_zb, r_x0 = Res("zb"), Res("x0b")

        def conv_tile(ctile, role):
            ti = role * 2 + ctile
            uj, ur = U_r.next()
            P.memset("pool", U[:, uj, 0:1], 0.0, writes=[ur])
            P.memset("pool", U[:, uj, L + 1:L + 2], 0.0, writes=[ur])
            P.dma("sp", U[:, uj, 1:L + 1], T["pC"][role * 256 + ctile * 128:role * 256 + (ctile + 1) * 128, :],
                  writes=[ur])
            wc = lambda k: ppt[:, PP["hcw"] + 6 * k + ti:PP["hcw"] + 6 * k + ti + 1]
            P.act(Rv[:, role, :], U[:, uj, 0:L], AF.Identity, scale=wc(0), bias=ppt[:, PP["hcb"] + ti:PP["hcb"] + ti + 1],
                  reads=[ur, r_pp], writes=[r_R[role]])
            for k in (1, 2):
                P.stt(Rv[:, role, :], U[:, uj, k:k + L], wc(k), Rv[:, role, :], ALU.mult, ALU.add,
                      reads=[ur, r_pp], writes=[r_R[role]])

        def conv_finish(ctile):
            P.tt("dve", zb[:], Rv[:, 2, :], Rv[:, 1, :], ALU.mult, reads=[r_R[1], r_R[2]], writes=[r_zb])
            P.cp("act", x0b[:], Rv[:, 0, :], reads=[r_R[0]], writes=[r_x0])
            rows = slice(ctile * 128, (ctile + 1) * 128)
            P.dma("pool", T["zT"][rows, :], zb[:], reads=[r_zb])
            P.dma("pool", T["zx"][0, rows, :], zb[:], reads=[r_zb])
            P.dma("pool", T["zx"][1, rows, :], x0b[:], reads=[r_x0])

        cq = []
        for ctile in range(2):
            for role in range(3):
                cq.append((lambda ct=ctile, ro=role: conv_tile(ct, ro)))
            cq.append((lambda ct=ctile: conv_finish(ct)))
        for gi in range(2):
            for c in range(4 * gi, 4 * gi + 4):
                fs1(c)
            for c in range(4 * gi, 4 * gi + 4):
                fs2(c)
            for _ in range(2):
                if cq:
                    cq.pop(0)()
        for gi in range(4):
            for c in range(4 * gi, 4 * gi + 4):
                fs3(c)
            for _ in range(2):
                if cq:
                    cq.pop(0)()
        while cq:
            cq.pop(0)()
        P.barrier()
        P.emit_block()


def phase_hyena_b(P, nc, T, l):
    with contextlib.ExitStack() as st:
        sb = _sb(nc, st)
        ppt = sb("ppt", [128, NPP], F32)
        mc = sb("mc", [128, 64, 128], BF16)
        ms = sb("ms", [128, 64, 128], BF16)
        msn = sb("msn", [128, 64, 128], BF16)
        irs = sb("irs", [128, 128, 32], BF16)
        cst = sb("cst", [128, 780], F32)
        e1zb = sb("e1zb", [128, 264], BF16)
        e1hb = sb("e1hb", [128, 132], BF16)
        g1b = sb("g1b", [128, 256], BF16)
        onb = sb("onb", [128, 128], BF16)
        BIG = sb("BIG", [128, 16384], BF16)
        BST = sb("BST", [128, 128, 64], BF16)
        ZHs = sb("ZHs", [128, 32, 128], BF16)
        ZZs = sb("ZZs", [128, 16, 128], BF16)
        Hs = sb("Hs", [128, 3, 2, 2, 64], F32)
        Y1 = sb("Y1", [128, 64, 2, 64], BF16)
        Y2 = sb("Y2", [128, 64, 2, 64], BF16)
        zg = sb("zg", [128, L], BF16)
        x0g = sb("x0g", [128, L], BF16)
        yc = sb("yc", [128, 2, L], BF16)
        tq = sb("tq", [128, 2, 2, 256], F32)
        gt = sb("gt", [128, 2, 512], F32)
        sqb = sb("sqb", [128, 2, 512], BF16)
        rst = sb("rst", [128, 512], F32)
        ob = sb("ob", [128, 2, 512], BF16)
        pb = st.enter_context(nc.psum_tensor(_un("pb"), [128, 8, 512], F32))
        r_pp, r_m, r_c = Res("pp"), Res("m"), Res("c")
        P.dma("sp", ppt[:], T["pp"][l], writes=[r_pp])
        P.dma("sp", mc[:].rearrange("p a b -> p (a b)"), T["mcb"], writes=[r_m])
        P.dma("sp", ms[:].rearrange("p a b -> p (a b)"), T["msb"], writes=[r_m])
        P.dma("sp", msn[:].rearrange("p a b -> p (a b)"), T["msnb"], writes=[r_m])
        P.dma("sp", irs[0:64, :, :].rearrange("p a b -> p (a b)"), T["irb"], writes=[r_m])
        P.dma("sp", irs[64:128, :, :].rearrange("p a b -> p (a b)"), T["iib"], writes=[r_m])
        P.dma("sp", cst[:, 0:264], T["e1z"], writes=[r_c])
        P.dma("sp", cst[:, 264:396], T["e1h"], writes=[r_c])
        P.dma("sp", cst[:, 396:652], T["g1"], writes=[r_c])
        P.dma("sp", cst[:, 652:780], T["ones"], writes=[r_c])
        P.cp("dve", e1zb[:], cst[:, 0:264], reads=[r_c], writes=[r_c])
        P.cp("dve", e1hb[:], cst[:, 264:396], reads=[r_c], writes=[r_c])
        P.cp("dve", g1b[:], cst[:, 396:652], reads=[r_c], writes=[r_c])
        P.cp("dve", onb[:], cst[:, 652:780], reads=[r_c], writes=[r_c])
        A = BIG[:, :].rearrange("p (r k s c) -> p r k s c", r=2, k=64, s=2)
        Bst = BST[:, :, :]
        r_bst = Res("bst")
        r_zgh, r_x0h = [Res("zg0"), Res("zg1")], [Res("x0g0"), Res("x0g1")]
        deferred = []
        stepc = [0]

        def bg_step():
            stepc[0] += 1
            if deferred and stepc[0] % 5 == 0:
                deferred.pop(0)()
        r_Y2 = Res("Y2")
        r_zh, r_zz, r_big, r_Y, r_zg, r_x0g = Res("zh"), Res("zz"), Res("big"), Res("Y"), Res("zg"), Res("x0g")
        r_lo = r_hi = r_big
        r_yc = [Res("yc0"), Res("yc1")]
        pb_r, tq_r, gt_r, hs_r = Ring(8, "pb"), Ring(2, "tq"), Ring(2, "gt"), Ring(3, "hs")
        P.memset("pool", Y1[:], 0.0, writes=[r_Y])
        P.memset("pool", Y2[:], 0.0, writes=[r_Y2])
        ecnt = [0]

        def f1_evac(j, r, which, c4):
            P.cp("act", A[:, :, 0:33, which, c4 * 4:c4 * 4 + 4],
                 pb[:, j, 0:264].rearrange("p (c r k) -> p r k c", c=4, r=2), reads=[r], writes=[r_big])

        for g in range(4):
            ctile, hp = g // 2, 64 * (g % 2)
            c0 = 64 * g
            for cl in range(2):
                P.dma("sp", ZHs[cl * 64:(cl + 1) * 64, :, :],
                      T["hcT"][c0 + cl:c0 + 64:2, :].rearrange("c (a b) -> a c b", b=128), writes=[r_zh])
            for cl in range(4):
                P.dma("sp", ZZs[cl * 32:(cl + 1) * 32, :, :],
                      T["zT"][c0 + cl:c0 + 64:4, :].rearrange("c (a b) -> a c b", b=128), writes=[r_zz])
            for c4 in range(16):
                j, r = pb_r.next()
                for mm_ in range(2):
                    P.mm(pb[:, j, mm_ * 132:(mm_ + 1) * 132], ZHs[:, c4 * 2 + mm_, :], e1hb[:], reads=[r_zh, r_c],
                         writes=[r])
                f1_evac(j, r, 0, c4)
                bg_step()
            for c4 in range(16):
                j, r = pb_r.next()
                P.mm(pb[:, j, 0:264], ZZs[:, c4, :], e1zb[:], reads=[r_zz, r_c], writes=[r])
                f1_evac(j, r, 1, c4)
                bg_step()
            for kq in range(17):
                j, r = pb_r.next()
                bv = pb[:, j, :].rearrange("p (k r s c) -> p k r s c", k=2, r=2, s=2)
                nk = 2 if kq < 16 else 1
                for kk in range(nk):
                    ka = kq * 2 + kk
                    a_re = A[:, 0, ka, :, :].rearrange("p s c -> p (s c)")
                    a_im = A[:, 1, ka, :, :].rearrange("p s c -> p (s c)")
                    o_re = bv[:, kk, 0, :, :].rearrange("p s c -> p (s c)")
                    o_im = bv[:, kk, 1, :, :].rearrange("p s c -> p (s c)")
                    P.mm(o_re, mc[:, ka, :], a_re, start=True, stop=False, reads=[r_m, r_big], writes=[r])
                    P.mm(o_re, ms[:, ka, :], a_im, start=False, stop=True, reads=[r_m, r_big], writes=[r])
                    P.mm(o_im, mc[:, ka, :], a_im, start=True, stop=False, reads=[r_m, r_big], writes=[r])
                    P.mm(o_im, msn[:, ka, :], a_re, start=False, stop=True, reads=[r_m, r_big], writes=[r])
                hj, hr = hs_r.next()
                P.act(Hs[:, hj, 0:nk, :, :], bv[:, 0:nk, :, 0, :], AF.Copy, scale=1.0 / 8192.0, reads=[r], writes=[hr])
                tj, tr_ = tq_r.next()
                ks = slice(kq * 2, kq * 2 + nk)
                ta = tq[:, tj, 0, :].rearrange("p (k r c) -> p k r c", k=2, r=2)[:, 0:nk]
                tb = tq[:, tj, 1, :].rearrange("p (k r c) -> p k r c", k=2, r=2)[:, 0:nk]
                xz = bv[:, 0:nk, :, 1, :]
                P.tt("dve", ta, xz, Hs[:, hj, 0:nk, 0:1, :].broadcast_to([128, nk, 2, 64]), ALU.mult, reads=[r, hr],
                     writes=[tr_])
                P.tt("dve", tb, xz, Hs[:, hj, 0:nk, 1:2, :].broadcast_to([128, nk, 2, 64]), ALU.mult, reads=[r, hr],
                     writes=[tr_])
                P.tt("pool", Y1[:, :, 0, ks].rearrange("p c k -> p k c"), ta[:, :, 0, :], tb[:, :, 1, :], ALU.subtract,
                     reads=[tr_], writes=[r_Y])
                P.tt("pool", Y1[:, :, 1, ks].rearrange("p c k -> p k c"), tb[:, :, 0, :], ta[:, :, 1, :], ALU.add,
                     reads=[tr_], writes=[r_Y])
                P.tt("dve", Y2[:, :, 1, ks].rearrange("p c k -> p k c"), ta[:, :, 0, :], tb[:, :, 1, :], ALU.subtract,
                     reads=[tr_], writes=[r_Y2])
                P.stt(Y2[:, :, 0, ks].rearrange("p c k -> p k c"), tb[:, :, 0, :], -1.0, ta[:, :, 1, :], ALU.mult,
                      ALU.subtract, reads=[tr_], writes=[r_Y2])
                bg_step()
            while deferred:
                deferred.pop(0)()
            r_zg, r_x0g = r_zgh[g % 2], r_x0h[g % 2]
            P.dma("sp", zg[hp:hp + 64, :], T["zx"][0, c0:c0 + 64, :], writes=[r_zg])
            P.dma("sp", x0g[hp:hp + 64, :], T["zx"][1, c0:c0 + 64, :], writes=[r_x0g])
            for c4 in range(16):
                j, r = pb_r.next()
                for cc in range(4):
                    c = c4 * 4 + cc
                    o = pb[:, j, cc * 128:(cc + 1) * 128]
                    P.mm(o, Y1[:, c, :, :].rearrange("p r k -> p (r k)"), g1b[:, 0:128], start=True, stop=False,
                         reads=[r_Y, r_c], writes=[r])
                    P.mm(o, Y2[:, c, :, :].rearrange("p r k -> p (r k)"), g1b[:, 128:256], start=False, stop=True,
                         reads=[r_Y2, r_c], writes=[r])
                eng = "act" if ecnt[0] % 4 else "dve"
                ecnt[0] += 1
                P.cp(eng, Bst[:, :, c4 * 4:c4 * 4 + 4], pb[:, j, :].rearrange("p (c n) -> p n c", c=4),
                     reads=[r], writes=[r_bst])
            zv = zg[hp:hp + 64, :].rearrange("p (a b) -> p b a", b=128)
            xv = x0g[hp:hp + 64, :].rearrange("p (a b) -> p b a", b=128)
            yv = yc[hp:hp + 64, ctile, :].rearrange("p (a b) -> p b a", b=128)
            bias = ppt[hp:hp + 64, PP["hbias"] + ctile:PP["hbias"] + ctile + 1]
            def i2_bank(nq, hp=hp, ctile=ctile, zv=zv, xv=xv, yv=yv, bias=bias, r_zg=r_zg, r_x0g=r_x0g):
                j, r = pb_r.next()
                for nn in range(16):
                    nb = nq * 16 + nn
                    o = pb[hp:hp + 64, j, nn * 32:(nn + 1) * 32]
                    P.mm(o, Bst[:, nb, :], irs[:, nb, :], start=True, stop=True, reads=[r_bst, r_m], writes=[r])
                gj, gr = gt_r.next()
                gv = gt[hp:hp + 64, gj, :].rearrange("p (b a) -> p b a", a=32)
                P.stt(gv, zv[:, nq * 16:(nq + 1) * 16, :], bias, pb[hp:hp + 64, j, :].rearrange("p (b a) -> p b a", a=32),
                      ALU.mult, ALU.add, reads=[r_zg, r_pp, r], writes=[gr])
                P.tt("dve", yv[:, nq * 16:(nq + 1) * 16, :], gv, xv[:, nq * 16:(nq + 1) * 16, :], ALU.mult,
                     reads=[gr, r_x0g], writes=[r_yc[ctile]])

            for nq in range(8):
                deferred.append(lambda nq=nq, f=i2_bank: f(nq))
        while deferred:
            deferred.pop(0)()
        r_sq, r_rst = Res("sq"), Res("rst")
        ob_r = Ring(2, "ob")
        for ch in range(8):
            sl = slice(ch * 512, (ch + 1) * 512)
            for ctile in range(2):
                P.act(sqb[:, ctile, :], yc[:, ctile, sl], AF.Square, reads=[r_yc[ctile]], writes=[r_sq])
            j, r = pb_r.next()
            for ctile in range(2):
                P.mm(pb[:, j, :], onb[:], sqb[:, ctile, :], start=(ctile == 0), stop=(ctile == 1), reads=[r_c, r_sq],
                     writes=[r])
            P.act(rst[:], pb[:, j, :], AF.Sqrt, bias=EPS, scale=1.0 / D_C, reads=[r], writes=[r_rst])
            P.recip(rst[:], rst[:], reads=[r_rst], writes=[r_rst])
            for ctile in range(2):
                oj, orr = ob_r.next()
                P.stt(ob[:, oj, :], yc[:, ctile, sl], ppt[:, PP["gnc"] + ctile:PP["gnc"] + ctile + 1], rst[:], ALU.mult,
                      ALU.mult, reads=[r_yc[ctile], r_rst, r_pp], writes=[orr])
                P.dma("pool", T["yT"][768 + ctile * 128:768 + (ctile + 1) * 128, sl], ob[:, oj, :], reads=[orr])
        P.barrier()
        P.emit_block()


SCRATCH = {"pA": ([768, L], F32), "pQK": ([512, L], F32), "pV": ([L, 128], F32), "pC": ([768, L], F32),
           "yT": ([D, L], BF16), "x1": ([L, D], F32), "h2T": ([D, L], BF16), "xs0": ([L, D], F32),
           "zT": ([D_C, L], BF16), "hcT": ([D_C, 2 * L], BF16), "zx": ([2, D_C, L], BF16),
           "mcb": ([128, 8192], BF16), "msb": ([128, 8192], BF16), "msnb": ([128, 8192], BF16),
           "irb": ([64, 4096], BF16), "iib": ([64, 4096], BF16)}
INPUTS = {"x": [L, D], "w_in": [DEPTH, D, D_IN], "w_out": [DEPTH, D, D], "w_up": [DEPTH, D, D_FF],
          "w_down": [DEPTH, D_FF, D], "pp": [DEPTH, 128, NPP], "gates": [DEPTH, 4, 3, 128, 128],
          "hy_w1": [DEPTH, 33, 64], "hy_w2": [DEPTH, 64, 64], "hy_w3": [DEPTH, 64, 512], "fng": [128, D],
          "cvec": [128, 4]}
CONST_SHAPES = {"ident": (128, 128), "ones": (128, 128), "rope": (16, L), "prot": (128, 128),
                "maskn": (128, 128), "maskp": (128, 128), "hyz": (33, L), "tpos": (1, 2 * L),
                "e1": (64, 128), "g1": (128, 256), "g2": (128, 256), "e1z": (128, 264), "e1h": (128, 132)}


def make_T(nc, need=None, ext_in=(), ext_out=(), wdepth=DEPTH):
    T = {}
    for n, s in list(INPUTS.items()) + list(CONST_SHAPES.items()):
        if need is None or n in need:
            s = list(s)
            if n in ("w_in", "w_out", "w_up", "w_down"):
                s[0] = wdepth
            T[n] = nc.dram_tensor(n, s, F32, kind="ExternalInput").ap()
    for n, (s, d) in SCRATCH.items():
        kind = "Internal"
        if n in ext_in:
            kind = "ExternalInput"
        if n in ext_out:
            kind = "ExternalOutput"
        T[n] = nc.dram_tensor(n, list(s), d, kind=kind).ap()
    T["out"] = nc.dram_tensor("out", [L, D], F32, kind="ExternalOutput").ap()
    return T


def build_program():
    nc = bass.Bass("TRN2", target_bir_lowering=False)
    T = make_T(nc)
    P = Prog(nc)
    with contextlib.ExitStack() as st:
        P.alloc_sems(st)
        for l in range(DEPTH):
            xsrc = T["x"] if l == 0 else T["xs0"]
            phase_inproj(P, nc, T, l, xsrc, with_prep=(l == 0))
            phase_lru(P, nc, T, l)
            phase_attn(P, nc, T, l)
            phase_hyena_a(P, nc, T, l)
            phase_hyena_b(P, nc, T, l)
            last = (l == DEPTH - 1)
            with contextlib.ExitStack() as wst:
                wu_p = wst.enter_context(nc.sbuf_tensor(_un("wuP"), [128, 8, D_FF], BF16))
                pre = {"wu": wu_p, "r_wuh": [[Res("wu") for _ in range(8)] for _ in range(4)]}
                phase_outproj(P, nc, T, l, xsrc, pre=pre)
                phase_mlp(P, nc, T, l, T["out"] if last else T["xs0"], last, pre=pre)
    return nc


def _perm_w_in(w):
    w = w.copy()
    w[:, :, 768:1152] = _perm_heads(w[:, :, 768:1152], 2)
    return np.ascontiguousarray(w)


def _perm_w_out(w):
    w = w.copy()
    w[:, 384:768, :] = _perm_heads(w[:, 384:768, :], 1)
    return np.ascontiguousarray(w)


def kernel(**inputs):
    I = {k: np.asarray(v) for k, v in inputs.items()}
    f32 = lambda a: np.ascontiguousarray(np.asarray(a, dtype=np.float32))
    shared = {
        "w_in": _perm_w_in(f32(I["w_in"])), "w_out": _perm_w_out(f32(I["w_out"])),
        "w_up": f32(I["w_up"]), "w_down": f32(I["w_down"]),
        "pp": np.stack([pack_params(I, l) for l in range(DEPTH)]),
        "gates": np.stack([pack_gates(I, l) for l in range(DEPTH)]),
        "hy_w1": f32(I["hy_w1"]), "hy_w2": f32(I["hy_w2"]), "hy_w3": f32(I["hy_w3"]),
        "fng": np.ascontiguousarray(np.broadcast_to(f32(I["final_norm_g"])[None, :], (128, D))),
    }
    shared.update(host_small_consts())
    x = f32(I["x"])
    nb = x.shape[0]
    n_cores = 8
    in_maps = []
    for c in range(n_cores):
        m = dict(shared)
        m["x"] = np.ascontiguousarray(x[c % nb])
        in_maps.append(m)
    nc = build_program()
    res = run_bass_kernel_spmd(nc, in_maps, core_ids=list(range(n_cores)))
    out = np.stack([np.asarray(res.results[b]["out"], dtype=np.float32) for b in range(nb)], 0)
    return out
```
